# Optimizing a Trainium2 kernel written in Bass

```python
import jax, jax.numpy as jnp
from jax import lax
import numpy as np

D_MODEL = 1024
BATCH = 2
SEQ = 16384
DEPTH = 2

CHUNK = 64
N_MIXERS = 4
HEAD_DIM = 64
D_MIX = D_MODEL
D_BRANCH = D_MIX // N_MIXERS
N_HEADS = D_BRANCH // HEAD_DIM
LOOKBACK_CHUNKS = 8
BAND = (LOOKBACK_CHUNKS + 1) * CHUNK
MAX_REL = 128
SG_CHUNK = 128
Q_BLOCK = 128
EPS = 1e-6
IN_SIZES = ([D_BRANCH] * 4
            + [D_BRANCH] * 3
            + [D_BRANCH] * 4
            + [N_HEADS]
            + [D_BRANCH] * 4)
N_IN = sum(IN_SIZES)

kernel_name = "hybrid_chunk_stream_heads"


def rms_norm(x, g):
    xf = x.astype(jnp.float32)
    y = xf * lax.rsqrt(jnp.mean(xf * xf, axis=-1, keepdims=True) + EPS)
    return (y * g.astype(jnp.float32)).astype(x.dtype)


def layer_norm(x, g):
    xf = x.astype(jnp.float32)
    mu = jnp.mean(xf, axis=-1, keepdims=True)
    xc = xf - mu
    y = xc * lax.rsqrt(jnp.mean(xc * xc, axis=-1, keepdims=True) + EPS)
    return (y * g.astype(jnp.float32)).astype(x.dtype)


def heads(t):
    b, s, _ = t.shape
    return t.reshape(b, s, N_HEADS, HEAD_DIM)


def chunk_band(t):
    b, s, h, dh = t.shape
    nc = s // CHUNK
    tp = jnp.pad(t, ((0, 0), (LOOKBACK_CHUNKS * CHUNK, 0), (0, 0), (0, 0)))
    tc = tp.reshape(b, nc + LOOKBACK_CHUNKS, CHUNK, h, dh)
    return jnp.concatenate([tc[:, m:m + nc] for m in range(LOOKBACK_CHUNKS + 1)], axis=2)


def chunk_relbias_attention(q, k, v, rel_bias):
    b, s, h, dh = q.shape
    nc = s // CHUNK
    qc = q.reshape(b, nc, CHUNK, h, dh)
    kb = chunk_band(k)
    vb = chunk_band(v)
    i = np.arange(CHUNK)[:, None]
    j = np.arange(BAND)[None, :]
    rel = np.clip(i - j + LOOKBACK_CHUNKS * CHUNK, -MAX_REL, MAX_REL) + MAX_REL
    bias = rel_bias[:, rel].astype(jnp.float32)
    key_chunk = jnp.arange(nc)[:, None] - LOOKBACK_CHUNKS + jnp.arange(BAND)[None, :] // CHUNK
    valid = key_chunk >= 0
    sc = jnp.einsum('bnihd,bnjhd->bnhij', qc, kb).astype(jnp.float32) * (dh ** -0.5)
    sc = sc + bias[None, None]
    sc = jnp.where(valid[None, :, None, None, :], sc, -jnp.inf)
    p = jax.nn.softmax(sc, axis=-1).astype(v.dtype)
    out = jnp.einsum('bnhij,bnjhd->bnihd', p, vb)
    return out.reshape(b, s, h * dh)


def spatial_gating(u, v, v_gain, w_s, b_s):
    b, s, _ = u.shape
    c = D_BRANCH // N_HEADS
    vn = layer_norm(v, v_gain).reshape(b, s // SG_CHUNK, SG_CHUNK, N_HEADS, c)
    w = w_s * jnp.tril(jnp.ones((SG_CHUNK, SG_CHUNK), w_s.dtype))[None]
    mixed = jnp.einsum('gts,bnsgc->bntgc', w, vn) + jnp.transpose(b_s)[None, None, :, :, None]
    return u * mixed.reshape(b, s, D_BRANCH)


def query_blocks(t):
    b, s = t.shape[:2]
    return jnp.moveaxis(t.reshape((b, s // Q_BLOCK, Q_BLOCK) + t.shape[2:]), 1, 0)


def forgetting_attention(q, k, v, f_logit):
    b, s, h, dh = q.shape
    nb = s // Q_BLOCK
    c = jnp.cumsum(jax.nn.log_sigmoid(f_logit.astype(jnp.float32)), axis=1)
    c_k = jnp.transpose(c, (0, 2, 1))[:, :, None, :]
    k_pos = jnp.arange(s)
    scale = dh ** -0.5

    def block(args):
        q_i, c_i, s0 = args
        sc = jnp.einsum('bqhd,bkhd->bhqk', q_i, k).astype(jnp.float32) * scale
        sc = sc + jnp.transpose(c_i, (0, 2, 1))[..., None] - c_k
        q_pos = s0 + jnp.arange(Q_BLOCK)
        mask = k_pos[None, :] <= q_pos[:, None]
        sc = jnp.where(mask, sc, -jnp.inf)
        p = jax.nn.softmax(sc, axis=-1).astype(v.dtype)
        return jnp.einsum('bhqk,bkhd->bqhd', p, v)

    out = lax.map(block, (query_blocks(q), query_blocks(c), jnp.arange(nb) * Q_BLOCK))
    return jnp.moveaxis(out, 0, 1).reshape(b, s, h * dh)


def stick_breaking_attention(q, k, v):
    b, s, h, dh = q.shape
    nb = s // Q_BLOCK
    k_pos = jnp.arange(s)
    scale = dh ** -0.5

    def block(args):
        q_i, s0 = args
        z = jnp.einsum('bqhd,bkhd->bhqk', q_i, k).astype(jnp.float32) * scale
        q_pos = s0 + jnp.arange(Q_BLOCK)
        mask = k_pos[None, :] < q_pos[:, None]
        log_1m = jnp.where(mask, jax.nn.log_sigmoid(-z), 0.0)
        between = lax.cumsum(log_1m, axis=3, reverse=True) - log_1m
        a = jnp.where(mask, jnp.exp(jax.nn.log_sigmoid(z) + between), 0.0)
        return jnp.einsum('bhqk,bkhd->bqhd', a.astype(v.dtype), v)

    out = lax.map(block, (query_blocks(q), jnp.arange(nb) * Q_BLOCK))
    return jnp.moveaxis(out, 0, 1).reshape(b, s, h * dh)


def hybrid_layer(x, norm_g, w_in, b_f, rel_bias, w_s, b_s, v_gain, branch_gain, w_out):
    h = rms_norm(x, norm_g)
    p = jnp.einsum('bsd,dn->bsn', h, w_in)
    (qa, ka, va, ga,
     ub, vbr, gb,
     qc, kc, vc, gc, fc,
     qd, kd, vd, gd) = jnp.split(p, [int(o) for o in np.cumsum(IN_SIZES)[:-1]], axis=-1)
    y_a = chunk_relbias_attention(heads(qa), heads(ka), heads(va), rel_bias)
    y_b = spatial_gating(ub, vbr, v_gain, w_s, b_s)
    y_c = forgetting_attention(heads(qc), heads(kc), heads(vc), fc + b_f)
    y_d = stick_breaking_attention(heads(qd), heads(kd), heads(vd))
    merged = jnp.concatenate([
        rms_norm(y_a, branch_gain[0]) * jax.nn.silu(ga),
        rms_norm(y_b, branch_gain[1]) * jax.nn.silu(gb),
        rms_norm(y_c, branch_gain[2]) * jax.nn.silu(gc),
        rms_norm(y_d, branch_gain[3]) * jax.nn.silu(gd),
    ], axis=-1)
    return x + jnp.einsum('bsm,md->bsd', merged, w_out)


def setup_inputs(seed: int = 0) -> dict:
    key = jax.random.key(seed)
    ks = jax.random.split(key, 12)
    f32 = jnp.float32
    x = jax.random.normal(ks[0], (BATCH, SEQ, D_MODEL), f32)
    norm_g = 1.0 + 0.02 * jax.random.normal(ks[1], (DEPTH, D_MODEL), f32)
    w_in = jax.random.normal(ks[2], (DEPTH, D_MODEL, N_IN), f32) * D_MODEL ** -0.5
    b_f = 4.0 + 0.5 * jax.random.normal(ks[3], (DEPTH, N_HEADS), f32)
    rel_bias = 0.5 * jax.random.normal(ks[4], (DEPTH, N_HEADS, 2 * MAX_REL + 1), f32)
    w_s = jax.random.normal(ks[5], (DEPTH, N_HEADS, SG_CHUNK, SG_CHUNK), f32) * SG_CHUNK ** -0.5
    b_s = 1.0 + 0.1 * jax.random.normal(ks[6], (DEPTH, N_HEADS, SG_CHUNK), f32)
    v_gain = 1.0 + 0.02 * jax.random.normal(ks[7], (DEPTH, D_BRANCH), f32)
    branch_gain = 1.0 + 0.02 * jax.random.normal(ks[8], (DEPTH, N_MIXERS, D_BRANCH), f32)
    w_out = jax.random.normal(ks[9], (DEPTH, D_MIX, D_MODEL), f32) * (0.5 * D_MIX ** -0.5)
    final_g = 1.0 + 0.02 * jax.random.normal(ks[10], (D_MODEL,), f32)
    return {"x": x, "norm_g": norm_g, "w_in": w_in, "b_f": b_f, "rel_bias": rel_bias,
            "w_s": w_s, "b_s": b_s, "v_gain": v_gain, "branch_gain": branch_gain,
            "w_out": w_out, "final_g": final_g}


def reference(x, norm_g, w_in, b_f, rel_bias, w_s, b_s, v_gain, branch_gain, w_out, final_g):
    for l in range(DEPTH):
        x = hybrid_layer(x, norm_g[l], w_in[l], b_f[l], rel_bias[l], w_s[l], b_s[l],
                         v_gain[l], branch_gain[l], w_out[l])
    return rms_norm(x, final_g)
```

```python
import os
import numpy as np
import ml_dtypes
from contextlib import ExitStack
import concourse.bass as bass
import concourse.mybir as mybir
from concourse.bass_utils import run_bass_kernel_spmd

F32 = mybir.dt.float32
BF16 = mybir.dt.bfloat16
AF = mybir.ActivationFunctionType
ALU = mybir.AluOpType
AX = mybir.AxisListType
NPBF = ml_dtypes.bfloat16

EPS = 1e-6
NEG = -30000.0
D_MODEL = 1024
N_IN = 3844
COL = dict(qA=0, kA=256, vA=512, gA=768, uB=1024, vB=1280, gB=1536, qC=1792, kC=2048,
           vC=2304, gC=2560, fC=2816, qD=2820, kD=3076, vD=3332, gD=3588)


class Buf:
    __slots__ = ("w", "r", "excl")

    def __init__(self, excl=False):
        self.w = None
        self.r = {}
        self.excl = excl


class Chan:
    _n = 0

    def __init__(self, sem):
        self.sem = sem
        self.val = 0
        Chan._n += 1
        self.uid = Chan._n


class Prog:
    ENGS = ("pe", "act", "dve", "pool", "sp")
    EPOCH = 8000

    def __init__(self):
        self.nc = bass.Bass("TRN2", target_bir_lowering=False)
        self.es = ExitStack()
        self.ops = {e: [] for e in self.ENGS}
        self.cnt = {e: 0 for e in self.ENGS}
        self.esems = {e: [] for e in self.ENGS}
        self.waited = {e: {} for e in self.ENGS}
        self.nsem = 0
        self.ntens = 0
        self.live_dma = {}
        self.free_chans = {}
        self.phase_chans = []
        self.pid_cache = {}
        self.use_pid = False
        self.arena = None
        self.arena_ptr = 0
        self.bank_ptr = 0
        self.es0 = ExitStack()

    def phase_begin(self):
        self.phase_chans = []
        self.arena_mark = (self.arena_ptr, self.bank_ptr)

    def phase_end(self):
        self.barrier()
        self.arena_ptr, self.bank_ptr = self.arena_mark
        for ch in self.phase_chans:
            self.free_chans.setdefault(ch.kind, []).append(ch)
        self.phase_chans = []

    def new_sem(self, name):
        self.nsem += 1
        return self.nc.alloc_semaphore(name=f"{name}{self.nsem}")

    def chan(self, kind="sp"):
        pool = self.free_chans.setdefault(kind, [])
        if pool:
            ch = pool.pop()
        else:
            ch = Chan(self.new_sem("ch"))
            ch.kind = kind
        self.phase_chans.append(ch)
        return ch

    ARENA_F32 = 45056
    _LET = "abcdefg"

    def _arena_init(self):
        if self.arena is None:
            self.arena = self.es0.enter_context(self.nc.sbuf_tensor("arena", [128, self.ARENA_F32], F32))
            self.banks = [self.es0.enter_context(self.nc.psum_tensor(f"bank{i}", [128, 512], F32)) for i in range(8)]

    def sb(self, shape, dt, name=None):
        self._arena_init()
        esz = 2 if dt == BF16 else 4
        n = int(np.prod(shape[1:]))
        n4 = (n * esz + 31) // 32 * 8
        off = self.arena_ptr
        assert off + n4 <= self.ARENA_F32, f"SBUF arena overflow ({name})"
        self.arena_ptr += n4
        ap = self.arena[0:shape[0], off:off + n4]
        if dt != F32:
            ap = ap.bitcast(dt)
        ap = ap[:, 0:n]
        if len(shape) > 2:
            names = " ".join(self._LET[:len(shape) - 1])
            kw = {self._LET[i]: shape[1 + i] for i in range(len(shape) - 1)}
            ap = ap.rearrange(f"p ({names}) -> p {names}", **kw)
        return ap

    def ps(self, shape, dt, name=None):
        self._arena_init()
        assert self.bank_ptr < 8, "out of PSUM banks"
        bk = self.banks[self.bank_ptr][:]
        self.bank_ptr += 1
        if dt != F32:
            bk = bk.bitcast(dt)
        return bk[0:shape[0], 0:int(np.prod(shape[1:]))]

    def dram(self, name, shape, dt, kind="Internal"):
        return self.nc.dram_tensor(name, list(shape), dt, kind=kind).ap()

    def _need(self, eng, ev):
        if ev is None:
            return None
        if ev[0] == "e":
            _, e2, idx = ev
            if e2 == eng and eng == "pe":
                return None
            key = ("e", e2)
            if self.waited[eng].get(key, 0) >= idx:
                return None
            self.waited[eng][key] = idx
            ep = (idx - 1) // self.EPOCH
            return (self.esems[e2][ep], (idx - 1) % self.EPOCH + 1)
        _, ch, val = ev
        key = ("d", ch.uid)
        if self.waited[eng].get(key, 0) >= val:
            return None
        self.waited[eng][key] = val
        return (ch.sem, val)

    def _deps(self, eng, reads, writes):
        evs = []
        for b in reads:
            evs.append(b.w)
            if b.excl:
                evs.extend(ev for ev in b.r.values() if not (ev[0] == "e" and ev[1] == eng))
        for b in writes:
            evs.append(b.w)
            evs.extend(b.r.values())
        return [w for w in (self._need(eng, ev) for ev in evs) if w]

    @staticmethod
    def _mark(ev, key, reads, writes):
        for b in reads:
            b.r[key] = ev
        for b in writes:
            b.w = ev
            b.r = {}

    def op(self, eng, fn, reads=(), writes=()):
        waits = self._deps(eng, reads, writes)
        self.cnt[eng] += 1
        idx = self.cnt[eng]
        ep = (idx - 1) // self.EPOCH
        while len(self.esems[eng]) <= ep:
            self.esems[eng].append(self.new_sem(eng))
        self.ops[eng].append((waits, fn, self.esems[eng][ep], 1))
        ev = ("e", eng, idx)
        self._mark(ev, ("e", eng), reads, writes)
        return ev

    def dma(self, eng, ch, out, in_, reads=(), writes=(), slow=False):
        waits = self._deps(eng, reads, writes)
        ch.val += 16
        import traceback as _tb
        site = _tb.extract_stack(limit=4)[:-1] if os.environ.get("DEBUGDMA") else None

        def fn(e, o=out, i=in_, slow=slow, site=site):
            o = o(e) if callable(o) else o
            i = i(e) if callable(i) else i
            if slow:
                return e.dma_start(out=o, in_=i, allow_slow_non_contiguous=True)
            try:
                r = e.dma_start(out=o, in_=i)
                self.ndma_ok = getattr(self, "ndma_ok", 0) + 1
                return r
            except Exception:
                print("DMA ok before:", getattr(self, "ndma_ok", 0), "site", site, flush=True)
                print("DMA FAIL out", o.shape, o.ap, "in", i.shape, i.ap, flush=True)
                raise
        self.ops[eng].append((waits, fn, ch.sem, 16))
        ev = ("d", ch, ch.val)
        self.live_dma[ch.uid] = ev
        self._mark(ev, ("d", ch.uid), reads, writes)
        return ev

    def coll(self, ch, kind, groups, in_ap, out_ap, reads=(), writes=(), inc=1, after=()):
        eng = "pool"
        waits = self._deps(eng, reads, writes) + self._deps(eng, (), after)
        ch.val += inc
        fn = lambda e, k=kind, g=groups, i=in_ap, o=out_ap: e.collective_compute(
            k, ALU.bypass, replica_groups=g, ins=[i], outs=[o])
        self.ops[eng].append((waits, fn, ch.sem, inc))
        ev = ("d", ch, ch.val)
        self.live_dma[ch.uid] = ev
        self._mark(ev, ("d", ch.uid), reads, writes)
        return ev

    def barrier(self):
        evs = [("e", e, self.cnt[e]) for e in self.ENGS if self.cnt[e] > 0]
        evs += list(self.live_dma.values())
        for eng in self.ENGS:
            waits = [w for w in (self._need(eng, ev) for ev in evs) if w]
            if waits:
                self.ops[eng].append((waits, None, None, 0))

    def wait_all(self, eng, bufs):
        waits = self._deps(eng, bufs, ())
        self.ops[eng].append((waits, None, None, 0))

    def flush(self):
        nc = self.nc
        if not any(self.ops[e] for e in self.ENGS):
            return
        with nc.Block() as block:
            table = (("pe", block.tensor), ("act", block.scalar), ("dve", block.vector),
                     ("pool", block.gpsimd), ("sp", block.sync))
            for name, deco in table:
                ops = self.ops[name]
                if not ops:
                    continue

                def body(engine, ops=ops, name=name):
                    if self.use_pid and name == "sp":
                        self.pid_cache[id(engine)] = engine.partition_id() % 4
                    for waits, fn, sem, inc in ops:
                        for (s, v) in waits:
                            engine.wait_ge(s, v)
                        if fn is not None:
                            try:
                                fn(engine).then_inc(sem, inc)
                            except Exception:
                                print("FAILED OP on", name, "waits", [(str(s), v) for s, v in waits], flush=True)
                                raise

                deco(body)
        self.ops = {e: [] for e in self.ENGS}

    def emit(self):
        self.flush()
        self.es.close()
        self.es0.close()
        return self.nc


def MM(P, out, lhsT, rhs, start, stop, reads, writes, skip=False):
    if skip:
        fn = lambda e, o=out, l=lhsT, r=rhs, a=start, b=stop: e.matmul(o, l, r, start=a, stop=b, skip_group_check=True)
    else:
        fn = lambda e, o=out, l=lhsT, r=rhs, a=start, b=stop: e.matmul(o, l, r, start=a, stop=b)
    return P.op("pe", fn, reads, writes)


def ACT(P, out, in_, func, reads, writes, bias=None, scale=None, accum=None):
    kw = {}
    if bias is not None:
        kw["bias"] = bias
    if scale is not None:
        kw["scale"] = scale
    if accum is not None:
        kw["accum_out"] = accum
    fn = lambda e, o=out, i=in_, f=func, kw=kw: e.activation(o, i, f, **kw)
    return P.op("act", fn, reads, writes)


def TS(P, eng, out, in0, s1, s2, op0, op1, reads, writes):
    if op1 is None:
        fn = lambda e, o=out, i=in0, a=s1, p0=op0: e.tensor_scalar(o, i, a, None, p0)
    else:
        fn = lambda e, o=out, i=in0, a=s1, b=s2, p0=op0, p1=op1: e.tensor_scalar(o, i, a, b, p0, p1)
    return P.op(eng, fn, reads, writes)


def TT(P, eng, out, in0, in1, op, reads, writes):
    fn = lambda e, o=out, a=in0, b=in1, p=op: e.tensor_tensor(o, a, b, p)
    return P.op(eng, fn, reads, writes)


def STT(P, eng, out, in0, scalar, in1, op0, op1, reads, writes):
    fn = lambda e, o=out, a=in0, s=scalar, b=in1, p0=op0, p1=op1: e.scalar_tensor_tensor(o, a, s, b, p0, p1)
    return P.op(eng, fn, reads, writes)


def CP(P, eng, out, in_, reads, writes):
    if eng == "act":
        fn = lambda e, o=out, i=in_: e.copy(o, i)
    else:
        fn = lambda e, o=out, i=in_: e.tensor_copy(o, i)
    return P.op(eng, fn, reads, writes)


def RSQRT(P, out, in_, eps, reads, writes):
    TS(P, "dve", out, in_, eps, None, ALU.add, None, reads, writes)
    ACT(P, out, out, AF.Sqrt, writes, writes)
    P.op("dve", lambda e, o=out: e.reciprocal(o, o), writes, writes)


def MEMSET(P, eng, ap, val, writes):
    fn = lambda e, a=ap, v=val: e.memset(a, v)
    return P.op(eng, fn, (), writes)


class Ring:
    def __init__(self, items):
        self.items = items
        self.i = 0

    def next(self):
        it = self.items[self.i % len(self.items)]
        self.i += 1
        return it


def host_consts():
    k = np.arange(128)
    c = {}
    c["ident"] = np.eye(128, dtype=np.float32)
    c["negtri"] = np.where(k[:, None] >= k[None, :], -1.0, 0.0).astype(np.float32)
    c["triU"] = (k[:, None] <= k[None, :]).astype(np.float32)
    c["SU"] = (k[:, None] < k[None, :]).astype(np.float32)
    q = np.arange(512)
    kk = (np.arange(4)[None, :, None] * 128 + k[:, None, None])
    c["negmaskC"] = np.where(kk <= q[None, None, :], 0.0, NEG).astype(np.float32)
    c["negmaskD"] = np.where(kk < q[None, None, :], 0.0, NEG).astype(np.float32)
    c["mask01D"] = (kk < q[None, None, :]).astype(np.float32)
    q1 = np.arange(128)
    kc = (2 * (np.arange(5)[None, :, None] - 4) + (k[:, None, None] // 64))
    cq = (q1[None, None, :] // 64)
    valid = (kc >= cq - 8) & (kc <= cq)
    c["maskA"] = np.where(valid, 0.0, NEG).astype(np.float32)
    c["tril01T"] = (k[:, None] <= k[None, :]).astype(np.float32)
    return c


def host_biasA(rel_bias_h):
    k = np.arange(128)
    q1 = np.arange(128)
    kpos = ((np.arange(5)[None, :, None] - 4) * 128 + k[:, None, None])
    d = np.clip(q1[None, None, :] - kpos, -128, 128) + 128
    return np.ascontiguousarray(rel_bias_h[d]).astype(np.float32)


import os
DBG = int(os.environ.get("P1DBG", "9"))
SILU = AF.Copy if os.environ.get("NOSILU") else AF.Silu


def emit_p1(P, io, T, cst):
    NT = T // 512
    W = P.sb([128, 8, N_IN], BF16, "W"); Wbs = [Buf() for _ in range(8)]
    HW_ = N_IN // 2
    wst = [(P.sb([128, HW_], F32, "wst"), Buf(), P.chan()) for _ in range(2)]
    w_v = io["w_in"].rearrange("(c p) n -> p c n", p=128)
    g32 = P.sb([128, 8], F32, "g32"); g32b = Buf()
    P.dma("sp", P.chan(), g32[:], io["g8"], (), (g32b,))
    TS(P, "dve", g32[:], g32[:], 32.0, None, ALU.mult, None, (g32b,), (g32b,))
    for c in range(8):
        for hf in range(2):
            st, sb_, ch = wst[hf]
            cs_ = slice(hf * HW_, (hf + 1) * HW_)
            P.dma("sp", ch, st[:], w_v[:, c, cs_], (), (sb_,))
            TS(P, "dve", W[:, c, cs_], st[:], g32[:, c:c + 1], None, ALU.mult, None, (sb_, g32b), (Wbs[c],))
    vg = P.sb([128, 256], F32, "vg"); vgb = Buf()
    P.dma("sp", P.chan(), vg[:], io["vgain_bc"], (), (vgb,))
    bsT = P.sb([128, 4], F32, "bsT"); bsb = Buf()
    P.dma("sp", P.chan(), bsT[:], io["bsT"], (), (bsb,))
    wsf = P.sb([128, 4, 128], F32, "wsf"); wsfb = Buf()
    P.dma("sp", P.chan(), wsf[:], io["wsT"], (), (wsfb,))
    trl = P.sb([128, 128], F32, "trl"); trlb = Buf()
    P.dma("sp", P.chan(), trl[:], cst["tril01T"], (), (trlb,))
    Wtr = P.sb([128, 4, 128], BF16, "Wtr"); Wtrb = Buf()
    for g in range(4):
        TT(P, "dve", Wtr[:, g, :], wsf[:, g, :], trl[:], ALU.mult, (wsfb, trlb), (Wtrb,))
    ones = P.sb([128, 128], BF16, "ones"); onesb = Buf()
    MEMSET(P, "dve", ones[:], 1.0, (onesb,))

    xts = Ring([(P.sb([128, 8, 512], F32, "xt"), Buf(), P.chan()) for _ in range(2)])
    sq = P.sb([128, 8, 512], BF16, "sq"); sqb = Buf()
    rbc = P.sb([128, 512], F32, "rbc"); rbcb = Buf()
    hTs = Ring([(P.sb([128, 8, 512], BF16, "hT"), [Buf() for _ in range(8)]) for _ in range(2)])
    banks = [(P.ps([128, 512], F32, "bk"), Buf(True)) for _ in range(8)]
    ssq_bank = banks[0]
    fm_banks = Ring(banks[1:3])
    tm_banks = Ring(banks[3:7])
    mix_bank = banks[7]
    fm_st = Ring([(P.sb([128, 512], BF16, "fmst"), Buf(), P.chan("pool")) for _ in range(3)])
    f_st = Ring([(P.sb([4, 512], F32, "fst"), Buf(), P.chan("pool")) for _ in range(2)])
    v_st = Ring([(P.sb([128, 3, 256], BF16, "vst"), Buf(), P.chan("pool")) for _ in range(2)])
    g_st = Ring([(P.sb([128, 1024], F32, "gst"), Buf(), P.chan("pool")) for _ in range(2)])
    yb_st = Ring([(P.sb([128, 256], F32, "ybst"), Buf(), P.chan("pool")) for _ in range(2)])
    u_sb = P.sb([128, 256], F32, "u"); ub = Buf()
    vtmp = P.sb([128, 256], F32, "vtmp"); vtb = Buf()
    vn = P.sb([128, 256], BF16, "vn"); vnb = Buf()
    junk = P.sb([128, 256], F32, "junk"); jb = Buf()
    st4 = P.sb([128, 8], F32, "st4"); st4b = Buf()

    x_v = io["xT"].rearrange("(c p) t -> p c t", p=128)
    FM = [("qA", 0, 0.125), ("kA", 1, 1.0), ("qC", 2, 0.125), ("kC", 3, 1.0), ("qD", 4, 0.125), ("kD", 5, 1.0)]
    evac_i = 0
    for t in range(NT):
        ts = slice(t * 512, (t + 1) * 512)
        xt, xb, xch = xts.next()
        P.dma("sp", xch, xt[:], x_v[:, :, ts], (), (xb,))
        ACT(P, sq[:], xt[:], AF.Square, (xb,), (sqb,))
        sbk, sbb = ssq_bank
        for c in range(8):
            MM(P, sbk[:], ones[:], sq[:, c, :], c == 0, c == 7, (onesb, sqb), (sbb,))
        RSQRT(P, rbc[:], sbk[:], 1024.0 * EPS, (sbb,), (rbcb,))
        hT, hb = hTs.next()
        for c in range(8):
            TT(P, "dve" if c % 2 == 0 else "pool", hT[:, c, :], xt[:, c, :], rbc[:], ALU.mult,
               (xb, rbcb), (hb[c],))
        for name, ti, scl in FM:
            for hp in range(2):
                c0 = COL[name] + hp * 128
                bk, bb = fm_banks.next()
                for c in range(8):
                    MM(P, bk[:], W[:, c, c0:c0 + 128], hT[:, c, :], c == 0, c == 7, (Wbs[c], hb[c]), (bb,))
                st, stb, sch = fm_st.next()
                evac_i += 1
                if evac_i % 2 == 0:
                    ACT(P, st[:], bk[:], AF.Copy, (bb,), (stb,), scale=scl)
                else:
                    TS(P, "dve", st[:], bk[:], scl, None, ALU.mult, None, (bb,), (stb,))
                P.dma("pool", sch, io["qk_w"](2 * hp, ti, t * 512), st[0:64, :], (stb,), ())
                P.dma("pool", sch, io["qk_w"](2 * hp + 1, ti, t * 512), st[64:128, :], (stb,), ())
        if DBG < 2:
            continue
        bk, bb = fm_banks.next()
        for c in range(8):
            MM(P, bk[0:4, :], W[:, c, COL["fC"]:COL["fC"] + 4], hT[:, c, :], c == 0, c == 7, (Wbs[c], hb[c]), (bb,))
        st, stb, sch = f_st.next()
        CP(P, "dve", st[:], bk[0:4, :], (bb,), (stb,))
        P.dma("pool", sch, io["f_send"][:, ts], st[:], (stb,), ())
        if DBG < 3:
            continue
        for sub in range(4):
            tok0 = t * 512 + sub * 128
            hs = slice(sub * 128, (sub + 1) * 128)

            def tm_group(c0, n, col_off=0, bank=None):
                bk_, bb_ = bank if bank is not None else tm_banks.next()
                for c in range(8):
                    MM(P, bk_[:, col_off:col_off + n], hT[:, c, hs], W[:, c, c0:c0 + n], c == 0, c == 7,
                       (Wbs[c], hb[c]), (bb_,))
                return bk_, bb_

            vst, vstb, vch = v_st.next()
            gst, gstb, gch = g_st.next()
            bk, bb = tm_group(COL["vA"], 512)
            CP(P, "dve", vst[:, 0, :], bk[:, 0:256], (bb,), (vstb,))
            ACT(P, gst[:, 0:256], bk[:, 256:512], SILU, (bb,), (gstb,))
            bk, bb = tm_group(COL["vC"], 512)
            CP(P, "dve", vst[:, 1, :], bk[:, 0:256], (bb,), (vstb,))
            ACT(P, gst[:, 512:768], bk[:, 256:512], SILU, (bb,), (gstb,))
            bk, bb = tm_group(COL["vD"], 512)
            CP(P, "dve", vst[:, 2, :], bk[:, 0:256], (bb,), (vstb,))
            ACT(P, gst[:, 768:1024], bk[:, 256:512], SILU, (bb,), (gstb,))
            bk, bb = tm_group(COL["gB"], 256)
            ACT(P, gst[:, 256:512], bk[:, 0:256], SILU, (bb,), (gstb,))
            for vi in range(3 if not os.environ.get("NOV") else 0):
                P.dma("pool", vch, io["v_w"](vi, tok0),
                      vst[:, vi, :].rearrange("p (h d) -> p h d", h=4), (vstb,), ())
            if not os.environ.get("NOG"):
                P.dma("pool", gch, io["gs"][tok0:tok0 + 128, :], gst[:], (gstb,), ())
            if DBG < 4:
                continue
            bk, bb = tm_group(COL["uB"], 512)
            CP(P, "act", u_sb[:], bk[:, 0:256], (bb,), (ub,))
            vps = bk[:, 256:512]
            P.op("dve", lambda e, o=st4[:, 0:1], i=vps: e.reduce_sum(o, i, axis=AX.X), (bb,), (st4b,))
            ACT(P, junk[:], vps, AF.Square, (bb,), (jb, st4b), accum=st4[:, 1:2])
            TS(P, "dve", st4[:, 2:3], st4[:, 0:1], -1.0 / 256, None, ALU.mult, None, (st4b,), (st4b,))
            TT(P, "dve", st4[:, 3:4], st4[:, 2:3], st4[:, 2:3], ALU.mult, (st4b,), (st4b,))
            STT(P, "dve", st4[:, 4:5], st4[:, 1:2], 1.0 / 256, st4[:, 3:4], ALU.mult, ALU.subtract, (st4b,), (st4b,))
            RSQRT(P, st4[:, 5:6], st4[:, 4:5], EPS, (st4b,), (st4b,))
            TS(P, "dve", vtmp[:], vps, st4[:, 2:3], st4[:, 5:6], ALU.add, ALU.mult, (bb, st4b), (vtb,))
            TT(P, "dve", vn[:], vtmp[:], vg[:], ALU.mult, (vtb, vgb), (vnb,))
            mk, mb = mix_bank
            for g in range(4):
                gs_ = slice(g * 64, (g + 1) * 64)
                MM(P, mk[:, gs_], Wtr[:, g, :], vn[:, gs_], True, True, (Wtrb, vnb), (mb,))
            yst, ystb, ych = yb_st.next()
            for g in range(4):
                gs_ = slice(g * 64, (g + 1) * 64)
                STT(P, "dve", yst[:, gs_], mk[:, gs_], bsT[:, g:g + 1], u_sb[:, gs_], ALU.add, ALU.mult,
                    (mb, bsb, ub), (ystb,))
            P.dma("pool", ych, io["yb"][tok0:tok0 + 128, :], yst[:], (ystb,), ())
    outs = [r[1] for r in fm_st.items + f_st.items + v_st.items + g_st.items + yb_st.items]
    return outs


SKEW_AC = tuple(int(v) for v in os.environ.get("SKEW_AC", "0,0,2").split(","))
SKEW_D = tuple(int(v) for v in os.environ.get("SKEW_D", "0,0,1,2,3,4,5").split(","))


def run_pipeline(tiles, skews):
    n = len(tiles)
    for st in range(n + max(skews)):
        for j, sk in enumerate(skews):
            i = st - sk
            if 0 <= i < n:
                tiles[i][j]()


def bc_last(ap2, n):
    return ap2.unsqueeze(2).broadcast_to([ap2.shape[0], ap2.shape[1], n])


def emit_p2(P, io, S, Tc, cst, mixers=("A", "C", "D")):
    NB = S // 128
    NQT = S // 512
    assert NB <= 128

    def cload(src, shape, dt=F32, name="c"):
        t = P.sb(shape, dt, name); b = Buf()
        P.dma("sp", P.chan(), t[:], src, (), (b,))
        return t, b

    ident_f, identfb = cload(cst["ident"], [128, 128], name="identf")
    ident_b = P.sb([128, 128], BF16, "identb"); identb = Buf()
    CP(P, "dve", ident_b[:], ident_f[:], (identfb,), (identb,))
    ntf, ntfb = cload(cst["negtri"], [128, 128], name="ntf")
    negtri = P.sb([128, 128], BF16, "negtri"); negtrib = Buf()
    CP(P, "dve", negtri[:], ntf[:], (ntfb,), (negtrib,))
    ones_col = P.sb([128, 1], BF16, "onec"); onecb = Buf()
    MEMSET(P, "dve", ones_col[:], 1.0, (onecb,))
    stg = P.sb([128, 4, 512], F32, "stg"); stgb = Buf()
    masks = {}
    for nm in ("negmaskC", "negmaskD", "mask01D"):
        P.dma("sp", P.chan(), stg[:], cst[nm], (), (stgb,))
        mt = P.sb([128, 4, 512], BF16, nm); mb_ = Buf()
        CP(P, "dve", mt[:], stg[:], (stgb,), (mb_,))
        masks[nm] = (mt, mb_)

    QT = P.sb([128, S], BF16, "QT"); QTb = Buf()
    KT = P.sb([128, S], BF16, "KT"); KTb = Buf()
    V = P.sb([128, NB, 65], BF16, "V"); Vb = Buf()
    MEMSET(P, "pool", V[:, :, 64:65], 1.0, (Vb,))
    qch, kch, vch = P.chan(), P.chan(), P.chan()
    nbl = Tc // 128

    def load_qkv(m):
        ntq = io["ntq"]
        tq_ = Tc // ntq
        nbq = tq_ // 128
        for i in range(4):
            for tq in range(ntq):
                c0 = i * Tc + tq * tq_
                P.dma("sp", qch, QT[0:64, c0:c0 + tq_], io["qk"](i, 2 * m, tq), (), (QTb,))
                P.dma("sp", kch, KT[0:64, c0:c0 + tq_], io["qk"](i, 2 * m + 1, tq), (), (KTb,))
                b0 = i * nbl + tq * nbq
                P.dma("sp", vch, V[:, b0:b0 + nbq, 0:64], io["v"](i, m, tq), (), (Vb,))

    banks = [(P.ps([128, 512], F32, "bk"), Buf(True)) for _ in range(8)]
    Pts = Ring([(P.sb([128, 512], BF16, "Pt"), Buf()) for _ in range(4)])
    ysts = Ring([(P.sb([128, 4, 64], io.get("y_dt", F32), "yst"), Buf(), P.chan("pool")) for _ in range(2)])
    after_mixer = io.get("after_mixer", lambda mi, bufs: None)
    ybufs = [r[1] for r in ysts.items]
    rc = P.sb([128, 4], F32, "rc"); rcb = Buf()

    def y_out(yst, ystb, ych, mi, tok0, nsub):
        i, off = tok0 // Tc, tok0 % Tc
        P.dma("pool", ych, io["y_w"](i, mi, off, nsub), yst[:, 0:nsub, :], (ystb,), ())

    if "A" in mixers:
        load_qkv(0)
        bA, bAb = cload(io["biasA"], [128, 5, 128], name="bA")
        mA, mAb = cload(cst["maskA"], [128, 5, 128], name="mA")
        TT(P, "dve", bA[:], bA[:], mA[:], ALU.add, (bAb, mAb), (bAb,))
        BAhi = P.sb([128, 5, 128], BF16, "BAhi"); BAlo = P.sb([128, 5, 128], BF16, "BAlo"); BAb = Buf()
        CP(P, "dve", BAhi[:], bA[:], (bAb,), (BAb,))
        TT(P, "dve", BAlo[:], bA[:], BAhi[:], ALU.subtract, (bAb, BAb), (BAb,))
        Sr = Ring(banks[0:4]); Or = Ring(banks[4:6])
        tiles = []
        for qb in range(NB):
            kbs = list(range(max(0, qb - 4), qb + 1))
            Obk, Ob = Or.next()
            qs = slice(qb * 128, (qb + 1) * 128)
            for idx, kb in enumerate(kbs):
                j = kb - qb + 4
                ks = slice(kb * 128, (kb + 1) * 128)
                Sbk, Sb = Sr.next()
                Pt, Ptb = Pts.next()

                def st0(Sbk=Sbk, Sb=Sb, ks=ks, qs=qs, j=j):
                    MM(P, Sbk[:, 0:128], KT[0:64, ks], QT[0:64, qs], True, False, (KTb, QTb), (Sb,))
                    MM(P, Sbk[:, 0:128], ident_b[:], BAhi[:, j, :], False, False, (identb, BAb), (Sb,))
                    MM(P, Sbk[:, 0:128], ident_b[:], BAlo[:, j, :], False, True, (identb, BAb), (Sb,))

                def st1(Sbk=Sbk, Sb=Sb, Pt=Pt, Ptb=Ptb):
                    ACT(P, Pt[:, 0:128], Sbk[:, 0:128], AF.Exp, (Sb,), (Ptb,))

                def st2(Pt=Pt, Ptb=Ptb, Obk=Obk, Ob=Ob, kb=kb, idx=idx, n=len(kbs), qb=qb):
                    MM(P, Obk[:, 0:65], Pt[:, 0:128], V[:, kb, :], idx == 0, idx == n - 1, (Ptb, Vb), (Ob,))
                    if idx == n - 1:
                        yst, ystb, ych = ysts.next()
                        P.op("dve", lambda e, o=rc[:, 0:1], i=Obk[:, 64:65]: e.reciprocal(o, i), (Ob,), (rcb,))
                        TS(P, "dve", yst[:, 0, :], Obk[:, 0:64], rc[:, 0:1], None, ALU.mult, None, (Ob, rcb), (ystb,))
                        y_out(yst, ystb, ych, 0, qb * 128, 1)

                tiles.append((st0, st1, st2))
        run_pipeline(tiles, SKEW_AC)
        after_mixer(0, ybufs)

    if "C" in mixers:
        load_qkv(1)
        Ff = P.sb([128, 128], F32, "Ff"); Ffb = Buf()
        fch = P.chan()
        for i in range(4):
            P.dma("sp", fch, Ff[i * nbl:(i + 1) * nbl, :], io["f"](i), (), (Ffb,))
        bfc, bfcb = cload(io["bf_col"], [128, 1], name="bfc")
        TS(P, "dve", bfc[:], bfc[:], -1.0, None, ALU.mult, None, (bfcb,), (bfcb,))
        ACT(P, Ff[0:NB, :], Ff[0:NB, :], AF.Exp, (Ffb, bfcb), (Ffb,), bias=bfc[0:NB, :], scale=-1.0)
        ACT(P, Ff[0:NB, :], Ff[0:NB, :], AF.Ln, (Ffb,), (Ffb,), bias=1.0)
        tot = P.sb([128, 1], F32, "tot"); totb = Buf()
        P.op("dve", lambda e, o=tot[0:NB, :], i=Ff[0:NB, :]: e.reduce_sum(o, i, axis=AX.X), (Ffb,), (totb,))
        onesf = P.sb([128, 128], F32, "onesf"); onesfb = Buf()
        MEMSET(P, "dve", onesf[:], 1.0, (onesfb,))
        totbc = P.sb([128, 128], F32, "totbc"); totbcb = Buf()
        TS(P, "dve", totbc[0:NB, :], onesf[0:NB, :], tot[0:NB, :], None, ALU.mult, None, (onesfb, totb), (totbcb,))
        triU, triUb = cload(cst["triU"], [128, 128], name="triU")
        SU, SUb = cload(cst["SU"], [128, 128], name="SU")
        b6, b6b = banks[6]
        b7, b7b = banks[7]
        P.op("pe", lambda e, o=b6[:, 0:NB], i=Ff[0:NB, :], idn=ident_f[0:NB, 0:NB]: e.transpose(o, i, idn),
             (Ffb, identfb), (b6b,))
        LT = P.sb([128, 128], F32, "LT"); LTb = Buf()
        CP(P, "dve", LT[:, 0:NB], b6[:, 0:NB], (b6b,), (LTb,))
        MM(P, b7[0:NB, 0:128], LT[:, 0:NB], triU[:], True, False, (LTb, triUb), (b7b,))
        MM(P, b7[0:NB, 0:128], SU[0:NB, 0:NB], totbc[0:NB, :], False, True, (SUb, totbcb), (b7b,))
        cpos = P.sb([128, 128], F32, "cpos"); cposb = Buf()
        CP(P, "dve", cpos[0:NB, :], b7[0:NB, 0:128], (b7b,), (cposb,))
        parts = P.sb([128, 6, 128], BF16, "parts"); partsb = Buf()
        r1 = P.sb([128, 128], F32, "r1"); r1b = Buf()
        CP(P, "dve", parts[0:NB, 3, :], cpos[0:NB, :], (cposb,), (partsb,))
        TT(P, "dve", r1[0:NB, :], cpos[0:NB, :], parts[0:NB, 3, :], ALU.subtract, (cposb, partsb), (r1b,))
        CP(P, "dve", parts[0:NB, 4, :], r1[0:NB, :], (r1b,), (partsb,))
        TT(P, "dve", r1[0:NB, :], r1[0:NB, :], parts[0:NB, 4, :], ALU.subtract, (r1b, partsb), (r1b,))
        CP(P, "dve", parts[0:NB, 5, :], r1[0:NB, :], (r1b,), (partsb,))
        for r in range(3):
            TS(P, "dve", parts[0:NB, r, :], parts[0:NB, 3 + r, :], -1.0, None, ALU.mult, None, (partsb,), (partsb,))
        csb = Buf()
        P.dma("sp", P.chan(), io["cs"].rearrange("r (i p) -> i r p", p=128), parts[0:NB, :, :], (partsb,), (csb,))
        MEMSET(P, "dve", QT[64:70, :], 1.0, (QTb,))
        MEMSET(P, "dve", KT[64:70, :], 1.0, (KTb,))
        P.dma("sp", qch, QT[64:67, :], io["cs"][0:3, :], (csb,), (QTb,))
        P.dma("sp", kch, KT[67:70, :], io["cs"][3:6, :], (csb,), (KTb,))
        nmC, nmCb = masks["negmaskC"]
        Sr = Ring(banks[0:4]); Or = Ring(banks[4:6])
        tiles = []
        for qt in range(NQT):
            q0 = qt * 512
            qs = slice(q0, q0 + 512)
            nkb = 4 * qt + 4
            Obk, Ob = Or.next()
            O3 = Obk[:, 0:260].rearrange("p (s d) -> p s d", d=65)
            for kb in range(nkb):
                ks = slice(kb * 128, (kb + 1) * 128)
                j = kb - 4 * qt
                Sbk, Sb = Sr.next()
                Pt, Ptb = Pts.next()

                def st0(Sbk=Sbk, Sb=Sb, ks=ks, qs=qs, j=j):
                    MM(P, Sbk[:], KT[0:70, ks], QT[0:70, qs], True, j < 0, (KTb, QTb), (Sb,))
                    if j >= 0:
                        MM(P, Sbk[:], ident_b[:], nmC[:, j, :], False, True, (identb, nmCb), (Sb,))

                def st1(Sbk=Sbk, Sb=Sb, Pt=Pt, Ptb=Ptb):
                    ACT(P, Pt[:], Sbk[:], AF.Exp, (Sb,), (Ptb,))

                def st2(Pt=Pt, Ptb=Ptb, O3=O3, Ob=Ob, kb=kb, j=j, qt=qt, q0=q0, nkb=nkb):
                    for sub in range(4):
                        if j > sub:
                            continue
                        MM(P, O3[:, sub, :], Pt[:, sub * 128:(sub + 1) * 128], V[:, kb, :], kb == 0 and sub == 0,
                           kb == 4 * qt + sub, (Ptb, Vb), (Ob,), skip=True)
                    if kb == nkb - 1:
                        yst, ystb, ych = ysts.next()
                        P.op("dve", lambda e, o=rc[:, 0:4], i=O3[:, :, 64]: e.reciprocal(o, i), (Ob,), (rcb,))
                        TT(P, "dve", yst[:], O3[:, :, 0:64], bc_last(rc[:, 0:4], 64), ALU.mult, (Ob, rcb), (ystb,))
                        y_out(yst, ystb, ych, 1, q0, 4)

                tiles.append((st0, st1, st2))
        run_pipeline(tiles, SKEW_AC)
        after_mixer(1, ybufs)

    if "D" in mixers:
        load_qkv(2)
        nmD, nmDb = masks["negmaskD"]
        m01, m01b = masks["mask01D"]
        Zr = Ring(banks[0:2]); Ar = Ring(banks[2:4]); OCr = Ring(banks[4:6])
        Es = Ring([(P.sb([128, 512], F32, "E"), Buf()) for _ in range(3)])
        SPs = Ring([(P.sb([128, 512], BF16, "SP"), Buf()) for _ in range(7)])
        acc = P.sb([128, 4, 64], F32, "acc"); accb = Buf()
        tmp = P.sb([128, 4, 64], F32, "tmp"); tmpb = Buf()
        carry = P.sb([128, 4], F32, "carry"); carryb = Buf()
        ec = P.sb([128, 4], F32, "ec"); ecb = Buf()
        tiles = []
        for qt in range(NQT):
            q0 = qt * 512
            qs = slice(q0, q0 + 512)
            kmax = 4 * qt + 3
            for kb in range(kmax, -1, -1):
                ks = slice(kb * 128, (kb + 1) * 128)
                j = kb - 4 * qt
                Zbk, Zb = Zr.next()
                E, Eb = Es.next()
                SP, SPb = SPs.next()
                Abk, Ab = Ar.next()
                Pt, Ptb = Pts.next()
                OCbk, OCb = OCr.next()
                OC3 = OCbk[:, 0:260].rearrange("p (s d) -> p s d", d=65)

                def s0(Zbk=Zbk, Zb=Zb, ks=ks, qs=qs):
                    MM(P, Zbk[:], KT[0:64, ks], QT[0:64, qs], True, True, (KTb, QTb), (Zb,))

                def s1(Zbk=Zbk, Zb=Zb, E=E, Eb=Eb):
                    ACT(P, E[:], Zbk[:], AF.Exp, (Zb,), (Eb,))

                def s2(E=E, Eb=Eb, SP=SP, SPb=SPb, j=j):
                    ACT(P, SP[:], E[:], AF.Ln, (Eb,), (SPb,), bias=1.0)
                    if j >= 0:
                        TT(P, "dve", SP[:], SP[:], m01[:, j, :], ALU.mult, (SPb, m01b), (SPb,))

                def s3(Abk=Abk, Ab=Ab, SP=SP, SPb=SPb, ks=ks, qs=qs, j=j):
                    MM(P, Abk[:], KT[0:64, ks], QT[0:64, qs], True, False, (KTb, QTb), (Ab,))
                    MM(P, Abk[:], negtri[:], SP[:], False, j < 0, (negtrib, SPb), (Ab,))
                    if j >= 0:
                        MM(P, Abk[:], ident_b[:], nmD[:, j, :], False, True, (identb, nmDb), (Ab,))

                def s4(Abk=Abk, Ab=Ab, Pt=Pt, Ptb=Ptb):
                    ACT(P, Pt[:], Abk[:], AF.Exp, (Ab,), (Ptb,))

                def s5(Pt=Pt, Ptb=Ptb, SP=SP, SPb=SPb, OC3=OC3, OCb=OCb, kb=kb):
                    for sub in range(4):
                        ss = slice(sub * 128, (sub + 1) * 128)
                        MM(P, OC3[:, sub, 0:64], Pt[:, ss], V[:, kb, 0:64], True, True, (Ptb, Vb), (OCb,))
                        MM(P, OC3[:, sub, 64:65], SP[:, ss], ones_col[:], True, True, (SPb, onecb), (OCb,))

                def s6(OC3=OC3, OCb=OCb, kb=kb, kmax=kmax, q0=q0):
                    if kb == kmax:
                        CP(P, "dve", acc[:], OC3[:, :, 0:64], (OCb,), (accb,))
                        TS(P, "dve", carry[:], OC3[:, :, 64], -1.0, None, ALU.mult, None, (OCb,), (carryb,))
                    else:
                        ACT(P, ec[:], carry[:], AF.Exp, (carryb,), (ecb,))
                        TT(P, "dve", tmp[:], OC3[:, :, 0:64], bc_last(ec[:, 0:4], 64), ALU.mult, (OCb, ecb), (tmpb,))
                        TT(P, "pool", acc[:], acc[:], tmp[:], ALU.add, (accb, tmpb), (accb,))
                        TT(P, "dve", carry[:], carry[:], OC3[:, :, 64], ALU.subtract, (carryb, OCb), (carryb,))
                    if kb == 0:
                        yst, ystb, ych = ysts.next()
                        CP(P, "pool", yst[:], acc[:], (accb,), (ystb,))
                        y_out(yst, ystb, ych, 2, q0, 4)

                tiles.append((s0, s1, s2, s3, s4, s5, s6))
        run_pipeline(tiles, SKEW_D)
        after_mixer(2, ybufs)
    return [r[1] for r in ysts.items]


def emit_p3(P, io, T, cst, final):
    NT = T // 512
    ident_f = P.sb([128, 128], F32, "identf"); identfb = Buf()
    P.dma("sp", P.chan(), ident_f[:], cst["ident"], (), (identfb,))
    ident_b = P.sb([128, 128], BF16, "identb"); identb = Buf()
    CP(P, "dve", ident_b[:], ident_f[:], (identfb,), (identb,))
    Wo = P.sb([128, 8, 1024], BF16, "Wo"); Wobs = [Buf() for _ in range(8)]
    wst = [(P.sb([128, 1024], F32, "wst"), Buf(), P.chan()) for _ in range(2)]
    w_v = io["w_out"].rearrange("(c p) n -> p c n", p=128)
    for c in range(8):
        st, sb_, ch = wst[c % 2]
        P.dma("sp", ch, st[:], w_v[:, c, :], (), (sb_,))
        CP(P, "dve" if c % 2 == 0 else "pool", Wo[:, c, :], st[:], (sb_,), (Wobs[c],))
    bg = P.sb([128, 1024], F32, "bg"); bgb = Buf()
    P.dma("sp", P.chan(), bg[:], io["bgain_bc"], (), (bgb,))
    TS(P, "dve", bg[:], bg[:], 16.0, None, ALU.mult, None, (bgb,), (bgb,))
    if final:
        fg = P.sb([128, 8], F32, "fg"); fgb = Buf()
        P.dma("sp", P.chan(), fg[:], io["fg8"], (), (fgb,))
        TS(P, "dve", fg[:], fg[:], 32.0, None, ALU.mult, None, (fgb,), (fgb,))
        ones = P.sb([128, 128], BF16, "ones"); onesb = Buf()
        MEMSET(P, "dve", ones[:], 1.0, (onesb,))
        sq = P.sb([128, 8, 512], BF16, "sq"); sqb = Buf()
        rbc = P.sb([128, 512], F32, "rbc"); rbcb = Buf()

    y_dt = io.get("y_dt", F32)
    yts = Ring([(P.sb([128, 3, 256], y_dt, "yt"), P.sb([128, 256], F32, "ytB"), Buf(), P.chan()) for _ in range(2)])
    gts = Ring([(P.sb([128, 1024], F32, "gt"), Buf(), P.chan()) for _ in range(2)])
    xts = Ring([(P.sb([128, 8, 512], F32, "xt"), Buf(), P.chan()) for _ in range(2)])
    xns = Ring([(P.sb([128, 8, 512], F32, "xn"), [Buf() for _ in range(8)], P.chan("pool")) for _ in range(2)])
    mTs = Ring([(P.sb([128, 8, 512], BF16, "mT"), Buf()) for _ in range(2)])
    t1 = P.sb([128, 1024], F32, "t1"); t1b = Buf()
    m = P.sb([128, 1024], BF16, "m"); mb = Buf()
    junk = P.sb([128, 256], F32, "junk"); jb = Buf()
    ssq = P.sb([128, 4], F32, "ssq"); ssqb = Buf()
    tps = Ring([(P.ps([128, 1024], BF16, "tp"), Buf(True)) for _ in range(2)])
    obanks = Ring([(P.ps([128, 512], F32, "ob"), Buf(True)) for _ in range(4)])
    if final:
        sbank = (P.ps([128, 512], F32, "sbk"), Buf(True))
    x_v = io["xT"].rearrange("(c p) t -> p c t", p=128)
    o_v = io["xT_out"].rearrange("(c p) t -> p c t", p=128)
    for t in range(NT):
        ts = slice(t * 512, (t + 1) * 512)
        xt, xb, xch = xts.next()
        P.dma("sp", xch, xt[:], x_v[:, :, ts], (), (xb,))
        mT, mTb = mTs.next()
        for sub in range(4):
            tok0 = t * 512 + sub * 128
            yt3, ytB, ytb, ych = yts.next()
            for mi in range(3):
                P.dma("sp", ych, yt3[:, mi, :].rearrange("p (h d) -> p h d", h=4),
                      io["y"](mi, tok0), (), (ytb,))
            P.dma("sp", ych, ytB[:], io["yb"][tok0:tok0 + 128, :], (), (ytb,))
            ysrc = (yt3[:, 0, :], ytB[:], yt3[:, 1, :], yt3[:, 2, :])
            gt, gtb, gch = gts.next()
            P.dma("sp", gch, gt[:], io["gs"][tok0:tok0 + 128, :], (), (gtb,))
            for br in range(4):
                ACT(P, junk[:], ysrc[br], AF.Square, (ytb,), (jb, ssqb), accum=ssq[:, br:br + 1])
            RSQRT(P, ssq[:], ssq[:], 256.0 * EPS, (ssqb,), (ssqb,))
            TT(P, "pool", t1[:], gt[:], bg[:], ALU.mult, (gtb, bgb), (t1b,))
            for br in range(4):
                cs_ = slice(br * 256, (br + 1) * 256)
                STT(P, "dve", m[:, cs_], ysrc[br], ssq[:, br:br + 1], t1[:, cs_], ALU.mult, ALU.mult,
                    (ytb, ssqb, t1b), (mb,))
            tp, tpb = tps.next()
            for c in range(8):
                P.op("pe", lambda e, o=tp[:, c * 128:(c + 1) * 128], i=m[:, c * 128:(c + 1) * 128], idn=ident_b[:]:
                     e.transpose(o, i, idn), (mb, identb), (tpb,))
            CP(P, "act", mT[:, :, sub * 128:(sub + 1) * 128], tp[:].rearrange("p (c t) -> p c t", c=8), (tpb,), (mTb,))
        xn, xnbs, xnch = xns.next()
        for cb in range(8):
            ob, obb = obanks.next()
            for c in range(8):
                MM(P, ob[:], Wo[:, c, cb * 128:(cb + 1) * 128], mT[:, c, :], c == 0, c == 7, (Wobs[c], mTb), (obb,))
            TT(P, "dve", xn[:, cb, :], xt[:, cb, :], ob[:], ALU.add, (xb, obb), (xnbs[cb],))
        if final:
            ACT(P, sq[:], xn[:], AF.Square, tuple(xnbs), (sqb,))
            sbk, sbb = sbank
            for c in range(8):
                MM(P, sbk[:], ones[:], sq[:, c, :], c == 0, c == 7, (onesb, sqb), (sbb,))
            RSQRT(P, rbc[:], sbk[:], 1024.0 * EPS, (sbb,), (rbcb,))
            for c in range(8):
                STT(P, "dve", xn[:, c, :], xn[:, c, :], fg[:, c:c + 1], rbc[:], ALU.mult, ALU.mult,
                    (xnbs[c], fgb, rbcb), (xnbs[c],))
        P.dma("pool", xnch, o_v[:, :, ts], xn[:], tuple(xnbs), ())
    return [b for r in xns.items for b in r[1]]


T_CORE = 4096
SEQ = 16384
_CACHE = {}


def _ext(P, name, shape, dt, kind):
    return P.dram(name, shape, dt, kind)


def static_p1_io(io):
    io["qk_w"] = lambda h, ti, t0: io["qk_send"][h, ti, :, t0:t0 + 512]
    io["v_w"] = lambda vi, tok0: io["v_send"][:, vi, tok0:tok0 + 128, :].rearrange("h t d -> t h d")


def static_p2_io(io):
    io["ntq"] = 1
    io["qk"] = lambda i, idx, tq: io["qk_recv"][i, idx]
    io["v"] = lambda i, m, tq: io["v_recv"][i, m].rearrange("(n p) d -> p n d", p=128)
    io["f"] = lambda i: io["f_recv"][i].rearrange("(n p) -> n p", p=128)
    io["y_w"] = lambda i, mi, off, nsub: io["y_send"][i, mi, off:off + nsub * 128, :].rearrange("(s p) d -> p s d", p=128)


def static_p3_io(io):
    io["y"] = lambda mi, tok0: io["y_recv"][:, mi, tok0:tok0 + 128, :].rearrange("h t d -> t h d")


def build_prog1():
    P = Prog(); T = T_CORE; io = {}
    io["xT"] = _ext(P, "xT", [1024, T], F32, "ExternalInput")
    io["w_in"] = _ext(P, "w_in", [1024, N_IN], F32, "ExternalInput")
    io["g8"] = _ext(P, "g8", [128, 8], F32, "ExternalInput")
    io["vgain_bc"] = _ext(P, "vgain_bc", [128, 256], F32, "ExternalInput")
    io["wsT"] = _ext(P, "wsT", [128, 4, 128], F32, "ExternalInput")
    io["bsT"] = _ext(P, "bsT", [128, 4], F32, "ExternalInput")
    cst = {"tril01T": _ext(P, "tril01T", [128, 128], F32, "ExternalInput")}
    io["qk_send"] = _ext(P, "qk_send", [4, 6, 64, T], BF16, "ExternalOutput")
    io["v_send"] = _ext(P, "v_send", [4, 3, T, 64], BF16, "ExternalOutput")
    io["f_send"] = _ext(P, "f_send", [4, T], F32, "ExternalOutput")
    io["gs"] = _ext(P, "gs", [T, 1024], F32, "ExternalOutput")
    io["yb"] = _ext(P, "yb", [T, 256], F32, "ExternalOutput")
    static_p1_io(io)
    outs = emit_p1(P, io, T, cst)
    P.wait_all("sp", outs)
    return P.emit()


P2_CONSTS = ("ident", "negtri", "triU", "SU", "negmaskC", "negmaskD", "mask01D", "maskA")


def build_prog2():
    P = Prog(); S = SEQ; Tc = T_CORE; io = {}
    io["qk_recv"] = _ext(P, "qk_recv", [4, 6, 64, Tc], BF16, "ExternalInput")
    io["v_recv"] = _ext(P, "v_recv", [4, 3, Tc, 64], BF16, "ExternalInput")
    io["f_recv"] = _ext(P, "f_recv", [4, Tc], F32, "ExternalInput")
    io["biasA"] = _ext(P, "biasA", [128, 5, 128], F32, "ExternalInput")
    io["bf_col"] = _ext(P, "bf_col", [128, 1], F32, "ExternalInput")
    io["y_send"] = _ext(P, "y_send", [4, 3, Tc, 64], F32, "ExternalOutput")
    io["cs"] = P.dram("cs", [6, S], BF16, "Internal")
    cn = host_consts()
    cst = {k: _ext(P, k, list(cn[k].shape), F32, "ExternalInput") for k in P2_CONSTS}
    static_p2_io(io)
    outs = emit_p2(P, io, S, Tc, cst)
    P.wait_all("sp", outs)
    return P.emit()


def build_prog3(final):
    P = Prog(); T = T_CORE; io = {}
    io["y_recv"] = _ext(P, "y_recv", [4, 3, T, 64], F32, "ExternalInput")
    io["yb"] = _ext(P, "yb", [T, 256], F32, "ExternalInput")
    io["gs"] = _ext(P, "gs", [T, 1024], F32, "ExternalInput")
    io["xT"] = _ext(P, "xT", [1024, T], F32, "ExternalInput")
    io["w_out"] = _ext(P, "w_out", [1024, 1024], F32, "ExternalInput")
    io["bgain_bc"] = _ext(P, "bgain_bc", [128, 1024], F32, "ExternalInput")
    if final:
        io["fg8"] = _ext(P, "fg8", [128, 8], F32, "ExternalInput")
    io["xT_out"] = _ext(P, "xT_out", [1024, T], F32, "ExternalOutput")
    cst = {"ident": _ext(P, "ident", [128, 128], F32, "ExternalInput")}
    static_p3_io(io)
    outs = emit_p3(P, io, T, cst, final)
    P.wait_all("sp", outs)
    return P.emit()


def _run(nc, in_maps):
    res = run_bass_kernel_spmd(nc, in_maps, core_ids=list(range(8)))
    return res.results


def kernel_unfused(x, norm_g, w_in, b_f, rel_bias, w_s, b_s, v_gain, branch_gain, w_out, final_g):
    f32 = np.float32
    x = np.asarray(x, f32)
    cn = host_consts()
    depth = 2
    c32 = lambda a: np.ascontiguousarray(np.asarray(a, f32))
    xT = [c32(x[c // 4, (c % 4) * T_CORE:(c % 4 + 1) * T_CORE, :].T) for c in range(8)]
    nc1 = build_prog1()
    nc2 = build_prog2()
    for l in range(depth):
        final = l == depth - 1
        com1 = dict(w_in=c32(w_in[l]), g8=c32(np.asarray(norm_g[l]).reshape(8, 128).T),
                    vgain_bc=c32(np.tile(np.asarray(v_gain[l])[None, :], (128, 1))),
                    wsT=c32(np.asarray(w_s[l]).transpose(2, 0, 1)), bsT=c32(np.asarray(b_s[l]).T),
                    tril01T=cn["tril01T"])
        r1 = _run(nc1, [dict(com1, xT=xT[c]) for c in range(8)])
        im2 = []
        for c in range(8):
            b, h = c // 4, c % 4
            d = dict(qk_recv=np.stack([r1[b * 4 + i]["qk_send"][h] for i in range(4)]),
                     v_recv=np.stack([r1[b * 4 + i]["v_send"][h] for i in range(4)]),
                     f_recv=np.stack([r1[b * 4 + i]["f_send"][h] for i in range(4)]),
                     biasA=host_biasA(np.asarray(rel_bias[l][h], f32)),
                     bf_col=np.full((128, 1), np.asarray(b_f, f32)[l, h], f32))
            for k in P2_CONSTS:
                d[k] = cn[k]
            im2.append(d)
        r2 = _run(nc2, im2)
        nc3 = build_prog3(final)
        com3 = dict(w_out=c32(w_out[l]), ident=cn["ident"],
                    bgain_bc=c32(np.tile(np.asarray(branch_gain[l]).reshape(1, 1024), (128, 1))))
        if final:
            com3["fg8"] = c32(np.asarray(final_g).reshape(8, 128).T)
        im3 = []
        for c in range(8):
            b, i = c // 4, c % 4
            im3.append(dict(com3, y_recv=np.stack([r2[b * 4 + h]["y_send"][i] for h in range(4)]),
                            yb=r1[c]["yb"], gs=r1[c]["gs"], xT=xT[c]))
        r3 = _run(nc3, im3)
        xT = [np.asarray(r3[c]["xT_out"]) for c in range(8)]
    out = np.empty((2, SEQ, D_MODEL), f32)
    for c in range(8):
        out[c // 4, (c % 4) * T_CORE:(c % 4 + 1) * T_CORE, :] = xT[c].T
    return out


GROUPS = [[0, 1, 2, 3], [4, 5, 6, 7]]


def build_fused(T=T_CORE, depth=2):
    S = 4 * T
    NTQ = 4 if T >= 2048 else 1
    TQ = T // NTQ
    NT8 = max(1, T // 512)
    T8 = T // NT8
    P = Prog()
    P.use_pid = True
    cn = host_consts()
    cst = {k: P.dram(k, list(v.shape), F32, "ExternalInput") for k, v in cn.items()}
    xT_in = P.dram("xT", [1024, T], F32, "ExternalInput")
    out_ext = P.dram("outT", [1024, T], F32, "ExternalOutput")
    xT_mid = P.dram("xT_mid", [1024, T], F32, "Internal")
    gs = P.dram("gs_i", [T, 1024], F32, "Internal")
    yb = P.dram("yb_i", [T, 256], F32, "Internal")
    cs = P.dram("cs_i", [6, S], BF16, "Internal")
    fg8 = P.dram("fg8", [128, 8], F32, "ExternalInput")
    cch = P.chan("coll")
    FSTOP = int(os.environ.get("FSTOP", "99"))

    def hsel(e):
        return bass.ds(P.pid_cache[id(e)], 1)

    def exchange(name, nch, ch, dt):
        send = P.dram(f"{name}_s", [nch, 4, ch], dt, "Internal")
        gath = P.dram(f"{name}_g", [nch, 16, ch], dt, "Internal")
        recv = P.dram(f"{name}_r", [nch, 4, ch], dt, "Internal")

        def gather(c0=0, c1=nch, wait_bufs=()):
            for c in range(c0, c1):
                P.coll(cch, "AllGather", GROUPS, send[c], gath[c], (), (), after=tuple(wait_bufs))

        def finish():
            P.barrier()
            d = Buf()
            g4 = gath.rearrange("c (a h) x -> c a h x", h=4)
            P.dma("sp", P.chan(), recv.rearrange("c a (o x) -> c a o x", o=1),
                  lambda e: g4[:, :, hsel(e), :], (), (d,))
        return send, recv, gather, finish

    for l in range(depth):
        final = l == depth - 1
        ext = lambda nm, shp: P.dram(f"{nm}{l}", shp, F32, "ExternalInput")
        qk_s, qk_r, qk_g, qk_f = exchange(f"qk{l}", 6 * NTQ, 64 * TQ, BF16)
        v_s, v_r, v_g, v_f = exchange(f"v{l}", 3 * NTQ, TQ * 64, BF16)
        f_s, f_r, f_g, f_f = exchange(f"f{l}", 1, T, F32)
        y_s, y_r, y_g, y_f = exchange(f"y{l}", 3 * NT8, T8 * 64, BF16)
        io1 = dict(xT=xT_in if l == 0 else xT_mid, w_in=ext("w_in", [1024, N_IN]), g8=ext("g8_", [128, 8]),
                   vgain_bc=ext("vgain_bc", [128, 256]), wsT=ext("wsT", [128, 4, 128]), bsT=ext("bsT", [128, 4]),
                   f_send=f_s[0], gs=gs, yb=yb)
        io1["qk_w"] = lambda h, ti, t0, qk_s=qk_s: qk_s[ti * NTQ + t0 // TQ, h].rearrange(
            "(d t) -> d t", d=64)[:, t0 % TQ:t0 % TQ + 512]
        io1["v_w"] = lambda vi, tok0, v_s=v_s: v_s[vi * NTQ + tok0 // TQ].rearrange(
            "h (t d) -> t h d", d=64)[tok0 % TQ:tok0 % TQ + 128]
        P.phase_begin()
        if not os.environ.get("SKIP1"):
            emit_p1(P, io1, T, cst)
        P.phase_end()
        if FSTOP <= 1:
            break
        qk_g(); v_g(); f_g()
        qk_f(); v_f(); f_f()
        P.barrier()
        if FSTOP <= 3:
            break
        io2 = dict(biasA=ext("biasA", [128, 5, 128]), bf_col=ext("bf_col", [128, 1]), cs=cs, ntq=NTQ, y_dt=BF16)
        io2["after_mixer"] = lambda mi, bufs, y_g=y_g: y_g(mi * NT8, (mi + 1) * NT8, bufs)
        io2["qk"] = lambda i, idx, tq, qk_r=qk_r: qk_r[idx * NTQ + tq, i].rearrange("(d t) -> d t", d=64)
        io2["v"] = lambda i, m, tq, v_r=v_r: v_r[m * NTQ + tq, i].rearrange("(n p d) -> p n d", p=128, d=64)
        io2["f"] = lambda i, f_r=f_r: f_r[0, i].rearrange("(n p) -> n p", p=128)
        io2["y_w"] = lambda i, mi, off, nsub, y_s=y_s: y_s[mi * NT8 + off // T8, i].rearrange(
            "(t d) -> t d", d=64)[off % T8:off % T8 + nsub * 128].rearrange("(s p) d -> p s d", p=128)
        P.phase_begin()
        emit_p2(P, io2, S, T, cst)
        P.phase_end()
        if FSTOP <= 4:
            break
        y_f()
        P.barrier()
        io3 = dict(yb=yb, gs=gs, xT=io1["xT"], w_out=ext("w_out", [1024, 1024]), bgain_bc=ext("bgain_bc", [128, 1024]),
                   fg8=fg8, xT_out=out_ext if final else xT_mid, y_dt=BF16)
        io3["y"] = lambda mi, tok0, y_r=y_r: y_r[mi * NT8 + tok0 // T8].rearrange(
            "h (t d) -> t h d", d=64)[tok0 % T8:tok0 % T8 + 128]
        P.phase_begin()
        final_bufs = emit_p3(P, io3, T, cst, final)
        if final:
            P.wait_all("sp", final_bufs)
        P.phase_end()
    print("fused ops:", {k: len(v) for k, v in P.ops.items()}, "sems", P.nsem, flush=True)
    return P.emit()


def fused_inputs(x, norm_g, w_in, b_f, rel_bias, w_s, b_s, v_gain, branch_gain, w_out, final_g, T=T_CORE, depth=2):
    f32 = np.float32
    c32 = lambda a: np.ascontiguousarray(np.asarray(a, f32))
    cn = host_consts()
    com = dict(cn)
    com["fg8"] = c32(np.asarray(final_g).reshape(8, 128).T)
    for l in range(depth):
        com[f"w_in{l}"] = c32(w_in[l])
        com[f"g8_{l}"] = c32(np.asarray(norm_g[l]).reshape(8, 128).T)
        com[f"vgain_bc{l}"] = c32(np.tile(np.asarray(v_gain[l])[None, :], (128, 1)))
        com[f"wsT{l}"] = c32(np.asarray(w_s[l]).transpose(2, 0, 1))
        com[f"bsT{l}"] = c32(np.asarray(b_s[l]).T)
        com[f"w_out{l}"] = c32(w_out[l])
        com[f"bgain_bc{l}"] = c32(np.tile(np.asarray(branch_gain[l]).reshape(1, 1024), (128, 1)))
    ims = []
    for c in range(8):
        b, h = c // 4, c % 4
        d = dict(com)
        d["xT"] = c32(np.asarray(x)[b, h * T:(h + 1) * T, :].T)
        for l in range(depth):
            d[f"biasA{l}"] = host_biasA(np.asarray(rel_bias[l][h], f32))
            d[f"bf_col{l}"] = np.full((128, 1), np.asarray(b_f, f32)[l, h], f32)
        ims.append(d)
    return ims


def kernel(x, norm_g, w_in, b_f, rel_bias, w_s, b_s, v_gain, branch_gain, w_out, final_g):
    nc = build_fused()
    ims = fused_inputs(x, norm_g, w_in, b_f, rel_bias, w_s, b_s, v_gain, branch_gain, w_out, final_g)
    res = run_bass_kernel_spmd(nc, ims, core_ids=list(range(8))).results
    out = np.empty((2, SEQ, D_MODEL), np.float32)
    for c in range(8):
        out[c // 4, (c % 4) * T_CORE:(c % 4 + 1) * T_CORE, :] = np.asarray(res[c]["outT"]).T
    return out
```

```python
import os
import numpy as np
import ml_dtypes
from contextlib import ExitStack
import concourse.bass as bass
import concourse.mybir as mybir
from concourse.bass_utils import run_bass_kernel_spmd

F32 = mybir.dt.float32
BF16 = mybir.dt.bfloat16
AF = mybir.ActivationFunctionType
ALU = mybir.AluOpType
AX = mybir.AxisListType
NPBF = ml_dtypes.bfloat16

EPS = 1e-6
NEG = -30000.0
D_MODEL = 1024
N_IN = 3844
COL = dict(qA=0, kA=256, vA=512, gA=768, uB=1024, vB=1280, gB=1536, qC=1792, kC=2048,
           vC=2304, gC=2560, fC=2816, qD=2820, kD=3076, vD=3332, gD=3588)


class Buf:
    __slots__ = ("w", "r", "excl")

    def __init__(self, excl=False):
        self.w = None
        self.r = {}
        self.excl = excl


class Chan:
    _n = 0

    def __init__(self, sem):
        self.sem = sem
        self.val = 0
        Chan._n += 1
        self.uid = Chan._n


class Prog:
    ENGS = ("pe", "act", "dve", "pool", "sp")
    EPOCH = 8000

    def __init__(self):
        self.nc = bass.Bass("TRN2", target_bir_lowering=False)
        self.es = ExitStack()
        self.ops = {e: [] for e in self.ENGS}
        self.cnt = {e: 0 for e in self.ENGS}
        self.esems = {e: [] for e in self.ENGS}
        self.waited = {e: {} for e in self.ENGS}
        self.nsem = 0
        self.ntens = 0
        self.live_dma = {}
        self.free_chans = {}
        self.phase_chans = []
        self.pid_cache = {}
        self.use_pid = False
        self.arena = None
        self.arena_ptr = 0
        self.bank_ptr = 0
        self.es0 = ExitStack()

    def phase_begin(self):
        self.phase_chans = []
        self.arena_mark = (self.arena_ptr, self.bank_ptr)

    def phase_end(self):
        self.barrier()
        self.arena_ptr, self.bank_ptr = self.arena_mark
        for ch in self.phase_chans:
            self.free_chans.setdefault(ch.kind, []).append(ch)
        self.phase_chans = []

    def new_sem(self, name):
        self.nsem += 1
        return self.nc.alloc_semaphore(name=f"{name}{self.nsem}")

    def chan(self, kind="sp"):
        pool = self.free_chans.setdefault(kind, [])
        if pool:
            ch = pool.pop()
        else:
            ch = Chan(self.new_sem("ch"))
            ch.kind = kind
        self.phase_chans.append(ch)
        return ch

    ARENA_F32 = 45056
    _LET = "abcdefg"

    def _arena_init(self):
        if self.arena is None:
            self.arena = self.es0.enter_context(self.nc.sbuf_tensor("arena", [128, self.ARENA_F32], F32))
            self.banks = [self.es0.enter_context(self.nc.psum_tensor(f"bank{i}", [128, 512], F32)) for i in range(8)]

    def sb(self, shape, dt, name=None):
        self._arena_init()
        esz = 2 if dt == BF16 else 4
        n = int(np.prod(shape[1:]))
        n4 = (n * esz + 31) // 32 * 8
        off = self.arena_ptr
        assert off + n4 <= self.ARENA_F32, f"SBUF arena overflow ({name})"
        self.arena_ptr += n4
        ap = self.arena[0:shape[0], off:off + n4]
        if dt != F32:
            ap = ap.bitcast(dt)
        ap = ap[:, 0:n]
        if len(shape) > 2:
            names = " ".join(self._LET[:len(shape) - 1])
            kw = {self._LET[i]: shape[1 + i] for i in range(len(shape) - 1)}
            ap = ap.rearrange(f"p ({names}) -> p {names}", **kw)
        return ap

    def ps(self, shape, dt, name=None):
        self._arena_init()
        assert self.bank_ptr < 8, "out of PSUM banks"
        bk = self.banks[self.bank_ptr][:]
        self.bank_ptr += 1
        if dt != F32:
            bk = bk.bitcast(dt)
        return bk[0:shape[0], 0:int(np.prod(shape[1:]))]

    def dram(self, name, shape, dt, kind="Internal"):
        return self.nc.dram_tensor(name, list(shape), dt, kind=kind).ap()

    def _need(self, eng, ev):
        if ev is None:
            return None
        if ev[0] == "e":
            _, e2, idx = ev
            if e2 == eng and eng == "pe":
                return None
            key = ("e", e2)
            if self.waited[eng].get(key, 0) >= idx:
                return None
            self.waited[eng][key] = idx
            ep = (idx - 1) // self.EPOCH
            return (self.esems[e2][ep], (idx - 1) % self.EPOCH + 1)
        _, ch, val = ev
        key = ("d", ch.uid)
        if self.waited[eng].get(key, 0) >= val:
            return None
        self.waited[eng][key] = val
        return (ch.sem, val)

    def _deps(self, eng, reads, writes):
        evs = []
        for b in reads:
            evs.append(b.w)
            if b.excl:
                evs.extend(ev for ev in b.r.values() if not (ev[0] == "e" and ev[1] == eng))
        for b in writes:
            evs.append(b.w)
            evs.extend(b.r.values())
        return [w for w in (self._need(eng, ev) for ev in evs) if w]

    @staticmethod
    def _mark(ev, key, reads, writes):
        for b in reads:
            b.r[key] = ev
        for b in writes:
            b.w = ev
            b.r = {}

    def op(self, eng, fn, reads=(), writes=()):
        waits = self._deps(eng, reads, writes)
        self.cnt[eng] += 1
        idx = self.cnt[eng]
        ep = (idx - 1) // self.EPOCH
        while len(self.esems[eng]) <= ep:
            self.esems[eng].append(self.new_sem(eng))
        self.ops[eng].append((waits, fn, self.esems[eng][ep], 1))
        ev = ("e", eng, idx)
        self._mark(ev, ("e", eng), reads, writes)
        return ev

    def dma(self, eng, ch, out, in_, reads=(), writes=(), slow=False):
        waits = self._deps(eng, reads, writes)
        ch.val += 16
        import traceback as _tb
        site = _tb.extract_stack(limit=4)[:-1] if os.environ.get("DEBUGDMA") else None

        def fn(e, o=out, i=in_, slow=slow, site=site):
            o = o(e) if callable(o) else o
            i = i(e) if callable(i) else i
            if slow:
                return e.dma_start(out=o, in_=i, allow_slow_non_contiguous=True)
            try:
                r = e.dma_start(out=o, in_=i)
                self.ndma_ok = getattr(self, "ndma_ok", 0) + 1
                return r
            except Exception:
                print("DMA ok before:", getattr(self, "ndma_ok", 0), "site", site, flush=True)
                print("DMA FAIL out", o.shape, o.ap, "in", i.shape, i.ap, flush=True)
                raise
        self.ops[eng].append((waits, fn, ch.sem, 16))
        ev = ("d", ch, ch.val)
        self.live_dma[ch.uid] = ev
        self._mark(ev, ("d", ch.uid), reads, writes)
        return ev

    def coll(self, ch, kind, groups, in_ap, out_ap, reads=(), writes=(), inc=1, after=()):
        eng = "pool"
        waits = self._deps(eng, reads, writes) + self._deps(eng, (), after)
        ch.val += inc
        fn = lambda e, k=kind, g=groups, i=in_ap, o=out_ap: e.collective_compute(
            k, ALU.bypass, replica_groups=g, ins=[i], outs=[o])
        self.ops[eng].append((waits, fn, ch.sem, inc))
        ev = ("d", ch, ch.val)
        self.live_dma[ch.uid] = ev
        self._mark(ev, ("d", ch.uid), reads, writes)
        return ev

    def barrier(self):
        evs = [("e", e, self.cnt[e]) for e in self.ENGS if self.cnt[e] > 0]
        evs += list(self.live_dma.values())
        for eng in self.ENGS:
            waits = [w for w in (self._need(eng, ev) for ev in evs) if w]
            if waits:
                self.ops[eng].append((waits, None, None, 0))

    def wait_all(self, eng, bufs):
        waits = self._deps(eng, bufs, ())
        self.ops[eng].append((waits, None, None, 0))

    def flush(self):
        nc = self.nc
        if not any(self.ops[e] for e in self.ENGS):
            return
        with nc.Block() as block:
            table = (("pe", block.tensor), ("act", block.scalar), ("dve", block.vector),
                     ("pool", block.gpsimd), ("sp", block.sync))
            for name, deco in table:
                ops = self.ops[name]
                if not ops:
                    continue

                def body(engine, ops=ops, name=name):
                    if self.use_pid and name == "sp":
                        self.pid_cache[id(engine)] = engine.partition_id() % 4
                    for waits, fn, sem, inc in ops:
                        for (s, v) in waits:
                            engine.wait_ge(s, v)
                        if fn is not None:
                            try:
                                fn(engine).then_inc(sem, inc)
                            except Exception:
                                print("FAILED OP on", name, "waits", [(str(s), v) for s, v in waits], flush=True)
                                raise

                deco(body)
        self.ops = {e: [] for e in self.ENGS}

    def emit(self):
        self.flush()
        self.es.close()
        self.es0.close()
        return self.nc


def MM(P, out, lhsT, rhs, start, stop, reads, writes, skip=False):
    if skip:
        fn = lambda e, o=out, l=lhsT, r=rhs, a=start, b=stop: e.matmul(o, l, r, start=a, stop=b, skip_group_check=True)
    else:
        fn = lambda e, o=out, l=lhsT, r=rhs, a=start, b=stop: e.matmul(o, l, r, start=a, stop=b)
    return P.op("pe", fn, reads, writes)


def ACT(P, out, in_, func, reads, writes, bias=None, scale=None, accum=None):
    kw = {}
    if bias is not None:
        kw["bias"] = bias
    if scale is not None:
        kw["scale"] = scale
    if accum is not None:
        kw["accum_out"] = accum
    fn = lambda e, o=out, i=in_, f=func, kw=kw: e.activation(o, i, f, **kw)
    return P.op("act", fn, reads, writes)


def TS(P, eng, out, in0, s1, s2, op0, op1, reads, writes):
    if op1 is None:
        fn = lambda e, o=out, i=in0, a=s1, p0=op0: e.tensor_scalar(o, i, a, None, p0)
    else:
        fn = lambda e, o=out, i=in0, a=s1, b=s2, p0=op0, p1=op1: e.tensor_scalar(o, i, a, b, p0, p1)
    return P.op(eng, fn, reads, writes)


def TT(P, eng, out, in0, in1, op, reads, writes):
    fn = lambda e, o=out, a=in0, b=in1, p=op: e.tensor_tensor(o, a, b, p)
    return P.op(eng, fn, reads, writes)


def STT(P, eng, out, in0, scalar, in1, op0, op1, reads, writes):
    fn = lambda e, o=out, a=in0, s=scalar, b=in1, p0=op0, p1=op1: e.scalar_tensor_tensor(o, a, s, b, p0, p1)
    return P.op(eng, fn, reads, writes)


def CP(P, eng, out, in_, reads, writes):
    if eng == "act":
        fn = lambda e, o=out, i=in_: e.copy(o, i)
    else:
        fn = lambda e, o=out, i=in_: e.tensor_copy(o, i)
    return P.op(eng, fn, reads, writes)


def RSQRT(P, out, in_, eps, reads, writes):
    TS(P, "dve", out, in_, eps, None, ALU.add, None, reads, writes)
    ACT(P, out, out, AF.Sqrt, writes, writes)
    P.op("dve", lambda e, o=out: e.reciprocal(o, o), writes, writes)


def MEMSET(P, eng, ap, val, writes):
    fn = lambda e, a=ap, v=val: e.memset(a, v)
    return P.op(eng, fn, (), writes)


class Ring:
    def __init__(self, items):
        self.items = items
        self.i = 0

    def next(self):
        it = self.items[self.i % len(self.items)]
        self.i += 1
        return it


def host_consts():
    k = np.arange(128)
    c = {}
    c["ident"] = np.eye(128, dtype=np.float32)
    c["negtri"] = np.where(k[:, None] >= k[None, :], -1.0, 0.0).astype(np.float32)
    c["triU"] = (k[:, None] <= k[None, :]).astype(np.float32)
    c["SU"] = (k[:, None] < k[None, :]).astype(np.float32)
    q = np.arange(512)
    kk = (np.arange(4)[None, :, None] * 128 + k[:, None, None])
    c["negmaskC"] = np.where(kk <= q[None, None, :], 0.0, NEG).astype(np.float32)
    c["negmaskD"] = np.where(kk < q[None, None, :], 0.0, NEG).astype(np.float32)
    c["mask01D"] = (kk < q[None, None, :]).astype(np.float32)
    q1 = np.arange(128)
    kc = (2 * (np.arange(5)[None, :, None] - 4) + (k[:, None, None] // 64))
    cq = (q1[None, None, :] // 64)
    valid = (kc >= cq - 8) & (kc <= cq)
    c["maskA"] = np.where(valid, 0.0, NEG).astype(np.float32)
    c["tril01T"] = (k[:, None] <= k[None, :]).astype(np.float32)
    return c


def host_biasA(rel_bias_h):
    k = np.arange(128)
    q1 = np.arange(128)
    kpos = ((np.arange(5)[None, :, None] - 4) * 128 + k[:, None, None])
    d = np.clip(q1[None, None, :] - kpos, -128, 128) + 128
    return np.ascontiguousarray(rel_bias_h[d]).astype(np.float32)


import os
DBG = int(os.environ.get("P1DBG", "9"))
SILU = AF.Copy if os.environ.get("NOSILU") else AF.Silu


def emit_p1(P, io, T, cst):
    NT = T // 512
    W = P.sb([128, 8, N_IN], BF16, "W"); Wbs = [Buf() for _ in range(8)]
    HW_ = N_IN // 2
    wst = [(P.sb([128, HW_], F32, "wst"), Buf(), P.chan()) for _ in range(2)]
    w_v = io["w_in"].rearrange("(c p) n -> p c n", p=128)
    g32 = P.sb([128, 8], F32, "g32"); g32b = Buf()
    P.dma("sp", P.chan(), g32[:], io["g8"], (), (g32b,))
    TS(P, "dve", g32[:], g32[:], 32.0, None, ALU.mult, None, (g32b,), (g32b,))
    for c in range(8):
        for hf in range(2):
            st, sb_, ch = wst[hf]
            cs_ = slice(hf * HW_, (hf + 1) * HW_)
            P.dma("sp", ch, st[:], w_v[:, c, cs_], (), (sb_,))
            TS(P, "dve", W[:, c, cs_], st[:], g32[:, c:c + 1], None, ALU.mult, None, (sb_, g32b), (Wbs[c],))
    vg = P.sb([128, 256], F32, "vg"); vgb = Buf()
    P.dma("sp", P.chan(), vg[:], io["vgain_bc"], (), (vgb,))
    bsT = P.sb([128, 4], F32, "bsT"); bsb = Buf()
    P.dma("sp", P.chan(), bsT[:], io["bsT"], (), (bsb,))
    wsf = P.sb([128, 4, 128], F32, "wsf"); wsfb = Buf()
    P.dma("sp", P.chan(), wsf[:], io["wsT"], (), (wsfb,))
    trl = P.sb([128, 128], F32, "trl"); trlb = Buf()
    P.dma("sp", P.chan(), trl[:], cst["tril01T"], (), (trlb,))
    Wtr = P.sb([128, 4, 128], BF16, "Wtr"); Wtrb = Buf()
    for g in range(4):
        TT(P, "dve", Wtr[:, g, :], wsf[:, g, :], trl[:], ALU.mult, (wsfb, trlb), (Wtrb,))
    ones = P.sb([128, 128], BF16, "ones"); onesb = Buf()
    MEMSET(P, "dve", ones[:], 1.0, (onesb,))

    xts = Ring([(P.sb([128, 8, 512], F32, "xt"), Buf(), P.chan()) for _ in range(2)])
    sq = P.sb([128, 8, 512], BF16, "sq"); sqb = Buf()
    rbc = P.sb([128, 512], F32, "rbc"); rbcb = Buf()
    hTs = Ring([(P.sb([128, 8, 512], BF16, "hT"), [Buf() for _ in range(8)]) for _ in range(2)])
    banks = [(P.ps([128, 512], F32, "bk"), Buf(True)) for _ in range(8)]
    ssq_bank = banks[0]
    fm_banks = Ring(banks[1:3])
    tm_banks = Ring(banks[3:7])
    mix_bank = banks[7]
    fm_st = Ring([(P.sb([128, 512], BF16, "fmst"), Buf(), P.chan("pool")) for _ in range(3)])
    f_st = Ring([(P.sb([4, 512], F32, "fst"), Buf(), P.chan("pool")) for _ in range(2)])
    v_st = Ring([(P.sb([128, 3, 256], BF16, "vst"), Buf(), P.chan("pool")) for _ in range(2)])
    g_st = Ring([(P.sb([128, 1024], F32, "gst"), Buf(), P.chan("pool")) for _ in range(2)])
    yb_st = Ring([(P.sb([128, 256], F32, "ybst"), Buf(), P.chan("pool")) for _ in range(2)])
    u_sb = P.sb([128, 256], F32, "u"); ub = Buf()
    vtmp = P.sb([128, 256], F32, "vtmp"); vtb = Buf()
    vn = P.sb([128, 256], BF16, "vn"); vnb = Buf()
    junk = P.sb([128, 256], F32, "junk"); jb = Buf()
    st4 = P.sb([128, 8], F32, "st4"); st4b = Buf()

    x_v = io["xT"].rearrange("(c p) t -> p c t", p=128)
    FM = [("qA", 0, 0.125), ("kA", 1, 1.0), ("qC", 2, 0.125), ("kC", 3, 1.0), ("qD", 4, 0.125), ("kD", 5, 1.0)]
    evac_i = 0
    for t in range(NT):
        ts = slice(t * 512, (t + 1) * 512)
        xt, xb, xch = xts.next()
        P.dma("sp", xch, xt[:], x_v[:, :, ts], (), (xb,))
        ACT(P, sq[:], xt[:], AF.Square, (xb,), (sqb,))
        sbk, sbb = ssq_bank
        for c in range(8):
            MM(P, sbk[:], ones[:], sq[:, c, :], c == 0, c == 7, (onesb, sqb), (sbb,))
        RSQRT(P, rbc[:], sbk[:], 1024.0 * EPS, (sbb,), (rbcb,))
        hT, hb = hTs.next()
        for c in range(8):
            TT(P, "dve" if c % 2 == 0 else "pool", hT[:, c, :], xt[:, c, :], rbc[:], ALU.mult,
               (xb, rbcb), (hb[c],))
        for name, ti, scl in FM:
            for hp in range(2):
                c0 = COL[name] + hp * 128
                bk, bb = fm_banks.next()
                for c in range(8):
                    MM(P, bk[:], W[:, c, c0:c0 + 128], hT[:, c, :], c == 0, c == 7, (Wbs[c], hb[c]), (bb,))
                st, stb, sch = fm_st.next()
                evac_i += 1
                if evac_i % 2 == 0:
                    ACT(P, st[:], bk[:], AF.Copy, (bb,), (stb,), scale=scl)
                else:
                    TS(P, "dve", st[:], bk[:], scl, None, ALU.mult, None, (bb,), (stb,))
                P.dma("pool", sch, io["qk_w"](2 * hp, ti, t * 512), st[0:64, :], (stb,), ())
                P.dma("pool", sch, io["qk_w"](2 * hp + 1, ti, t * 512), st[64:128, :], (stb,), ())
        if DBG < 2:
            continue
        bk, bb = fm_banks.next()
        for c in range(8):
            MM(P, bk[0:4, :], W[:, c, COL["fC"]:COL["fC"] + 4], hT[:, c, :], c == 0, c == 7, (Wbs[c], hb[c]), (bb,))
        st, stb, sch = f_st.next()
        CP(P, "dve", st[:], bk[0:4, :], (bb,), (stb,))
        P.dma("pool", sch, io["f_send"][:, ts], st[:], (stb,), ())
        if DBG < 3:
            continue
        for sub in range(4):
            tok0 = t * 512 + sub * 128
            hs = slice(sub * 128, (sub + 1) * 128)

            def tm_group(c0, n, col_off=0, bank=None):
                bk_, bb_ = bank if bank is not None else tm_banks.next()
                for c in range(8):
                    MM(P, bk_[:, col_off:col_off + n], hT[:, c, hs], W[:, c, c0:c0 + n], c == 0, c == 7,
                       (Wbs[c], hb[c]), (bb_,))
                return bk_, bb_

            vst, vstb, vch = v_st.next()
            gst, gstb, gch = g_st.next()
            bk, bb = tm_group(COL["vA"], 512)
            CP(P, "dve", vst[:, 0, :], bk[:, 0:256], (bb,), (vstb,))
            ACT(P, gst[:, 0:256], bk[:, 256:512], SILU, (bb,), (gstb,))
            bk, bb = tm_group(COL["vC"], 512)
            CP(P, "dve", vst[:, 1, :], bk[:, 0:256], (bb,), (vstb,))
            ACT(P, gst[:, 512:768], bk[:, 256:512], SILU, (bb,), (gstb,))
            bk, bb = tm_group(COL["vD"], 512)
            CP(P, "dve", vst[:, 2, :], bk[:, 0:256], (bb,), (vstb,))
            ACT(P, gst[:, 768:1024], bk[:, 256:512], SILU, (bb,), (gstb,))
            bk, bb = tm_group(COL["gB"], 256)
            ACT(P, gst[:, 256:512], bk[:, 0:256], SILU, (bb,), (gstb,))
            for vi in range(3 if not os.environ.get("NOV") else 0):
                P.dma("pool", vch, io["v_w"](vi, tok0),
                      vst[:, vi, :].rearrange("p (h d) -> p h d", h=4), (vstb,), ())
            if not os.environ.get("NOG"):
                P.dma("pool", gch, io["gs"][tok0:tok0 + 128, :], gst[:], (gstb,), ())
            if DBG < 4:
                continue
            bk, bb = tm_group(COL["uB"], 512)
            CP(P, "act", u_sb[:], bk[:, 0:256], (bb,), (ub,))
            vps = bk[:, 256:512]
            P.op("dve", lambda e, o=st4[:, 0:1], i=vps: e.reduce_sum(o, i, axis=AX.X), (bb,), (st4b,))
            ACT(P, junk[:], vps, AF.Square, (bb,), (jb, st4b), accum=st4[:, 1:2])
            TS(P, "dve", st4[:, 2:3], st4[:, 0:1], -1.0 / 256, None, ALU.mult, None, (st4b,), (st4b,))
            TT(P, "dve", st4[:, 3:4], st4[:, 2:3], st4[:, 2:3], ALU.mult, (st4b,), (st4b,))
            STT(P, "dve", st4[:, 4:5], st4[:, 1:2], 1.0 / 256, st4[:, 3:4], ALU.mult, ALU.subtract, (st4b,), (st4b,))
            RSQRT(P, st4[:, 5:6], st4[:, 4:5], EPS, (st4b,), (st4b,))
            TS(P, "dve", vtmp[:], vps, st4[:, 2:3], st4[:, 5:6], ALU.add, ALU.mult, (bb, st4b), (vtb,))
            TT(P, "dve", vn[:], vtmp[:], vg[:], ALU.mult, (vtb, vgb), (vnb,))
            mk, mb = mix_bank
            for g in range(4):
                gs_ = slice(g * 64, (g + 1) * 64)
                MM(P, mk[:, gs_], Wtr[:, g, :], vn[:, gs_], True, True, (Wtrb, vnb), (mb,))
            yst, ystb, ych = yb_st.next()
            for g in range(4):
                gs_ = slice(g * 64, (g + 1) * 64)
                STT(P, "dve", yst[:, gs_], mk[:, gs_], bsT[:, g:g + 1], u_sb[:, gs_], ALU.add, ALU.mult,
                    (mb, bsb, ub), (ystb,))
            P.dma("pool", ych, io["yb"][tok0:tok0 + 128, :], yst[:], (ystb,), ())
        if "after_tile" in io:
            io["after_tile"](t, [r[1] for r in fm_st.items + v_st.items])
    outs = [r[1] for r in fm_st.items + f_st.items + v_st.items + g_st.items + yb_st.items]
    return outs


SKEW_AC = tuple(int(v) for v in os.environ.get("SKEW_AC", "0,0,2").split(","))
SKEW_D = tuple(int(v) for v in os.environ.get("SKEW_D", "0,0,1,2,3,4,5").split(","))


def run_pipeline(tiles, skews):
    n = len(tiles)
    for st in range(n + max(skews)):
        for j, sk in enumerate(skews):
            i = st - sk
            if 0 <= i < n:
                tiles[i][j]()


def bc_last(ap2, n):
    return ap2.unsqueeze(2).broadcast_to([ap2.shape[0], ap2.shape[1], n])


def emit_p2(P, io, S, Tc, cst, mixers=("A", "C", "D")):
    NB = S // 128
    NQT = S // 512
    assert NB <= 128

    def cload(src, shape, dt=F32, name="c"):
        t = P.sb(shape, dt, name); b = Buf()
        P.dma("sp", P.chan(), t[:], src, (), (b,))
        return t, b

    ident_f, identfb = cload(cst["ident"], [128, 128], name="identf")
    ident_b = P.sb([128, 128], BF16, "identb"); identb = Buf()
    CP(P, "dve", ident_b[:], ident_f[:], (identfb,), (identb,))
    ntf, ntfb = cload(cst["negtri"], [128, 128], name="ntf")
    negtri = P.sb([128, 128], BF16, "negtri"); negtrib = Buf()
    CP(P, "dve", negtri[:], ntf[:], (ntfb,), (negtrib,))
    ones_col = P.sb([128, 1], BF16, "onec"); onecb = Buf()
    MEMSET(P, "dve", ones_col[:], 1.0, (onecb,))
    stg = P.sb([128, 4, 512], F32, "stg"); stgb = Buf()
    masks = {}
    for nm in ("negmaskC", "negmaskD", "mask01D"):
        P.dma("sp", P.chan(), stg[:], cst[nm], (), (stgb,))
        mt = P.sb([128, 4, 512], BF16, nm); mb_ = Buf()
        CP(P, "dve", mt[:], stg[:], (stgb,), (mb_,))
        masks[nm] = (mt, mb_)

    QT = P.sb([128, S], BF16, "QT"); QTb = Buf()
    KT = P.sb([128, S], BF16, "KT"); KTb = Buf()
    V = P.sb([128, NB, 65], BF16, "V"); Vb = Buf()
    MEMSET(P, "pool", V[:, :, 64:65], 1.0, (Vb,))
    qch, kch, vch = P.chan(), P.chan(), P.chan()
    nbl = Tc // 128

    def load_qkv(m):
        ntq = io["ntq"]
        tq_ = Tc // ntq
        nbq = tq_ // 128
        for i in range(4):
            for tq in range(ntq):
                c0 = i * Tc + tq * tq_
                P.dma("sp", qch, QT[0:64, c0:c0 + tq_], io["qk"](i, 2 * m, tq), (), (QTb,))
                P.dma("sp", kch, KT[0:64, c0:c0 + tq_], io["qk"](i, 2 * m + 1, tq), (), (KTb,))
                b0 = i * nbl + tq * nbq
                P.dma("sp", vch, V[:, b0:b0 + nbq, 0:64], io["v"](i, m, tq), (), (Vb,))

    banks = [(P.ps([128, 512], F32, "bk"), Buf(True)) for _ in range(8)]
    Pts = Ring([(P.sb([128, 512], BF16, "Pt"), Buf()) for _ in range(4)])
    ysts = Ring([(P.sb([128, 4, 64], io.get("y_dt", F32), "yst"), Buf(), P.chan("pool")) for _ in range(2)])
    after_mixer = io.get("after_mixer", lambda mi, bufs: None)
    ybufs = [r[1] for r in ysts.items]
    rc = P.sb([128, 4], F32, "rc"); rcb = Buf()

    def y_out(yst, ystb, ych, mi, tok0, nsub):
        i, off = tok0 // Tc, tok0 % Tc
        P.dma("pool", ych, io["y_w"](i, mi, off, nsub), yst[:, 0:nsub, :], (ystb,), ())

    if "A" in mixers:
        load_qkv(0)
        bA, bAb = cload(io["biasA"], [128, 5, 128], name="bA")
        mA, mAb = cload(cst["maskA"], [128, 5, 128], name="mA")
        TT(P, "dve", bA[:], bA[:], mA[:], ALU.add, (bAb, mAb), (bAb,))
        BAhi = P.sb([128, 5, 128], BF16, "BAhi"); BAlo = P.sb([128, 5, 128], BF16, "BAlo"); BAb = Buf()
        CP(P, "dve", BAhi[:], bA[:], (bAb,), (BAb,))
        TT(P, "dve", BAlo[:], bA[:], BAhi[:], ALU.subtract, (bAb, BAb), (BAb,))
        Sr = Ring(banks[0:4]); Or = Ring(banks[4:6])
        tiles = []
        for qb in range(NB):
            kbs = list(range(max(0, qb - 4), qb + 1))
            Obk, Ob = Or.next()
            qs = slice(qb * 128, (qb + 1) * 128)
            for idx, kb in enumerate(kbs):
                j = kb - qb + 4
                ks = slice(kb * 128, (kb + 1) * 128)
                Sbk, Sb = Sr.next()
                Pt, Ptb = Pts.next()

                def st0(Sbk=Sbk, Sb=Sb, ks=ks, qs=qs, j=j):
                    MM(P, Sbk[:, 0:128], KT[0:64, ks], QT[0:64, qs], True, False, (KTb, QTb), (Sb,))
                    MM(P, Sbk[:, 0:128], ident_b[:], BAhi[:, j, :], False, False, (identb, BAb), (Sb,))
                    MM(P, Sbk[:, 0:128], ident_b[:], BAlo[:, j, :], False, True, (identb, BAb), (Sb,))

                def st1(Sbk=Sbk, Sb=Sb, Pt=Pt, Ptb=Ptb):
                    ACT(P, Pt[:, 0:128], Sbk[:, 0:128], AF.Exp, (Sb,), (Ptb,))

                def st2(Pt=Pt, Ptb=Ptb, Obk=Obk, Ob=Ob, kb=kb, idx=idx, n=len(kbs), qb=qb):
                    MM(P, Obk[:, 0:65], Pt[:, 0:128], V[:, kb, :], idx == 0, idx == n - 1, (Ptb, Vb), (Ob,))
                    if idx == n - 1:
                        yst, ystb, ych = ysts.next()
                        P.op("dve", lambda e, o=rc[:, 0:1], i=Obk[:, 64:65]: e.reciprocal(o, i), (Ob,), (rcb,))
                        TS(P, "dve", yst[:, 0, :], Obk[:, 0:64], rc[:, 0:1], None, ALU.mult, None, (Ob, rcb), (ystb,))
                        y_out(yst, ystb, ych, 0, qb * 128, 1)

                tiles.append((st0, st1, st2))
        run_pipeline(tiles, SKEW_AC)
        after_mixer(0, ybufs)

    if "C" in mixers:
        load_qkv(1)
        Ff = P.sb([128, 128], F32, "Ff"); Ffb = Buf()
        fch = P.chan()
        for i in range(4):
            P.dma("sp", fch, Ff[i * nbl:(i + 1) * nbl, :], io["f"](i), (), (Ffb,))
        bfc, bfcb = cload(io["bf_col"], [128, 1], name="bfc")
        TS(P, "dve", bfc[:], bfc[:], -1.0, None, ALU.mult, None, (bfcb,), (bfcb,))
        ACT(P, Ff[0:NB, :], Ff[0:NB, :], AF.Exp, (Ffb, bfcb), (Ffb,), bias=bfc[0:NB, :], scale=-1.0)
        ACT(P, Ff[0:NB, :], Ff[0:NB, :], AF.Ln, (Ffb,), (Ffb,), bias=1.0)
        tot = P.sb([128, 1], F32, "tot"); totb = Buf()
        P.op("dve", lambda e, o=tot[0:NB, :], i=Ff[0:NB, :]: e.reduce_sum(o, i, axis=AX.X), (Ffb,), (totb,))
        onesf = P.sb([128, 128], F32, "onesf"); onesfb = Buf()
        MEMSET(P, "dve", onesf[:], 1.0, (onesfb,))
        totbc = P.sb([128, 128], F32, "totbc"); totbcb = Buf()
        TS(P, "dve", totbc[0:NB, :], onesf[0:NB, :], tot[0:NB, :], None, ALU.mult, None, (onesfb, totb), (totbcb,))
        triU, triUb = cload(cst["triU"], [128, 128], name="triU")
        SU, SUb = cload(cst["SU"], [128, 128], name="SU")
        b6, b6b = banks[6]
        b7, b7b = banks[7]
        P.op("pe", lambda e, o=b6[:, 0:NB], i=Ff[0:NB, :], idn=ident_f[0:NB, 0:NB]: e.transpose(o, i, idn),
             (Ffb, identfb), (b6b,))
        LT = P.sb([128, 128], F32, "LT"); LTb = Buf()
        CP(P, "dve", LT[:, 0:NB], b6[:, 0:NB], (b6b,), (LTb,))
        MM(P, b7[0:NB, 0:128], LT[:, 0:NB], triU[:], True, False, (LTb, triUb), (b7b,))
        MM(P, b7[0:NB, 0:128], SU[0:NB, 0:NB], totbc[0:NB, :], False, True, (SUb, totbcb), (b7b,))
        cpos = P.sb([128, 128], F32, "cpos"); cposb = Buf()
        CP(P, "dve", cpos[0:NB, :], b7[0:NB, 0:128], (b7b,), (cposb,))
        parts = P.sb([128, 6, 128], BF16, "parts"); partsb = Buf()
        r1 = P.sb([128, 128], F32, "r1"); r1b = Buf()
        CP(P, "dve", parts[0:NB, 3, :], cpos[0:NB, :], (cposb,), (partsb,))
        TT(P, "dve", r1[0:NB, :], cpos[0:NB, :], parts[0:NB, 3, :], ALU.subtract, (cposb, partsb), (r1b,))
        CP(P, "dve", parts[0:NB, 4, :], r1[0:NB, :], (r1b,), (partsb,))
        TT(P, "dve", r1[0:NB, :], r1[0:NB, :], parts[0:NB, 4, :], ALU.subtract, (r1b, partsb), (r1b,))
        CP(P, "dve", parts[0:NB, 5, :], r1[0:NB, :], (r1b,), (partsb,))
        for r in range(3):
            TS(P, "dve", parts[0:NB, r, :], parts[0:NB, 3 + r, :], -1.0, None, ALU.mult, None, (partsb,), (partsb,))
        csb = Buf()
        P.dma("sp", P.chan(), io["cs"].rearrange("r (i p) -> i r p", p=128), parts[0:NB, :, :], (partsb,), (csb,))
        MEMSET(P, "dve", QT[64:70, :], 1.0, (QTb,))
        MEMSET(P, "dve", KT[64:70, :], 1.0, (KTb,))
        P.dma("sp", qch, QT[64:67, :], io["cs"][0:3, :], (csb,), (QTb,))
        P.dma("sp", kch, KT[67:70, :], io["cs"][3:6, :], (csb,), (KTb,))
        nmC, nmCb = masks["negmaskC"]
        Sr = Ring(banks[0:4]); Or = Ring(banks[4:6])
        tiles = []
        for qt in range(NQT):
            q0 = qt * 512
            qs = slice(q0, q0 + 512)
            nkb = 4 * qt + 4
            Obk, Ob = Or.next()
            O3 = Obk[:, 0:260].rearrange("p (s d) -> p s d", d=65)
            for kb in range(nkb):
                ks = slice(kb * 128, (kb + 1) * 128)
                j = kb - 4 * qt
                Sbk, Sb = Sr.next()
                Pt, Ptb = Pts.next()

                def st0(Sbk=Sbk, Sb=Sb, ks=ks, qs=qs, j=j):
                    MM(P, Sbk[:], KT[0:70, ks], QT[0:70, qs], True, j < 0, (KTb, QTb), (Sb,))
                    if j >= 0:
                        MM(P, Sbk[:], ident_b[:], nmC[:, j, :], False, True, (identb, nmCb), (Sb,))

                def st1(Sbk=Sbk, Sb=Sb, Pt=Pt, Ptb=Ptb):
                    ACT(P, Pt[:], Sbk[:], AF.Exp, (Sb,), (Ptb,))

                def st2(Pt=Pt, Ptb=Ptb, O3=O3, Ob=Ob, kb=kb, j=j, qt=qt, q0=q0, nkb=nkb):
                    for sub in range(4):
                        if j > sub:
                            continue
                        MM(P, O3[:, sub, :], Pt[:, sub * 128:(sub + 1) * 128], V[:, kb, :], kb == 0 and sub == 0,
                           kb == 4 * qt + sub, (Ptb, Vb), (Ob,), skip=True)
                    if kb == nkb - 1:
                        yst, ystb, ych = ysts.next()
                        P.op("dve", lambda e, o=rc[:, 0:4], i=O3[:, :, 64]: e.reciprocal(o, i), (Ob,), (rcb,))
                        TT(P, "dve", yst[:], O3[:, :, 0:64], bc_last(rc[:, 0:4], 64), ALU.mult, (Ob, rcb), (ystb,))
                        y_out(yst, ystb, ych, 1, q0, 4)

                tiles.append((st0, st1, st2))
        run_pipeline(tiles, SKEW_AC)
        after_mixer(1, ybufs)

    if "D" in mixers:
        load_qkv(2)
        nmD, nmDb = masks["negmaskD"]
        m01, m01b = masks["mask01D"]
        Zr = Ring(banks[0:2]); Ar = Ring(banks[2:4]); OCr = Ring(banks[4:6])
        Es = Ring([(P.sb([128, 512], F32, "E"), Buf()) for _ in range(3)])
        SPs = Ring([(P.sb([128, 512], BF16, "SP"), Buf()) for _ in range(7)])
        acc = P.sb([128, 4, 64], F32, "acc"); accb = Buf()
        tmp = P.sb([128, 4, 64], F32, "tmp"); tmpb = Buf()
        carry = P.sb([128, 4], F32, "carry"); carryb = Buf()
        ec = P.sb([128, 4], F32, "ec"); ecb = Buf()
        tiles = []
        for qt in range(NQT):
            q0 = qt * 512
            qs = slice(q0, q0 + 512)
            kmax = 4 * qt + 3
            for kb in range(kmax, -1, -1):
                ks = slice(kb * 128, (kb + 1) * 128)
                j = kb - 4 * qt
                Zbk, Zb = Zr.next()
                E, Eb = Es.next()
                SP, SPb = SPs.next()
                Abk, Ab = Ar.next()
                Pt, Ptb = Pts.next()
                OCbk, OCb = OCr.next()
                OC3 = OCbk[:, 0:260].rearrange("p (s d) -> p s d", d=65)

                def s0(Zbk=Zbk, Zb=Zb, ks=ks, qs=qs):
                    MM(P, Zbk[:], KT[0:64, ks], QT[0:64, qs], True, True, (KTb, QTb), (Zb,))

                def s1(Zbk=Zbk, Zb=Zb, E=E, Eb=Eb):
                    ACT(P, E[:], Zbk[:], AF.Exp, (Zb,), (Eb,))

                def s2(E=E, Eb=Eb, SP=SP, SPb=SPb, j=j):
                    ACT(P, SP[:], E[:], AF.Ln, (Eb,), (SPb,), bias=1.0)
                    if j >= 0:
                        TT(P, "dve", SP[:], SP[:], m01[:, j, :], ALU.mult, (SPb, m01b), (SPb,))

                def s3(Abk=Abk, Ab=Ab, SP=SP, SPb=SPb, ks=ks, qs=qs, j=j):
                    MM(P, Abk[:], KT[0:64, ks], QT[0:64, qs], True, False, (KTb, QTb), (Ab,))
                    MM(P, Abk[:], negtri[:], SP[:], False, j < 0, (negtrib, SPb), (Ab,))
                    if j >= 0:
                        MM(P, Abk[:], ident_b[:], nmD[:, j, :], False, True, (identb, nmDb), (Ab,))

                def s4(Abk=Abk, Ab=Ab, Pt=Pt, Ptb=Ptb):
                    ACT(P, Pt[:], Abk[:], AF.Exp, (Ab,), (Ptb,))

                def s5(Pt=Pt, Ptb=Ptb, SP=SP, SPb=SPb, OC3=OC3, OCb=OCb, kb=kb):
                    for sub in range(4):
                        ss = slice(sub * 128, (sub + 1) * 128)
                        MM(P, OC3[:, sub, 0:64], Pt[:, ss], V[:, kb, 0:64], True, True, (Ptb, Vb), (OCb,))
                        MM(P, OC3[:, sub, 64:65], SP[:, ss], ones_col[:], True, True, (SPb, onecb), (OCb,))

                def s6(OC3=OC3, OCb=OCb, kb=kb, kmax=kmax, q0=q0):
                    if kb == kmax:
                        CP(P, "dve", acc[:], OC3[:, :, 0:64], (OCb,), (accb,))
                        TS(P, "dve", carry[:], OC3[:, :, 64], -1.0, None, ALU.mult, None, (OCb,), (carryb,))
                    else:
                        ACT(P, ec[:], carry[:], AF.Exp, (carryb,), (ecb,))
                        TT(P, "dve", tmp[:], OC3[:, :, 0:64], bc_last(ec[:, 0:4], 64), ALU.mult, (OCb, ecb), (tmpb,))
                        TT(P, "pool", acc[:], acc[:], tmp[:], ALU.add, (accb, tmpb), (accb,))
                        TT(P, "dve", carry[:], carry[:], OC3[:, :, 64], ALU.subtract, (carryb, OCb), (carryb,))
                    if kb == 0:
                        yst, ystb, ych = ysts.next()
                        CP(P, "pool", yst[:], acc[:], (accb,), (ystb,))
                        y_out(yst, ystb, ych, 2, q0, 4)

                tiles.append((s0, s1, s2, s3, s4, s5, s6))
        run_pipeline(tiles, SKEW_D)
        after_mixer(2, ybufs)
    return [r[1] for r in ysts.items]


def emit_p3(P, io, T, cst, final):
    NT = T // 512
    ident_f = P.sb([128, 128], F32, "identf"); identfb = Buf()
    P.dma("sp", P.chan(), ident_f[:], cst["ident"], (), (identfb,))
    ident_b = P.sb([128, 128], BF16, "identb"); identb = Buf()
    CP(P, "dve", ident_b[:], ident_f[:], (identfb,), (identb,))
    Wo = P.sb([128, 8, 1024], BF16, "Wo"); Wobs = [Buf() for _ in range(8)]
    wst = [(P.sb([128, 1024], F32, "wst"), Buf(), P.chan()) for _ in range(2)]
    w_v = io["w_out"].rearrange("(c p) n -> p c n", p=128)
    for c in range(8):
        st, sb_, ch = wst[c % 2]
        P.dma("sp", ch, st[:], w_v[:, c, :], (), (sb_,))
        CP(P, "dve" if c % 2 == 0 else "pool", Wo[:, c, :], st[:], (sb_,), (Wobs[c],))
    bg = P.sb([128, 1024], F32, "bg"); bgb = Buf()
    P.dma("sp", P.chan(), bg[:], io["bgain_bc"], (), (bgb,))
    TS(P, "dve", bg[:], bg[:], 16.0, None, ALU.mult, None, (bgb,), (bgb,))
    if final:
        fg = P.sb([128, 8], F32, "fg"); fgb = Buf()
        P.dma("sp", P.chan(), fg[:], io["fg8"], (), (fgb,))
        TS(P, "dve", fg[:], fg[:], 32.0, None, ALU.mult, None, (fgb,), (fgb,))
        ones = P.sb([128, 128], BF16, "ones"); onesb = Buf()
        MEMSET(P, "dve", ones[:], 1.0, (onesb,))
        sq = P.sb([128, 8, 512], BF16, "sq"); sqb = Buf()
        rbc = P.sb([128, 512], F32, "rbc"); rbcb = Buf()

    y_dt = io.get("y_dt", F32)
    yts = Ring([(P.sb([128, 3, 256], y_dt, "yt"), P.sb([128, 256], F32, "ytB"), Buf(), P.chan()) for _ in range(2)])
    gts = Ring([(P.sb([128, 1024], F32, "gt"), Buf(), P.chan()) for _ in range(2)])
    xts = Ring([(P.sb([128, 8, 512], F32, "xt"), Buf(), P.chan()) for _ in range(2)])
    xns = Ring([(P.sb([128, 8, 512], F32, "xn"), [Buf() for _ in range(8)], P.chan("pool")) for _ in range(2)])
    mTs = Ring([(P.sb([128, 8, 512], BF16, "mT"), Buf()) for _ in range(2)])
    t1 = P.sb([128, 1024], F32, "t1"); t1b = Buf()
    m = P.sb([128, 1024], BF16, "m"); mb = Buf()
    junk = P.sb([128, 256], F32, "junk"); jb = Buf()
    ssq = P.sb([128, 4], F32, "ssq"); ssqb = Buf()
    tps = Ring([(P.ps([128, 1024], BF16, "tp"), Buf(True)) for _ in range(2)])
    obanks = Ring([(P.ps([128, 512], F32, "ob"), Buf(True)) for _ in range(4)])
    if final:
        sbank = (P.ps([128, 512], F32, "sbk"), Buf(True))
    x_v = io["xT"].rearrange("(c p) t -> p c t", p=128)
    o_v = io["xT_out"].rearrange("(c p) t -> p c t", p=128)
    for t in range(NT):
        ts = slice(t * 512, (t + 1) * 512)
        xt, xb, xch = xts.next()
        P.dma("sp", xch, xt[:], x_v[:, :, ts], (), (xb,))
        mT, mTb = mTs.next()
        for sub in range(4):
            tok0 = t * 512 + sub * 128
            yt3, ytB, ytb, ych = yts.next()
            for mi in range(3):
                P.dma("sp", ych, yt3[:, mi, :].rearrange("p (h d) -> p h d", h=4),
                      io["y"](mi, tok0), (), (ytb,))
            P.dma("sp", ych, ytB[:], io["yb"][tok0:tok0 + 128, :], (), (ytb,))
            ysrc = (yt3[:, 0, :], ytB[:], yt3[:, 1, :], yt3[:, 2, :])
            gt, gtb, gch = gts.next()
            P.dma("sp", gch, gt[:], io["gs"][tok0:tok0 + 128, :], (), (gtb,))
            for br in range(4):
                ACT(P, junk[:], ysrc[br], AF.Square, (ytb,), (jb, ssqb), accum=ssq[:, br:br + 1])
            RSQRT(P, ssq[:], ssq[:], 256.0 * EPS, (ssqb,), (ssqb,))
            TT(P, "pool", t1[:], gt[:], bg[:], ALU.mult, (gtb, bgb), (t1b,))
            for br in range(4):
                cs_ = slice(br * 256, (br + 1) * 256)
                STT(P, "dve", m[:, cs_], ysrc[br], ssq[:, br:br + 1], t1[:, cs_], ALU.mult, ALU.mult,
                    (ytb, ssqb, t1b), (mb,))
            tp, tpb = tps.next()
            for c in range(8):
                P.op("pe", lambda e, o=tp[:, c * 128:(c + 1) * 128], i=m[:, c * 128:(c + 1) * 128], idn=ident_b[:]:
                     e.transpose(o, i, idn), (mb, identb), (tpb,))
            CP(P, "act", mT[:, :, sub * 128:(sub + 1) * 128], tp[:].rearrange("p (c t) -> p c t", c=8), (tpb,), (mTb,))
        xn, xnbs, xnch = xns.next()
        for cb in range(8):
            ob, obb = obanks.next()
            for c in range(8):
                MM(P, ob[:], Wo[:, c, cb * 128:(cb + 1) * 128], mT[:, c, :], c == 0, c == 7, (Wobs[c], mTb), (obb,))
            TT(P, "dve", xn[:, cb, :], xt[:, cb, :], ob[:], ALU.add, (xb, obb), (xnbs[cb],))
        if final:
            ACT(P, sq[:], xn[:], AF.Square, tuple(xnbs), (sqb,))
            sbk, sbb = sbank
            for c in range(8):
                MM(P, sbk[:], ones[:], sq[:, c, :], c == 0, c == 7, (onesb, sqb), (sbb,))
            RSQRT(P, rbc[:], sbk[:], 1024.0 * EPS, (sbb,), (rbcb,))
            for c in range(8):
                STT(P, "dve", xn[:, c, :], xn[:, c, :], fg[:, c:c + 1], rbc[:], ALU.mult, ALU.mult,
                    (xnbs[c], fgb, rbcb), (xnbs[c],))
        P.dma("pool", xnch, o_v[:, :, ts], xn[:], tuple(xnbs), ())
    return [b for r in xns.items for b in r[1]]


T_CORE = 4096
SEQ = 16384
_CACHE = {}


def _ext(P, name, shape, dt, kind):
    return P.dram(name, shape, dt, kind)


def static_p1_io(io):
    io["qk_w"] = lambda h, ti, t0: io["qk_send"][h, ti, :, t0:t0 + 512]
    io["v_w"] = lambda vi, tok0: io["v_send"][:, vi, tok0:tok0 + 128, :].rearrange("h t d -> t h d")


def static_p2_io(io):
    io["ntq"] = 1
    io["qk"] = lambda i, idx, tq: io["qk_recv"][i, idx]
    io["v"] = lambda i, m, tq: io["v_recv"][i, m].rearrange("(n p) d -> p n d", p=128)
    io["f"] = lambda i: io["f_recv"][i].rearrange("(n p) -> n p", p=128)
    io["y_w"] = lambda i, mi, off, nsub: io["y_send"][i, mi, off:off + nsub * 128, :].rearrange("(s p) d -> p s d", p=128)


def static_p3_io(io):
    io["y"] = lambda mi, tok0: io["y_recv"][:, mi, tok0:tok0 + 128, :].rearrange("h t d -> t h d")


def build_prog1():
    P = Prog(); T = T_CORE; io = {}
    io["xT"] = _ext(P, "xT", [1024, T], F32, "ExternalInput")
    io["w_in"] = _ext(P, "w_in", [1024, N_IN], F32, "ExternalInput")
    io["g8"] = _ext(P, "g8", [128, 8], F32, "ExternalInput")
    io["vgain_bc"] = _ext(P, "vgain_bc", [128, 256], F32, "ExternalInput")
    io["wsT"] = _ext(P, "wsT", [128, 4, 128], F32, "ExternalInput")
    io["bsT"] = _ext(P, "bsT", [128, 4], F32, "ExternalInput")
    cst = {"tril01T": _ext(P, "tril01T", [128, 128], F32, "ExternalInput")}
    io["qk_send"] = _ext(P, "qk_send", [4, 6, 64, T], BF16, "ExternalOutput")
    io["v_send"] = _ext(P, "v_send", [4, 3, T, 64], BF16, "ExternalOutput")
    io["f_send"] = _ext(P, "f_send", [4, T], F32, "ExternalOutput")
    io["gs"] = _ext(P, "gs", [T, 1024], F32, "ExternalOutput")
    io["yb"] = _ext(P, "yb", [T, 256], F32, "ExternalOutput")
    static_p1_io(io)
    outs = emit_p1(P, io, T, cst)
    P.wait_all("sp", outs)
    return P.emit()


P2_CONSTS = ("ident", "negtri", "triU", "SU", "negmaskC", "negmaskD", "mask01D", "maskA")


def build_prog2():
    P = Prog(); S = SEQ; Tc = T_CORE; io = {}
    io["qk_recv"] = _ext(P, "qk_recv", [4, 6, 64, Tc], BF16, "ExternalInput")
    io["v_recv"] = _ext(P, "v_recv", [4, 3, Tc, 64], BF16, "ExternalInput")
    io["f_recv"] = _ext(P, "f_recv", [4, Tc], F32, "ExternalInput")
    io["biasA"] = _ext(P, "biasA", [128, 5, 128], F32, "ExternalInput")
    io["bf_col"] = _ext(P, "bf_col", [128, 1], F32, "ExternalInput")
    io["y_send"] = _ext(P, "y_send", [4, 3, Tc, 64], F32, "ExternalOutput")
    io["cs"] = P.dram("cs", [6, S], BF16, "Internal")
    cn = host_consts()
    cst = {k: _ext(P, k, list(cn[k].shape), F32, "ExternalInput") for k in P2_CONSTS}
    static_p2_io(io)
    outs = emit_p2(P, io, S, Tc, cst)
    P.wait_all("sp", outs)
    return P.emit()


def build_prog3(final):
    P = Prog(); T = T_CORE; io = {}
    io["y_recv"] = _ext(P, "y_recv", [4, 3, T, 64], F32, "ExternalInput")
    io["yb"] = _ext(P, "yb", [T, 256], F32, "ExternalInput")
    io["gs"] = _ext(P, "gs", [T, 1024], F32, "ExternalInput")
    io["xT"] = _ext(P, "xT", [1024, T], F32, "ExternalInput")
    io["w_out"] = _ext(P, "w_out", [1024, 1024], F32, "ExternalInput")
    io["bgain_bc"] = _ext(P, "bgain_bc", [128, 1024], F32, "ExternalInput")
    if final:
        io["fg8"] = _ext(P, "fg8", [128, 8], F32, "ExternalInput")
    io["xT_out"] = _ext(P, "xT_out", [1024, T], F32, "ExternalOutput")
    cst = {"ident": _ext(P, "ident", [128, 128], F32, "ExternalInput")}
    static_p3_io(io)
    outs = emit_p3(P, io, T, cst, final)
    P.wait_all("sp", outs)
    return P.emit()


def _run(nc, in_maps):
    res = run_bass_kernel_spmd(nc, in_maps, core_ids=list(range(8)))
    return res.results


def kernel_unfused(x, norm_g, w_in, b_f, rel_bias, w_s, b_s, v_gain, branch_gain, w_out, final_g):
    f32 = np.float32
    x = np.asarray(x, f32)
    cn = host_consts()
    depth = 2
    c32 = lambda a: np.ascontiguousarray(np.asarray(a, f32))
    xT = [c32(x[c // 4, (c % 4) * T_CORE:(c % 4 + 1) * T_CORE, :].T) for c in range(8)]
    nc1 = build_prog1()
    nc2 = build_prog2()
    for l in range(depth):
        final = l == depth - 1
        com1 = dict(w_in=c32(w_in[l]), g8=c32(np.asarray(norm_g[l]).reshape(8, 128).T),
                    vgain_bc=c32(np.tile(np.asarray(v_gain[l])[None, :], (128, 1))),
                    wsT=c32(np.asarray(w_s[l]).transpose(2, 0, 1)), bsT=c32(np.asarray(b_s[l]).T),
                    tril01T=cn["tril01T"])
        r1 = _run(nc1, [dict(com1, xT=xT[c]) for c in range(8)])
        im2 = []
        for c in range(8):
            b, h = c // 4, c % 4
            d = dict(qk_recv=np.stack([r1[b * 4 + i]["qk_send"][h] for i in range(4)]),
                     v_recv=np.stack([r1[b * 4 + i]["v_send"][h] for i in range(4)]),
                     f_recv=np.stack([r1[b * 4 + i]["f_send"][h] for i in range(4)]),
                     biasA=host_biasA(np.asarray(rel_bias[l][h], f32)),
                     bf_col=np.full((128, 1), np.asarray(b_f, f32)[l, h], f32))
            for k in P2_CONSTS:
                d[k] = cn[k]
            im2.append(d)
        r2 = _run(nc2, im2)
        nc3 = build_prog3(final)
        com3 = dict(w_out=c32(w_out[l]), ident=cn["ident"],
                    bgain_bc=c32(np.tile(np.asarray(branch_gain[l]).reshape(1, 1024), (128, 1))))
        if final:
            com3["fg8"] = c32(np.asarray(final_g).reshape(8, 128).T)
        im3 = []
        for c in range(8):
            b, i = c // 4, c % 4
            im3.append(dict(com3, y_recv=np.stack([r2[b * 4 + h]["y_send"][i] for h in range(4)]),
                            yb=r1[c]["yb"], gs=r1[c]["gs"], xT=xT[c]))
        r3 = _run(nc3, im3)
        xT = [np.asarray(r3[c]["xT_out"]) for c in range(8)]
    out = np.empty((2, SEQ, D_MODEL), f32)
    for c in range(8):
        out[c // 4, (c % 4) * T_CORE:(c % 4 + 1) * T_CORE, :] = xT[c].T
    return out


GROUPS = [[0, 1, 2, 3], [4, 5, 6, 7]]


def build_fused(T=T_CORE, depth=2):
    S = 4 * T
    NTQ = 4 if T >= 2048 else 1
    TQ = T // NTQ
    NT8 = max(1, T // 512)
    T8 = T // NT8
    P = Prog()
    P.use_pid = True
    cn = host_consts()
    cst = {k: P.dram(k, list(v.shape), F32, "ExternalInput") for k, v in cn.items()}
    xT_in = P.dram("xT", [1024, T], F32, "ExternalInput")
    out_ext = P.dram("outT", [1024, T], F32, "ExternalOutput")
    xT_mid = P.dram("xT_mid", [1024, T], F32, "Internal")
    gs = P.dram("gs_i", [T, 1024], F32, "Internal")
    yb = P.dram("yb_i", [T, 256], F32, "Internal")
    cs = P.dram("cs_i", [6, S], BF16, "Internal")
    fg8 = P.dram("fg8", [128, 8], F32, "ExternalInput")
    cch = P.chan("coll")
    FSTOP = int(os.environ.get("FSTOP", "99"))

    def hsel(e):
        return bass.ds(P.pid_cache[id(e)], 1)

    def exchange(name, nch, ch, dt):
        send = P.dram(f"{name}_s", [nch, 4, ch], dt, "Internal")
        gath = P.dram(f"{name}_g", [nch, 16, ch], dt, "Internal")
        recv = P.dram(f"{name}_r", [nch, 4, ch], dt, "Internal")

        def gather(c0=0, c1=nch, wait_bufs=()):
            for c in range(c0, c1):
                P.coll(cch, "AllGather", GROUPS, send[c], gath[c], (), (), after=tuple(wait_bufs))

        def finish():
            P.barrier()
            d = Buf()
            g4 = gath.rearrange("c (a h) x -> c a h x", h=4)
            P.dma("sp", P.chan(), recv.rearrange("c a (o x) -> c a o x", o=1),
                  lambda e: g4[:, :, hsel(e), :], (), (d,))
        return send, recv, gather, finish

    for l in range(depth):
        final = l == depth - 1
        ext = lambda nm, shp: P.dram(f"{nm}{l}", shp, F32, "ExternalInput")
        qk_s, qk_r, qk_g, qk_f = exchange(f"qk{l}", 6 * NTQ, 64 * TQ, BF16)
        v_s, v_r, v_g, v_f = exchange(f"v{l}", 3 * NTQ, TQ * 64, BF16)
        f_s, f_r, f_g, f_f = exchange(f"f{l}", 1, T, F32)
        y_s, y_r, y_g, y_f = exchange(f"y{l}", 3 * NT8, T8 * 64, BF16)
        io1 = dict(xT=xT_in if l == 0 else xT_mid, w_in=ext("w_in", [1024, N_IN]), g8=ext("g8_", [128, 8]),
                   vgain_bc=ext("vgain_bc", [128, 256]), wsT=ext("wsT", [128, 4, 128]), bsT=ext("bsT", [128, 4]),
                   f_send=f_s[0], gs=gs, yb=yb)
        io1["qk_w"] = lambda h, ti, t0, qk_s=qk_s: qk_s[ti * NTQ + t0 // TQ, h].rearrange(
            "(d t) -> d t", d=64)[:, t0 % TQ:t0 % TQ + 512]
        io1["v_w"] = lambda vi, tok0, v_s=v_s: v_s[vi * NTQ + tok0 // TQ].rearrange(
            "h (t d) -> t h d", d=64)[tok0 % TQ:tok0 % TQ + 128]
        tpq = TQ // 512

        def after_tile(t, bufs, qk_g=qk_g, v_g=v_g):
            if (t + 1) % tpq == 0:
                tq = t // tpq
                for ti in range(6):
                    qk_g(ti * NTQ + tq, ti * NTQ + tq + 1, bufs)
                for vi in range(3):
                    v_g(vi * NTQ + tq, vi * NTQ + tq + 1, bufs)

        io1["after_tile"] = after_tile
        P.phase_begin()
        if not os.environ.get("SKIP1"):
            emit_p1(P, io1, T, cst)
        P.phase_end()
        if FSTOP <= 1:
            break
        f_g()
        qk_f(); v_f(); f_f()
        P.barrier()
        if FSTOP <= 3:
            break
        io2 = dict(biasA=ext("biasA", [128, 5, 128]), bf_col=ext("bf_col", [128, 1]), cs=cs, ntq=NTQ, y_dt=BF16)
        io2["after_mixer"] = lambda mi, bufs, y_g=y_g: y_g(mi * NT8, (mi + 1) * NT8, bufs)
        io2["qk"] = lambda i, idx, tq, qk_r=qk_r: qk_r[idx * NTQ + tq, i].rearrange("(d t) -> d t", d=64)
        io2["v"] = lambda i, m, tq, v_r=v_r: v_r[m * NTQ + tq, i].rearrange("(n p d) -> p n d", p=128, d=64)
        io2["f"] = lambda i, f_r=f_r: f_r[0, i].rearrange("(n p) -> n p", p=128)
        io2["y_w"] = lambda i, mi, off, nsub, y_s=y_s: y_s[mi * NT8 + off // T8, i].rearrange(
            "(t d) -> t d", d=64)[off % T8:off % T8 + nsub * 128].rearrange("(s p) d -> p s d", p=128)
        P.phase_begin()
        emit_p2(P, io2, S, T, cst)
        P.phase_end()
        if FSTOP <= 4:
            break
        y_f()
        P.barrier()
        io3 = dict(yb=yb, gs=gs, xT=io1["xT"], w_out=ext("w_out", [1024, 1024]), bgain_bc=ext("bgain_bc", [128, 1024]),
                   fg8=fg8, xT_out=out_ext if final else xT_mid, y_dt=BF16)
        io3["y"] = lambda mi, tok0, y_r=y_r: y_r[mi * NT8 + tok0 // T8].rearrange(
            "h (t d) -> t h d", d=64)[tok0 % T8:tok0 % T8 + 128]
        P.phase_begin()
        final_bufs = emit_p3(P, io3, T, cst, final)
        if final:
            P.wait_all("sp", final_bufs)
        P.phase_end()
    print("fused ops:", {k: len(v) for k, v in P.ops.items()}, "sems", P.nsem, flush=True)
    return P.emit()


def fused_inputs(x, norm_g, w_in, b_f, rel_bias, w_s, b_s, v_gain, branch_gain, w_out, final_g, T=T_CORE, depth=2):
    f32 = np.float32
    c32 = lambda a: np.ascontiguousarray(np.asarray(a, f32))
    cn = host_consts()
    com = dict(cn)
    com["fg8"] = c32(np.asarray(final_g).reshape(8, 128).T)
    for l in range(depth):
        com[f"w_in{l}"] = c32(w_in[l])
        com[f"g8_{l}"] = c32(np.asarray(norm_g[l]).reshape(8, 128).T)
        com[f"vgain_bc{l}"] = c32(np.tile(np.asarray(v_gain[l])[None, :], (128, 1)))
        com[f"wsT{l}"] = c32(np.asarray(w_s[l]).transpose(2, 0, 1))
        com[f"bsT{l}"] = c32(np.asarray(b_s[l]).T)
        com[f"w_out{l}"] = c32(w_out[l])
        com[f"bgain_bc{l}"] = c32(np.tile(np.asarray(branch_gain[l]).reshape(1, 1024), (128, 1)))
    ims = []
    for c in range(8):
        b, h = c // 4, c % 4
        d = dict(com)
        d["xT"] = c32(np.asarray(x)[b, h * T:(h + 1) * T, :].T)
        for l in range(depth):
            d[f"biasA{l}"] = host_biasA(np.asarray(rel_bias[l][h], f32))
            d[f"bf_col{l}"] = np.full((128, 1), np.asarray(b_f, f32)[l, h], f32)
        ims.append(d)
    return ims


def kernel(x, norm_g, w_in, b_f, rel_bias, w_s, b_s, v_gain, branch_gain, w_out, final_g):
    nc = build_fused()
    ims = fused_inputs(x, norm_g, w_in, b_f, rel_bias, w_s, b_s, v_gain, branch_gain, w_out, final_g)
    res = run_bass_kernel_spmd(nc, ims, core_ids=list(range(8))).results
    out = np.empty((2, SEQ, D_MODEL), np.float32)
    for c in range(8):
        out[c // 4, (c % 4) * T_CORE:(c % 4 + 1) * T_CORE, :] = np.asarray(res[c]["outT"]).T
    return out
```

```python
import os
import numpy as np
import ml_dtypes
from contextlib import ExitStack
import concourse.bass as bass
import concourse.mybir as mybir
from concourse.bass_utils import run_bass_kernel_spmd

F32 = mybir.dt.float32
BF16 = mybir.dt.bfloat16
AF = mybir.ActivationFunctionType
ALU = mybir.AluOpType
AX = mybir.AxisListType
NPBF = ml_dtypes.bfloat16

EPS = 1e-6
NEG = -30000.0
D_MODEL = 1024
N_IN = 3844
COL = dict(qA=0, kA=256, vA=512, gA=768, uB=1024, vB=1280, gB=1536, qC=1792, kC=2048,
           vC=2304, gC=2560, fC=2816, qD=2820, kD=3076, vD=3332, gD=3588)


class Buf:
    __slots__ = ("w", "r", "excl")

    def __init__(self, excl=False):
        self.w = None
        self.r = {}
        self.excl = excl


class Chan:
    _n = 0

    def __init__(self, sem):
        self.sem = sem
        self.val = 0
        Chan._n += 1
        self.uid = Chan._n


class Prog:
    ENGS = ("pe", "act", "dve", "pool", "sp")
    EPOCH = 8000

    def __init__(self):
        self.nc = bass.Bass("TRN2", target_bir_lowering=False)
        self.es = ExitStack()
        self.ops = {e: [] for e in self.ENGS}
        self.cnt = {e: 0 for e in self.ENGS}
        self.esems = {e: [] for e in self.ENGS}
        self.waited = {e: {} for e in self.ENGS}
        self.nsem = 0
        self.ntens = 0
        self.live_dma = {}
        self.free_chans = {}
        self.phase_chans = []
        self.pid_cache = {}
        self.use_pid = False
        self.arena = None
        self.arena_ptr = 0
        self.bank_ptr = 0
        self.es0 = ExitStack()

    def phase_begin(self):
        self.phase_chans = []
        self.arena_mark = (self.arena_ptr, self.bank_ptr)

    def phase_end(self):
        self.barrier()
        self.arena_ptr, self.bank_ptr = self.arena_mark
        for ch in self.phase_chans:
            self.free_chans.setdefault(ch.kind, []).append(ch)
        self.phase_chans = []

    def new_sem(self, name):
        self.nsem += 1
        return self.nc.alloc_semaphore(name=f"{name}{self.nsem}")

    def chan(self, kind="sp"):
        pool = self.free_chans.setdefault(kind, [])
        if pool:
            ch = pool.pop()
        else:
            ch = Chan(self.new_sem("ch"))
            ch.kind = kind
        self.phase_chans.append(ch)
        return ch

    ARENA_F32 = 45056
    _LET = "abcdefg"

    def _arena_init(self):
        if self.arena is None:
            self.arena = self.es0.enter_context(self.nc.sbuf_tensor("arena", [128, self.ARENA_F32], F32))
            self.psum_all = self.es0.enter_context(self.nc.psum_tensor("psum_all", [128, 4096], F32))
            self.banks = [self.psum_all[:, i * 512:(i + 1) * 512] for i in range(8)]

    def sb(self, shape, dt, name=None):
        self._arena_init()
        esz = 2 if dt == BF16 else 4
        n = int(np.prod(shape[1:]))
        n4 = (n * esz + 31) // 32 * 8
        off = self.arena_ptr
        assert off + n4 <= self.ARENA_F32, f"SBUF arena overflow ({name})"
        self.arena_ptr += n4
        ap = self.arena[0:shape[0], off:off + n4]
        if dt != F32:
            ap = ap.bitcast(dt)
        ap = ap[:, 0:n]
        if len(shape) > 2:
            names = " ".join(self._LET[:len(shape) - 1])
            kw = {self._LET[i]: shape[1 + i] for i in range(len(shape) - 1)}
            ap = ap.rearrange(f"p ({names}) -> p {names}", **kw)
        return ap

    def ps(self, shape, dt, name=None):
        self._arena_init()
        assert self.bank_ptr < 8, "out of PSUM banks"
        nb = 2 if (int(np.prod(shape[1:])) * (2 if dt == BF16 else 4)) > 2048 else 1
        if nb == 2:
            self.bank_ptr += self.bank_ptr % 2
            bk = self.psum_all[:, self.bank_ptr * 512:(self.bank_ptr + 2) * 512]
        else:
            bk = self.banks[self.bank_ptr][:]
        self.bank_ptr += nb
        assert self.bank_ptr <= 8, "out of PSUM banks"
        if dt != F32:
            bk = bk.bitcast(dt)
        return bk[0:shape[0], 0:int(np.prod(shape[1:]))]

    def dram(self, name, shape, dt, kind="Internal"):
        return self.nc.dram_tensor(name, list(shape), dt, kind=kind).ap()

    def _need(self, eng, ev):
        if ev is None:
            return None
        if ev[0] == "e":
            _, e2, idx = ev
            if e2 == eng and eng == "pe":
                return None
            key = ("e", e2)
            if self.waited[eng].get(key, 0) >= idx:
                return None
            self.waited[eng][key] = idx
            ep = (idx - 1) // self.EPOCH
            return (self.esems[e2][ep], (idx - 1) % self.EPOCH + 1)
        _, ch, val = ev
        key = ("d", ch.uid)
        if self.waited[eng].get(key, 0) >= val:
            return None
        self.waited[eng][key] = val
        return (ch.sem, val)

    def _deps(self, eng, reads, writes):
        evs = []
        for b in reads:
            evs.append(b.w)
            if b.excl:
                evs.extend(ev for ev in b.r.values() if not (ev[0] == "e" and ev[1] == eng))
        for b in writes:
            evs.append(b.w)
            evs.extend(b.r.values())
        return [w for w in (self._need(eng, ev) for ev in evs) if w]

    @staticmethod
    def _mark(ev, key, reads, writes):
        for b in reads:
            b.r[key] = ev
        for b in writes:
            b.w = ev
            b.r = {}

    def op(self, eng, fn, reads=(), writes=()):
        waits = self._deps(eng, reads, writes)
        self.cnt[eng] += 1
        idx = self.cnt[eng]
        ep = (idx - 1) // self.EPOCH
        while len(self.esems[eng]) <= ep:
            self.esems[eng].append(self.new_sem(eng))
        self.ops[eng].append((waits, fn, self.esems[eng][ep], 1))
        ev = ("e", eng, idx)
        self._mark(ev, ("e", eng), reads, writes)
        return ev

    def dma(self, eng, ch, out, in_, reads=(), writes=(), slow=False):
        waits = self._deps(eng, reads, writes)
        ch.val += 16
        import traceback as _tb
        site = _tb.extract_stack(limit=4)[:-1] if os.environ.get("DEBUGDMA") else None

        def fn(e, o=out, i=in_, slow=slow, site=site):
            o = o(e) if callable(o) else o
            i = i(e) if callable(i) else i
            if slow:
                return e.dma_start(out=o, in_=i, allow_slow_non_contiguous=True)
            try:
                r = e.dma_start(out=o, in_=i)
                self.ndma_ok = getattr(self, "ndma_ok", 0) + 1
                return r
            except Exception:
                print("DMA ok before:", getattr(self, "ndma_ok", 0), "site", site, flush=True)
                print("DMA FAIL out", o.shape, o.ap, "in", i.shape, i.ap, flush=True)
                raise
        self.ops[eng].append((waits, fn, ch.sem, 16))
        ev = ("d", ch, ch.val)
        self.live_dma[ch.uid] = ev
        self._mark(ev, ("d", ch.uid), reads, writes)
        return ev

    def coll(self, ch, kind, groups, in_ap, out_ap, reads=(), writes=(), inc=1, after=()):
        eng = "pool"
        waits = self._deps(eng, reads, writes) + self._deps(eng, (), after)
        ch.val += inc
        fn = lambda e, k=kind, g=groups, i=in_ap, o=out_ap: e.collective_compute(
            k, ALU.bypass, replica_groups=g, ins=[i], outs=[o])
        self.ops[eng].append((waits, fn, ch.sem, inc))
        ev = ("d", ch, ch.val)
        self.live_dma[ch.uid] = ev
        self._mark(ev, ("d", ch.uid), reads, writes)
        return ev

    def barrier(self):
        evs = [("e", e, self.cnt[e]) for e in self.ENGS if self.cnt[e] > 0]
        evs += list(self.live_dma.values())
        for eng in self.ENGS:
            waits = [w for w in (self._need(eng, ev) for ev in evs) if w]
            if waits:
                self.ops[eng].append((waits, None, None, 0))

    def wait_all(self, eng, bufs):
        waits = self._deps(eng, bufs, ())
        self.ops[eng].append((waits, None, None, 0))

    def flush(self):
        nc = self.nc
        if not any(self.ops[e] for e in self.ENGS):
            return
        with nc.Block() as block:
            table = (("pe", block.tensor), ("act", block.scalar), ("dve", block.vector),
                     ("pool", block.gpsimd), ("sp", block.sync))
            for name, deco in table:
                ops = self.ops[name]
                if not ops:
                    continue

                def body(engine, ops=ops, name=name):
                    if self.use_pid and name == "sp":
                        self.pid_cache[id(engine)] = engine.partition_id() % 4
                    for waits, fn, sem, inc in ops:
                        for (s, v) in waits:
                            engine.wait_ge(s, v)
                        if fn is not None:
                            try:
                                fn(engine).then_inc(sem, inc)
                            except Exception:
                                print("FAILED OP on", name, "waits", [(str(s), v) for s, v in waits], flush=True)
                                raise

                deco(body)
        self.ops = {e: [] for e in self.ENGS}

    def emit(self):
        self.flush()
        self.es.close()
        self.es0.close()
        return self.nc


def MM(P, out, lhsT, rhs, start, stop, reads, writes, skip=False):
    if skip:
        fn = lambda e, o=out, l=lhsT, r=rhs, a=start, b=stop: e.matmul(o, l, r, start=a, stop=b, skip_group_check=True)
    else:
        fn = lambda e, o=out, l=lhsT, r=rhs, a=start, b=stop: e.matmul(o, l, r, start=a, stop=b)
    return P.op("pe", fn, reads, writes)


def ACT(P, out, in_, func, reads, writes, bias=None, scale=None, accum=None):
    kw = {}
    if bias is not None:
        kw["bias"] = bias
    if scale is not None:
        kw["scale"] = scale
    if accum is not None:
        kw["accum_out"] = accum
    fn = lambda e, o=out, i=in_, f=func, kw=kw: e.activation(o, i, f, **kw)
    return P.op("act", fn, reads, writes)


def TS(P, eng, out, in0, s1, s2, op0, op1, reads, writes):
    if op1 is None:
        fn = lambda e, o=out, i=in0, a=s1, p0=op0: e.tensor_scalar(o, i, a, None, p0)
    else:
        fn = lambda e, o=out, i=in0, a=s1, b=s2, p0=op0, p1=op1: e.tensor_scalar(o, i, a, b, p0, p1)
    return P.op(eng, fn, reads, writes)


def TT(P, eng, out, in0, in1, op, reads, writes):
    fn = lambda e, o=out, a=in0, b=in1, p=op: e.tensor_tensor(o, a, b, p)
    return P.op(eng, fn, reads, writes)


def STT(P, eng, out, in0, scalar, in1, op0, op1, reads, writes):
    fn = lambda e, o=out, a=in0, s=scalar, b=in1, p0=op0, p1=op1: e.scalar_tensor_tensor(o, a, s, b, p0, p1)
    return P.op(eng, fn, reads, writes)


def CP(P, eng, out, in_, reads, writes):
    if eng == "act":
        fn = lambda e, o=out, i=in_: e.copy(o, i)
    else:
        fn = lambda e, o=out, i=in_: e.tensor_copy(o, i)
    return P.op(eng, fn, reads, writes)


def RSQRT(P, out, in_, eps, reads, writes):
    TS(P, "dve", out, in_, eps, None, ALU.add, None, reads, writes)
    ACT(P, out, out, AF.Sqrt, writes, writes)
    P.op("dve", lambda e, o=out: e.reciprocal(o, o), writes, writes)


def MEMSET(P, eng, ap, val, writes):
    fn = lambda e, a=ap, v=val: e.memset(a, v)
    return P.op(eng, fn, (), writes)


class Ring:
    def __init__(self, items):
        self.items = items
        self.i = 0

    def next(self):
        it = self.items[self.i % len(self.items)]
        self.i += 1
        return it


def host_consts():
    k = np.arange(128)
    c = {}
    c["ident"] = np.eye(128, dtype=np.float32)
    c["negtri"] = np.where(k[:, None] >= k[None, :], -1.0, 0.0).astype(np.float32)
    c["triU"] = (k[:, None] <= k[None, :]).astype(np.float32)
    c["SU"] = (k[:, None] < k[None, :]).astype(np.float32)
    q = np.arange(512)
    kk = (np.arange(4)[None, :, None] * 128 + k[:, None, None])
    c["negmaskC"] = np.where(kk <= q[None, None, :], 0.0, NEG).astype(np.float32)
    c["negmaskD"] = np.where(kk < q[None, None, :], 0.0, NEG).astype(np.float32)
    c["mask01D"] = (kk < q[None, None, :]).astype(np.float32)
    q1 = np.arange(128)
    kc = (2 * (np.arange(5)[None, :, None] - 4) + (k[:, None, None] // 64))
    cq = (q1[None, None, :] // 64)
    valid = (kc >= cq - 8) & (kc <= cq)
    c["maskA"] = np.where(valid, 0.0, NEG).astype(np.float32)
    c["tril01T"] = (k[:, None] <= k[None, :]).astype(np.float32)
    return c


def host_biasA(rel_bias_h):
    k = np.arange(128)
    q1 = np.arange(128)
    kpos = ((np.arange(5)[None, :, None] - 4) * 128 + k[:, None, None])
    d = np.clip(q1[None, None, :] - kpos, -128, 128) + 128
    return np.ascontiguousarray(rel_bias_h[d]).astype(np.float32)


import os
DBG = int(os.environ.get("P1DBG", "9"))
SILU = AF.Copy if os.environ.get("NOSILU") else AF.Silu


def emit_p1(P, io, T, cst):
    NT = T // 512
    W = P.sb([128, 8, N_IN], BF16, "W"); Wbs = [Buf() for _ in range(8)]
    HW_ = N_IN // 2
    wst = [(P.sb([128, HW_], F32, "wst"), Buf(), P.chan()) for _ in range(2)]
    w_v = io["w_in"].rearrange("(c p) n -> p c n", p=128)
    g32 = P.sb([128, 8], F32, "g32"); g32b = Buf()
    P.dma("sp", P.chan(), g32[:], io["g8"], (), (g32b,))
    TS(P, "dve", g32[:], g32[:], 32.0, None, ALU.mult, None, (g32b,), (g32b,))
    for c in range(8):
        for hf in range(2):
            st, sb_, ch = wst[hf]
            cs_ = slice(hf * HW_, (hf + 1) * HW_)
            P.dma("sp", ch, st[:], w_v[:, c, cs_], (), (sb_,))
            TS(P, "dve", W[:, c, cs_], st[:], g32[:, c:c + 1], None, ALU.mult, None, (sb_, g32b), (Wbs[c],))
    vg = P.sb([128, 256], F32, "vg"); vgb = Buf()
    P.dma("sp", P.chan(), vg[:], io["vgain_bc"], (), (vgb,))
    bsT = P.sb([128, 4], F32, "bsT"); bsb = Buf()
    P.dma("sp", P.chan(), bsT[:], io["bsT"], (), (bsb,))
    wsf = P.sb([128, 4, 128], F32, "wsf"); wsfb = Buf()
    P.dma("sp", P.chan(), wsf[:], io["wsT"], (), (wsfb,))
    trl = P.sb([128, 128], F32, "trl"); trlb = Buf()
    P.dma("sp", P.chan(), trl[:], cst["tril01T"], (), (trlb,))
    Wtr = P.sb([128, 4, 128], BF16, "Wtr"); Wtrb = Buf()
    for g in range(4):
        TT(P, "dve", Wtr[:, g, :], wsf[:, g, :], trl[:], ALU.mult, (wsfb, trlb), (Wtrb,))
    ones = P.sb([128, 128], BF16, "ones"); onesb = Buf()
    MEMSET(P, "dve", ones[:], 1.0, (onesb,))

    xts = Ring([(P.sb([128, 8, 512], F32, "xt"), Buf(), P.chan()) for _ in range(2)])
    sq = P.sb([128, 8, 512], BF16, "sq"); sqb = Buf()
    rbc = P.sb([128, 512], F32, "rbc"); rbcb = Buf()
    hTs = Ring([(P.sb([128, 8, 512], BF16, "hT"), [Buf() for _ in range(8)]) for _ in range(2)])
    banks = [(P.ps([128, 512], F32, "bk"), Buf(True)) for _ in range(8)]
    ssq_bank = banks[0]
    fm_banks = Ring(banks[1:3])
    tm_banks = Ring(banks[3:7])
    mix_bank = banks[7]
    fm_st = Ring([(P.sb([128, 512], BF16, "fmst"), Buf(), P.chan("pool")) for _ in range(3)])
    f_st = Ring([(P.sb([4, 512], F32, "fst"), Buf(), P.chan("pool")) for _ in range(2)])
    v_st = Ring([(P.sb([128, 3, 256], BF16, "vst"), Buf(), P.chan("pool")) for _ in range(2)])
    g_st = Ring([(P.sb([128, 1024], F32, "gst"), Buf(), P.chan("pool")) for _ in range(2)])
    yb_st = Ring([(P.sb([128, 256], F32, "ybst"), Buf(), P.chan("pool")) for _ in range(2)])
    u_sb = P.sb([128, 256], F32, "u"); ub = Buf()
    vtmp = P.sb([128, 256], F32, "vtmp"); vtb = Buf()
    vn = P.sb([128, 256], BF16, "vn"); vnb = Buf()
    junk = P.sb([128, 256], F32, "junk"); jb = Buf()
    st4 = P.sb([128, 8], F32, "st4"); st4b = Buf()

    x_v = io["xT"].rearrange("(c p) t -> p c t", p=128)
    FM = [("qA", 0, 0.125), ("kA", 1, 1.0), ("qC", 2, 0.125), ("kC", 3, 1.0), ("qD", 4, 0.125), ("kD", 5, 1.0)]
    evac_i = 0
    for t in range(NT):
        ts = slice(t * 512, (t + 1) * 512)
        xt, xb, xch = xts.next()
        P.dma("sp", xch, xt[:], x_v[:, :, ts], (), (xb,))
        ACT(P, sq[:], xt[:], AF.Square, (xb,), (sqb,))
        sbk, sbb = ssq_bank
        for c in range(8):
            MM(P, sbk[:], ones[:], sq[:, c, :], c == 0, c == 7, (onesb, sqb), (sbb,))
        RSQRT(P, rbc[:], sbk[:], 1024.0 * EPS, (sbb,), (rbcb,))
        hT, hb = hTs.next()
        for c in range(8):
            TT(P, "dve" if c % 2 == 0 else "pool", hT[:, c, :], xt[:, c, :], rbc[:], ALU.mult,
               (xb, rbcb), (hb[c],))
        for name, ti, scl in FM:
            for hp in range(2):
                c0 = COL[name] + hp * 128
                bk, bb = fm_banks.next()
                for c in range(8):
                    MM(P, bk[:], W[:, c, c0:c0 + 128], hT[:, c, :], c == 0, c == 7, (Wbs[c], hb[c]), (bb,))
                st, stb, sch = fm_st.next()
                evac_i += 1
                if evac_i % 2 == 0:
                    ACT(P, st[:], bk[:], AF.Copy, (bb,), (stb,), scale=scl)
                else:
                    TS(P, "dve", st[:], bk[:], scl, None, ALU.mult, None, (bb,), (stb,))
                P.dma("pool", sch, io["qk_w"](2 * hp, ti, t * 512), st[0:64, :], (stb,), ())
                P.dma("pool", sch, io["qk_w"](2 * hp + 1, ti, t * 512), st[64:128, :], (stb,), ())
        if DBG < 2:
            continue
        bk, bb = fm_banks.next()
        for c in range(8):
            MM(P, bk[0:4, :], W[:, c, COL["fC"]:COL["fC"] + 4], hT[:, c, :], c == 0, c == 7, (Wbs[c], hb[c]), (bb,))
        st, stb, sch = f_st.next()
        CP(P, "dve", st[:], bk[0:4, :], (bb,), (stb,))
        P.dma("pool", sch, io["f_send"][:, ts], st[:], (stb,), ())
        if DBG < 3:
            continue
        for sub in range(4):
            tok0 = t * 512 + sub * 128
            hs = slice(sub * 128, (sub + 1) * 128)

            def tm_group(c0, n, col_off=0, bank=None):
                bk_, bb_ = bank if bank is not None else tm_banks.next()
                for c in range(8):
                    MM(P, bk_[:, col_off:col_off + n], hT[:, c, hs], W[:, c, c0:c0 + n], c == 0, c == 7,
                       (Wbs[c], hb[c]), (bb_,))
                return bk_, bb_

            vst, vstb, vch = v_st.next()
            gst, gstb, gch = g_st.next()
            bk, bb = tm_group(COL["vA"], 512)
            CP(P, "dve", vst[:, 0, :], bk[:, 0:256], (bb,), (vstb,))
            ACT(P, gst[:, 0:256], bk[:, 256:512], SILU, (bb,), (gstb,))
            bk, bb = tm_group(COL["vC"], 512)
            CP(P, "dve", vst[:, 1, :], bk[:, 0:256], (bb,), (vstb,))
            ACT(P, gst[:, 512:768], bk[:, 256:512], SILU, (bb,), (gstb,))
            bk, bb = tm_group(COL["vD"], 512)
            CP(P, "dve", vst[:, 2, :], bk[:, 0:256], (bb,), (vstb,))
            ACT(P, gst[:, 768:1024], bk[:, 256:512], SILU, (bb,), (gstb,))
            bk, bb = tm_group(COL["gB"], 256)
            ACT(P, gst[:, 256:512], bk[:, 0:256], SILU, (bb,), (gstb,))
            for vi in range(3 if not os.environ.get("NOV") else 0):
                P.dma("pool", vch, io["v_w"](vi, tok0),
                      vst[:, vi, :].rearrange("p (h d) -> p h d", h=4), (vstb,), ())
            if not os.environ.get("NOG"):
                P.dma("pool", gch, io["gs"][tok0:tok0 + 128, :], gst[:], (gstb,), ())
            if DBG < 4:
                continue
            bk, bb = tm_group(COL["uB"], 512)
            CP(P, "act", u_sb[:], bk[:, 0:256], (bb,), (ub,))
            vps = bk[:, 256:512]
            P.op("dve", lambda e, o=st4[:, 0:1], i=vps: e.reduce_sum(o, i, axis=AX.X), (bb,), (st4b,))
            ACT(P, junk[:], vps, AF.Square, (bb,), (jb, st4b), accum=st4[:, 1:2])
            TS(P, "dve", st4[:, 2:3], st4[:, 0:1], -1.0 / 256, None, ALU.mult, None, (st4b,), (st4b,))
            TT(P, "dve", st4[:, 3:4], st4[:, 2:3], st4[:, 2:3], ALU.mult, (st4b,), (st4b,))
            STT(P, "dve", st4[:, 4:5], st4[:, 1:2], 1.0 / 256, st4[:, 3:4], ALU.mult, ALU.subtract, (st4b,), (st4b,))
            RSQRT(P, st4[:, 5:6], st4[:, 4:5], EPS, (st4b,), (st4b,))
            TS(P, "dve", vtmp[:], vps, st4[:, 2:3], st4[:, 5:6], ALU.add, ALU.mult, (bb, st4b), (vtb,))
            TT(P, "dve", vn[:], vtmp[:], vg[:], ALU.mult, (vtb, vgb), (vnb,))
            mk, mb = mix_bank
            for g in range(4):
                gs_ = slice(g * 64, (g + 1) * 64)
                MM(P, mk[:, gs_], Wtr[:, g, :], vn[:, gs_], True, True, (Wtrb, vnb), (mb,))
            yst, ystb, ych = yb_st.next()
            for g in range(4):
                gs_ = slice(g * 64, (g + 1) * 64)
                STT(P, "dve", yst[:, gs_], mk[:, gs_], bsT[:, g:g + 1], u_sb[:, gs_], ALU.add, ALU.mult,
                    (mb, bsb, ub), (ystb,))
            P.dma("pool", ych, io["yb"][tok0:tok0 + 128, :], yst[:], (ystb,), ())
        if "after_tile" in io:
            io["after_tile"](t, [r[1] for r in fm_st.items + v_st.items])
    outs = [r[1] for r in fm_st.items + f_st.items + v_st.items + g_st.items + yb_st.items]
    return outs


SKEW_AC = tuple(int(v) for v in os.environ.get("SKEW_AC", "0,0,2").split(","))
SKEW_D = tuple(int(v) for v in os.environ.get("SKEW_D", "0,0,1,2,3,4,5").split(","))


def run_pipeline(tiles, skews):
    n = len(tiles)
    for st in range(n + max(skews)):
        for j, sk in enumerate(skews):
            i = st - sk
            if 0 <= i < n:
                tiles[i][j]()


def bc_last(ap2, n):
    return ap2.unsqueeze(2).broadcast_to([ap2.shape[0], ap2.shape[1], n])


def emit_p2(P, io, S, Tc, cst, mixers=("A", "C", "D")):
    NB = S // 128
    NQT = S // 512
    assert NB <= 128

    def cload(src, shape, dt=F32, name="c"):
        t = P.sb(shape, dt, name); b = Buf()
        P.dma("sp", P.chan(), t[:], src, (), (b,))
        return t, b

    ident_f, identfb = cload(cst["ident"], [128, 128], name="identf")
    ident_b = P.sb([128, 128], BF16, "identb"); identb = Buf()
    CP(P, "dve", ident_b[:], ident_f[:], (identfb,), (identb,))
    ntf, ntfb = cload(cst["negtri"], [128, 128], name="ntf")
    negtri = P.sb([128, 128], BF16, "negtri"); negtrib = Buf()
    CP(P, "dve", negtri[:], ntf[:], (ntfb,), (negtrib,))
    ones_col = P.sb([128, 1], BF16, "onec"); onecb = Buf()
    MEMSET(P, "dve", ones_col[:], 1.0, (onecb,))
    stg = P.sb([128, 4, 512], F32, "stg"); stgb = Buf()
    masks = {}
    for nm in ("negmaskC", "negmaskD", "mask01D"):
        P.dma("sp", P.chan(), stg[:], cst[nm], (), (stgb,))
        mt = P.sb([128, 4, 512], BF16, nm); mb_ = Buf()
        CP(P, "dve", mt[:], stg[:], (stgb,), (mb_,))
        masks[nm] = (mt, mb_)

    QT = P.sb([128, S], BF16, "QT"); QTb = Buf()
    KT = P.sb([128, S], BF16, "KT"); KTb = Buf()
    V = P.sb([128, NB, 65], BF16, "V"); Vb = Buf()
    MEMSET(P, "pool", V[:, :, 64:65], 1.0, (Vb,))
    qch, kch, vch = P.chan(), P.chan(), P.chan()
    nbl = Tc // 128

    def load_qkv(m):
        ntq = io["ntq"]
        tq_ = Tc // ntq
        nbq = tq_ // 128
        for i in range(4):
            for tq in range(ntq):
                c0 = i * Tc + tq * tq_
                P.dma("sp", qch, QT[0:64, c0:c0 + tq_], io["qk"](i, 2 * m, tq), (), (QTb,))
                P.dma("sp", kch, KT[0:64, c0:c0 + tq_], io["qk"](i, 2 * m + 1, tq), (), (KTb,))
                b0 = i * nbl + tq * nbq
                P.dma("sp", vch, V[:, b0:b0 + nbq, 0:64], io["v"](i, m, tq), (), (Vb,))

    banks = [(P.ps([128, 512], F32, "bk"), Buf(True)) for _ in range(8)]
    psum_all = P.psum_all
    Pts = Ring([(P.sb([128, 512], BF16, "Pt"), Buf()) for _ in range(4)])
    ysts = Ring([(P.sb([128, 4, 64], io.get("y_dt", F32), "yst"), Buf(), P.chan("pool")) for _ in range(2)])
    after_mixer = io.get("after_mixer", lambda mi, bufs: None)
    ybufs = [r[1] for r in ysts.items]
    rc = P.sb([128, 4], F32, "rc"); rcb = Buf()

    def y_out(yst, ystb, ych, mi, tok0, nsub):
        i, off = tok0 // Tc, tok0 % Tc
        P.dma("pool", ych, io["y_w"](i, mi, off, nsub), yst[:, 0:nsub, :], (ystb,), ())

    if "A" in mixers:
        load_qkv(0)
        bA, bAb = cload(io["biasA"], [128, 5, 128], name="bA")
        mA, mAb = cload(cst["maskA"], [128, 5, 128], name="mA")
        TT(P, "dve", bA[:], bA[:], mA[:], ALU.add, (bAb, mAb), (bAb,))
        BAhi = P.sb([128, 5, 128], BF16, "BAhi"); BAlo = P.sb([128, 5, 128], BF16, "BAlo"); BAb = Buf()
        CP(P, "dve", BAhi[:], bA[:], (bAb,), (BAb,))
        TT(P, "dve", BAlo[:], bA[:], BAhi[:], ALU.subtract, (bAb, BAb), (BAb,))
        Sr = Ring(banks[0:4]); Or = Ring(banks[4:6])
        tiles = []
        for qb in range(NB):
            kbs = list(range(max(0, qb - 4), qb + 1))
            Obk, Ob = Or.next()
            qs = slice(qb * 128, (qb + 1) * 128)
            for idx, kb in enumerate(kbs):
                j = kb - qb + 4
                ks = slice(kb * 128, (kb + 1) * 128)
                Sbk, Sb = Sr.next()
                Pt, Ptb = Pts.next()

                def st0(Sbk=Sbk, Sb=Sb, ks=ks, qs=qs, j=j):
                    MM(P, Sbk[:, 0:128], KT[0:64, ks], QT[0:64, qs], True, False, (KTb, QTb), (Sb,))
                    MM(P, Sbk[:, 0:128], ident_b[:], BAhi[:, j, :], False, False, (identb, BAb), (Sb,))
                    MM(P, Sbk[:, 0:128], ident_b[:], BAlo[:, j, :], False, True, (identb, BAb), (Sb,))

                def st1(Sbk=Sbk, Sb=Sb, Pt=Pt, Ptb=Ptb):
                    ACT(P, Pt[:, 0:128], Sbk[:, 0:128], AF.Exp, (Sb,), (Ptb,))

                def st2(Pt=Pt, Ptb=Ptb, Obk=Obk, Ob=Ob, kb=kb, idx=idx, n=len(kbs), qb=qb):
                    MM(P, Obk[:, 0:65], Pt[:, 0:128], V[:, kb, :], idx == 0, idx == n - 1, (Ptb, Vb), (Ob,))
                    if idx == n - 1:
                        yst, ystb, ych = ysts.next()
                        P.op("dve", lambda e, o=rc[:, 0:1], i=Obk[:, 64:65]: e.reciprocal(o, i), (Ob,), (rcb,))
                        TS(P, "dve", yst[:, 0, :], Obk[:, 0:64], rc[:, 0:1], None, ALU.mult, None, (Ob, rcb), (ystb,))
                        y_out(yst, ystb, ych, 0, qb * 128, 1)

                tiles.append((st0, st1, st2))
        run_pipeline(tiles, SKEW_AC)
        after_mixer(0, ybufs)

    if "C" in mixers:
        load_qkv(1)
        Ff = P.sb([128, 128], F32, "Ff"); Ffb = Buf()
        fch = P.chan()
        for i in range(4):
            P.dma("sp", fch, Ff[i * nbl:(i + 1) * nbl, :], io["f"](i), (), (Ffb,))
        bfc, bfcb = cload(io["bf_col"], [128, 1], name="bfc")
        TS(P, "dve", bfc[:], bfc[:], -1.0, None, ALU.mult, None, (bfcb,), (bfcb,))
        ACT(P, Ff[0:NB, :], Ff[0:NB, :], AF.Exp, (Ffb, bfcb), (Ffb,), bias=bfc[0:NB, :], scale=-1.0)
        ACT(P, Ff[0:NB, :], Ff[0:NB, :], AF.Ln, (Ffb,), (Ffb,), bias=1.0)
        tot = P.sb([128, 1], F32, "tot"); totb = Buf()
        P.op("dve", lambda e, o=tot[0:NB, :], i=Ff[0:NB, :]: e.reduce_sum(o, i, axis=AX.X), (Ffb,), (totb,))
        onesf = P.sb([128, 128], F32, "onesf"); onesfb = Buf()
        MEMSET(P, "dve", onesf[:], 1.0, (onesfb,))
        totbc = P.sb([128, 128], F32, "totbc"); totbcb = Buf()
        TS(P, "dve", totbc[0:NB, :], onesf[0:NB, :], tot[0:NB, :], None, ALU.mult, None, (onesfb, totb), (totbcb,))
        triU, triUb = cload(cst["triU"], [128, 128], name="triU")
        SU, SUb = cload(cst["SU"], [128, 128], name="SU")
        b6, b6b = banks[6]
        b7, b7b = banks[7]
        P.op("pe", lambda e, o=b6[:, 0:NB], i=Ff[0:NB, :], idn=ident_f[0:NB, 0:NB]: e.transpose(o, i, idn),
             (Ffb, identfb), (b6b,))
        LT = P.sb([128, 128], F32, "LT"); LTb = Buf()
        CP(P, "dve", LT[:, 0:NB], b6[:, 0:NB], (b6b,), (LTb,))
        MM(P, b7[0:NB, 0:128], LT[:, 0:NB], triU[:], True, False, (LTb, triUb), (b7b,))
        MM(P, b7[0:NB, 0:128], SU[0:NB, 0:NB], totbc[0:NB, :], False, True, (SUb, totbcb), (b7b,))
        cpos = P.sb([128, 128], F32, "cpos"); cposb = Buf()
        CP(P, "dve", cpos[0:NB, :], b7[0:NB, 0:128], (b7b,), (cposb,))
        parts = P.sb([128, 6, 128], BF16, "parts"); partsb = Buf()
        r1 = P.sb([128, 128], F32, "r1"); r1b = Buf()
        CP(P, "dve", parts[0:NB, 3, :], cpos[0:NB, :], (cposb,), (partsb,))
        TT(P, "dve", r1[0:NB, :], cpos[0:NB, :], parts[0:NB, 3, :], ALU.subtract, (cposb, partsb), (r1b,))
        CP(P, "dve", parts[0:NB, 4, :], r1[0:NB, :], (r1b,), (partsb,))
        TT(P, "dve", r1[0:NB, :], r1[0:NB, :], parts[0:NB, 4, :], ALU.subtract, (r1b, partsb), (r1b,))
        CP(P, "dve", parts[0:NB, 5, :], r1[0:NB, :], (r1b,), (partsb,))
        for r in range(3):
            TS(P, "dve", parts[0:NB, r, :], parts[0:NB, 3 + r, :], -1.0, None, ALU.mult, None, (partsb,), (partsb,))
        csb = Buf()
        P.dma("sp", P.chan(), io["cs"].rearrange("r (i p) -> i r p", p=128), parts[0:NB, :, :], (partsb,), (csb,))
        MEMSET(P, "dve", QT[64:70, :], 1.0, (QTb,))
        MEMSET(P, "dve", KT[64:70, :], 1.0, (KTb,))
        P.dma("sp", qch, QT[64:67, :], io["cs"][0:3, :], (csb,), (QTb,))
        P.dma("sp", kch, KT[67:70, :], io["cs"][3:6, :], (csb,), (KTb,))
        nmC, nmCb = masks["negmaskC"]
        Sr = Ring(banks[0:4]); Or = Ring(banks[4:6])
        tiles = []
        for qt in range(NQT):
            q0 = qt * 512
            qs = slice(q0, q0 + 512)
            nkb = 4 * qt + 4
            Obk, Ob = Or.next()
            O3 = Obk[:, 0:260].rearrange("p (s d) -> p s d", d=65)
            for kb in range(nkb):
                ks = slice(kb * 128, (kb + 1) * 128)
                j = kb - 4 * qt
                Sbk, Sb = Sr.next()
                Pt, Ptb = Pts.next()

                def st0(Sbk=Sbk, Sb=Sb, ks=ks, qs=qs, j=j):
                    MM(P, Sbk[:], KT[0:70, ks], QT[0:70, qs], True, j < 0, (KTb, QTb), (Sb,))
                    if j >= 0:
                        MM(P, Sbk[:], ident_b[:], nmC[:, j, :], False, True, (identb, nmCb), (Sb,))

                def st1(Sbk=Sbk, Sb=Sb, Pt=Pt, Ptb=Ptb):
                    ACT(P, Pt[:], Sbk[:], AF.Exp, (Sb,), (Ptb,))

                def st2(Pt=Pt, Ptb=Ptb, O3=O3, Ob=Ob, kb=kb, j=j, qt=qt, q0=q0, nkb=nkb):
                    for sub in range(4):
                        if j > sub:
                            continue
                        MM(P, O3[:, sub, :], Pt[:, sub * 128:(sub + 1) * 128], V[:, kb, :], kb == 0 and sub == 0,
                           kb == 4 * qt + sub, (Ptb, Vb), (Ob,), skip=True)
                    if kb == nkb - 1:
                        yst, ystb, ych = ysts.next()
                        P.op("dve", lambda e, o=rc[:, 0:4], i=O3[:, :, 64]: e.reciprocal(o, i), (Ob,), (rcb,))
                        TT(P, "dve", yst[:], O3[:, :, 0:64], bc_last(rc[:, 0:4], 64), ALU.mult, (Ob, rcb), (ystb,))
                        y_out(yst, ystb, ych, 1, q0, 4)

                tiles.append((st0, st1, st2))
        run_pipeline(tiles, SKEW_AC)
        after_mixer(1, ybufs)

    if "D" in mixers:
        load_qkv(2)
        nmD, nmDb = masks["negmaskD"]
        m01, m01b = masks["mask01D"]
        zp = psum_all[:, 0:1024]; zpb = banks[0][1]
        ap_ = [(psum_all[:, 1024:2048], banks[2][1]), (psum_all[:, 2048:3072], banks[4][1])]
        Ar = Ring(ap_)
        OCr = Ring(banks[6:8])
        negones = P.sb([128, 128], BF16, "negones"); negonesb = Buf()
        MEMSET(P, "dve", negones[:], -1.0, (negonesb,))
        Es = Ring([(P.sb([128, 1024], F32, "E"), Buf()) for _ in range(3)])
        SPs = Ring([(P.sb([128, 1024], BF16, "SP"), Buf()) for _ in range(7)])
        PtD = Ring([(P.sb([128, 1024], BF16, "PtD"), Buf()) for _ in range(3)])
        acc = P.sb([128, 4, 64], F32, "acc"); accb = Buf()
        tmp = P.sb([128, 4, 64], F32, "tmp"); tmpb = Buf()
        carry = P.sb([128, 4], F32, "carry"); carryb = Buf()
        ec = P.sb([128, 4], F32, "ec"); ecb = Buf()
        tiles = []
        for qt in range(NQT):
            q0 = qt * 512
            qs = slice(q0, q0 + 512)
            kmax = 4 * qt + 3
            for k1 in range(kmax, -1, -2):
                k2 = k1 - 1
                kk = (k1, k2)
                kss = tuple(slice(k * 128, (k + 1) * 128) for k in kk)
                js = tuple(k - 4 * qt for k in kk)
                E, Eb = Es.next()
                SP, SPb = SPs.next()
                Abk, Ab = Ar.next()
                Pt, Ptb = PtD.next()
                OCbk, OCb = OCr.next()
                OC3 = OCbk[:, 0:260].rearrange("p (s d) -> p s d", d=65)

                def s0(kss=kss, qs=qs):
                    for h in range(2):
                        MM(P, zp[:, h * 512:(h + 1) * 512], KT[0:64, kss[h]], QT[0:64, qs], True, True, (KTb, QTb), (zpb,))

                def s1(E=E, Eb=Eb):
                    ACT(P, E[:], zp, AF.Exp, (zpb,), (Eb,))

                def s2(E=E, Eb=Eb, SP=SP, SPb=SPb, js=js):
                    ACT(P, SP[:], E[:], AF.Ln, (Eb,), (SPb,), bias=1.0)
                    for h in range(2):
                        if js[h] >= 0:
                            hs_ = slice(h * 512, (h + 1) * 512)
                            TT(P, "dve", SP[:, hs_], SP[:, hs_], m01[:, js[h], :], ALU.mult, (SPb, m01b), (SPb,))

                def s3(Abk=Abk, Ab=Ab, SP=SP, SPb=SPb, kss=kss, qs=qs, js=js):
                    for h in range(2):
                        o = Abk[:, h * 512:(h + 1) * 512]
                        hs_ = slice(h * 512, (h + 1) * 512)
                        last_is_tri = (h == 0) and js[h] < 0
                        MM(P, o, KT[0:64, kss[h]], QT[0:64, qs], True, False, (KTb, QTb), (Ab,))
                        MM(P, o, negtri[:], SP[:, hs_], False, last_is_tri, (negtrib, SPb), (Ab,))
                        if h == 1:
                            MM(P, o, negones[:], SP[:, 0:512], False, js[h] < 0, (negonesb, SPb), (Ab,))
                        if js[h] >= 0:
                            MM(P, o, ident_b[:], nmD[:, js[h], :], False, True, (identb, nmDb), (Ab,))

                def s4(Abk=Abk, Ab=Ab, Pt=Pt, Ptb=Ptb):
                    ACT(P, Pt[:], Abk, AF.Exp, (Ab,), (Ptb,))

                def s5(Pt=Pt, Ptb=Ptb, SP=SP, SPb=SPb, OC3=OC3, OCb=OCb, kk=kk):
                    for sub in range(4):
                        for h in range(2):
                            ss = slice(h * 512 + sub * 128, h * 512 + (sub + 1) * 128)
                            MM(P, OC3[:, sub, 0:64], Pt[:, ss], V[:, kk[h], 0:64], h == 0, h == 1, (Ptb, Vb), (OCb,))
                        for h in range(2):
                            ss = slice(h * 512 + sub * 128, h * 512 + (sub + 1) * 128)
                            MM(P, OC3[:, sub, 64:65], SP[:, ss], ones_col[:], h == 0, h == 1, (SPb, onecb), (OCb,))

                def s6(OC3=OC3, OCb=OCb, k1=k1, k2=k2, kmax=kmax, q0=q0):
                    if k1 == kmax:
                        CP(P, "dve", acc[:], OC3[:, :, 0:64], (OCb,), (accb,))
                        TS(P, "dve", carry[:], OC3[:, :, 64], -1.0, None, ALU.mult, None, (OCb,), (carryb,))
                    else:
                        ACT(P, ec[:], carry[:], AF.Exp, (carryb,), (ecb,))
                        TT(P, "dve", tmp[:], OC3[:, :, 0:64], bc_last(ec[:, 0:4], 64), ALU.mult, (OCb, ecb), (tmpb,))
                        TT(P, "pool", acc[:], acc[:], tmp[:], ALU.add, (accb, tmpb), (accb,))
                        TT(P, "dve", carry[:], carry[:], OC3[:, :, 64], ALU.subtract, (carryb, OCb), (carryb,))
                    if k2 == 0:
                        yst, ystb, ych = ysts.next()
                        CP(P, "pool", yst[:], acc[:], (accb,), (ystb,))
                        y_out(yst, ystb, ych, 2, q0, 4)

                tiles.append((s0, s1, s2, s3, s4, s5, s6))
        run_pipeline(tiles, SKEW_D)
        after_mixer(2, ybufs)
    return [r[1] for r in ysts.items]


def emit_p3(P, io, T, cst, final):
    NT = T // 512
    ident_f = P.sb([128, 128], F32, "identf"); identfb = Buf()
    P.dma("sp", P.chan(), ident_f[:], cst["ident"], (), (identfb,))
    ident_b = P.sb([128, 128], BF16, "identb"); identb = Buf()
    CP(P, "dve", ident_b[:], ident_f[:], (identfb,), (identb,))
    Wo = P.sb([128, 8, 1024], BF16, "Wo"); Wobs = [Buf() for _ in range(8)]
    wst = [(P.sb([128, 1024], F32, "wst"), Buf(), P.chan()) for _ in range(2)]
    w_v = io["w_out"].rearrange("(c p) n -> p c n", p=128)
    for c in range(8):
        st, sb_, ch = wst[c % 2]
        P.dma("sp", ch, st[:], w_v[:, c, :], (), (sb_,))
        CP(P, "dve" if c % 2 == 0 else "pool", Wo[:, c, :], st[:], (sb_,), (Wobs[c],))
    bg = P.sb([128, 1024], F32, "bg"); bgb = Buf()
    P.dma("sp", P.chan(), bg[:], io["bgain_bc"], (), (bgb,))
    TS(P, "dve", bg[:], bg[:], 16.0, None, ALU.mult, None, (bgb,), (bgb,))
    if final:
        fg = P.sb([128, 8], F32, "fg"); fgb = Buf()
        P.dma("sp", P.chan(), fg[:], io["fg8"], (), (fgb,))
        TS(P, "dve", fg[:], fg[:], 32.0, None, ALU.mult, None, (fgb,), (fgb,))
        ones = P.sb([128, 128], BF16, "ones"); onesb = Buf()
        MEMSET(P, "dve", ones[:], 1.0, (onesb,))
        sq = P.sb([128, 8, 512], BF16, "sq"); sqb = Buf()
        rbc = P.sb([128, 512], F32, "rbc"); rbcb = Buf()

    y_dt = io.get("y_dt", F32)
    yts = Ring([(P.sb([128, 3, 256], y_dt, "yt"), P.sb([128, 256], F32, "ytB"), Buf(), P.chan()) for _ in range(2)])
    gts = Ring([(P.sb([128, 1024], F32, "gt"), Buf(), P.chan()) for _ in range(2)])
    xts = Ring([(P.sb([128, 8, 512], F32, "xt"), Buf(), P.chan()) for _ in range(2)])
    xns = Ring([(P.sb([128, 8, 512], F32, "xn"), [Buf() for _ in range(8)], P.chan("pool")) for _ in range(2)])
    mTs = Ring([(P.sb([128, 8, 512], BF16, "mT"), Buf()) for _ in range(2)])
    t1 = P.sb([128, 1024], F32, "t1"); t1b = Buf()
    m = P.sb([128, 1024], BF16, "m"); mb = Buf()
    junk = P.sb([128, 256], F32, "junk"); jb = Buf()
    ssq = P.sb([128, 4], F32, "ssq"); ssqb = Buf()
    tps = Ring([(P.ps([128, 1024], BF16, "tp"), Buf(True)) for _ in range(2)])
    obanks = Ring([(P.ps([128, 512], F32, "ob"), Buf(True)) for _ in range(4)])
    if final:
        sbank = (P.ps([128, 512], F32, "sbk"), Buf(True))
    x_v = io["xT"].rearrange("(c p) t -> p c t", p=128)
    o_v = io["xT_out"].rearrange("(c p) t -> p c t", p=128)
    for t in range(NT):
        ts = slice(t * 512, (t + 1) * 512)
        xt, xb, xch = xts.next()
        P.dma("sp", xch, xt[:], x_v[:, :, ts], (), (xb,))
        mT, mTb = mTs.next()
        for sub in range(4):
            tok0 = t * 512 + sub * 128
            yt3, ytB, ytb, ych = yts.next()
            for mi in range(3):
                P.dma("sp", ych, yt3[:, mi, :].rearrange("p (h d) -> p h d", h=4),
                      io["y"](mi, tok0), (), (ytb,))
            P.dma("sp", ych, ytB[:], io["yb"][tok0:tok0 + 128, :], (), (ytb,))
            ysrc = (yt3[:, 0, :], ytB[:], yt3[:, 1, :], yt3[:, 2, :])
            gt, gtb, gch = gts.next()
            P.dma("sp", gch, gt[:], io["gs"][tok0:tok0 + 128, :], (), (gtb,))
            for br in range(4):
                ACT(P, junk[:], ysrc[br], AF.Square, (ytb,), (jb, ssqb), accum=ssq[:, br:br + 1])
            RSQRT(P, ssq[:], ssq[:], 256.0 * EPS, (ssqb,), (ssqb,))
            TT(P, "pool", t1[:], gt[:], bg[:], ALU.mult, (gtb, bgb), (t1b,))
            for br in range(4):
                cs_ = slice(br * 256, (br + 1) * 256)
                STT(P, "dve", m[:, cs_], ysrc[br], ssq[:, br:br + 1], t1[:, cs_], ALU.mult, ALU.mult,
                    (ytb, ssqb, t1b), (mb,))
            tp, tpb = tps.next()
            for c in range(8):
                P.op("pe", lambda e, o=tp[:, c * 128:(c + 1) * 128], i=m[:, c * 128:(c + 1) * 128], idn=ident_b[:]:
                     e.transpose(o, i, idn), (mb, identb), (tpb,))
            CP(P, "act", mT[:, :, sub * 128:(sub + 1) * 128], tp[:].rearrange("p (c t) -> p c t", c=8), (tpb,), (mTb,))
        xn, xnbs, xnch = xns.next()
        for cb in range(8):
            ob, obb = obanks.next()
            for c in range(8):
                MM(P, ob[:], Wo[:, c, cb * 128:(cb + 1) * 128], mT[:, c, :], c == 0, c == 7, (Wobs[c], mTb), (obb,))
            TT(P, "dve", xn[:, cb, :], xt[:, cb, :], ob[:], ALU.add, (xb, obb), (xnbs[cb],))
        if final:
            ACT(P, sq[:], xn[:], AF.Square, tuple(xnbs), (sqb,))
            sbk, sbb = sbank
            for c in range(8):
                MM(P, sbk[:], ones[:], sq[:, c, :], c == 0, c == 7, (onesb, sqb), (sbb,))
            RSQRT(P, rbc[:], sbk[:], 1024.0 * EPS, (sbb,), (rbcb,))
            for c in range(8):
                STT(P, "dve", xn[:, c, :], xn[:, c, :], fg[:, c:c + 1], rbc[:], ALU.mult, ALU.mult,
                    (xnbs[c], fgb, rbcb), (xnbs[c],))
        P.dma("pool", xnch, o_v[:, :, ts], xn[:], tuple(xnbs), ())
    return [b for r in xns.items for b in r[1]]


T_CORE = 4096
SEQ = 16384
_CACHE = {}


def _ext(P, name, shape, dt, kind):
    return P.dram(name, shape, dt, kind)


def static_p1_io(io):
    io["qk_w"] = lambda h, ti, t0: io["qk_send"][h, ti, :, t0:t0 + 512]
    io["v_w"] = lambda vi, tok0: io["v_send"][:, vi, tok0:tok0 + 128, :].rearrange("h t d -> t h d")


def static_p2_io(io):
    io["ntq"] = 1
    io["qk"] = lambda i, idx, tq: io["qk_recv"][i, idx]
    io["v"] = lambda i, m, tq: io["v_recv"][i, m].rearrange("(n p) d -> p n d", p=128)
    io["f"] = lambda i: io["f_recv"][i].rearrange("(n p) -> n p", p=128)
    io["y_w"] = lambda i, mi, off, nsub: io["y_send"][i, mi, off:off + nsub * 128, :].rearrange("(s p) d -> p s d", p=128)


def static_p3_io(io):
    io["y"] = lambda mi, tok0: io["y_recv"][:, mi, tok0:tok0 + 128, :].rearrange("h t d -> t h d")


def build_prog1():
    P = Prog(); T = T_CORE; io = {}
    io["xT"] = _ext(P, "xT", [1024, T], F32, "ExternalInput")
    io["w_in"] = _ext(P, "w_in", [1024, N_IN], F32, "ExternalInput")
    io["g8"] = _ext(P, "g8", [128, 8], F32, "ExternalInput")
    io["vgain_bc"] = _ext(P, "vgain_bc", [128, 256], F32, "ExternalInput")
    io["wsT"] = _ext(P, "wsT", [128, 4, 128], F32, "ExternalInput")
    io["bsT"] = _ext(P, "bsT", [128, 4], F32, "ExternalInput")
    cst = {"tril01T": _ext(P, "tril01T", [128, 128], F32, "ExternalInput")}
    io["qk_send"] = _ext(P, "qk_send", [4, 6, 64, T], BF16, "ExternalOutput")
    io["v_send"] = _ext(P, "v_send", [4, 3, T, 64], BF16, "ExternalOutput")
    io["f_send"] = _ext(P, "f_send", [4, T], F32, "ExternalOutput")
    io["gs"] = _ext(P, "gs", [T, 1024], F32, "ExternalOutput")
    io["yb"] = _ext(P, "yb", [T, 256], F32, "ExternalOutput")
    static_p1_io(io)
    outs = emit_p1(P, io, T, cst)
    P.wait_all("sp", outs)
    return P.emit()


P2_CONSTS = ("ident", "negtri", "triU", "SU", "negmaskC", "negmaskD", "mask01D", "maskA")


def build_prog2():
    P = Prog(); S = SEQ; Tc = T_CORE; io = {}
    io["qk_recv"] = _ext(P, "qk_recv", [4, 6, 64, Tc], BF16, "ExternalInput")
    io["v_recv"] = _ext(P, "v_recv", [4, 3, Tc, 64], BF16, "ExternalInput")
    io["f_recv"] = _ext(P, "f_recv", [4, Tc], F32, "ExternalInput")
    io["biasA"] = _ext(P, "biasA", [128, 5, 128], F32, "ExternalInput")
    io["bf_col"] = _ext(P, "bf_col", [128, 1], F32, "ExternalInput")
    io["y_send"] = _ext(P, "y_send", [4, 3, Tc, 64], F32, "ExternalOutput")
    io["cs"] = P.dram("cs", [6, S], BF16, "Internal")
    cn = host_consts()
    cst = {k: _ext(P, k, list(cn[k].shape), F32, "ExternalInput") for k in P2_CONSTS}
    static_p2_io(io)
    outs = emit_p2(P, io, S, Tc, cst)
    P.wait_all("sp", outs)
    return P.emit()


def build_prog3(final):
    P = Prog(); T = T_CORE; io = {}
    io["y_recv"] = _ext(P, "y_recv", [4, 3, T, 64], F32, "ExternalInput")
    io["yb"] = _ext(P, "yb", [T, 256], F32, "ExternalInput")
    io["gs"] = _ext(P, "gs", [T, 1024], F32, "ExternalInput")
    io["xT"] = _ext(P, "xT", [1024, T], F32, "ExternalInput")
    io["w_out"] = _ext(P, "w_out", [1024, 1024], F32, "ExternalInput")
    io["bgain_bc"] = _ext(P, "bgain_bc", [128, 1024], F32, "ExternalInput")
    if final:
        io["fg8"] = _ext(P, "fg8", [128, 8], F32, "ExternalInput")
    io["xT_out"] = _ext(P, "xT_out", [1024, T], F32, "ExternalOutput")
    cst = {"ident": _ext(P, "ident", [128, 128], F32, "ExternalInput")}
    static_p3_io(io)
    outs = emit_p3(P, io, T, cst, final)
    P.wait_all("sp", outs)
    return P.emit()


def _run(nc, in_maps):
    res = run_bass_kernel_spmd(nc, in_maps, core_ids=list(range(8)))
    return res.results


def kernel_unfused(x, norm_g, w_in, b_f, rel_bias, w_s, b_s, v_gain, branch_gain, w_out, final_g):
    f32 = np.float32
    x = np.asarray(x, f32)
    cn = host_consts()
    depth = 2
    c32 = lambda a: np.ascontiguousarray(np.asarray(a, f32))
    xT = [c32(x[c // 4, (c % 4) * T_CORE:(c % 4 + 1) * T_CORE, :].T) for c in range(8)]
    nc1 = build_prog1()
    nc2 = build_prog2()
    for l in range(depth):
        final = l == depth - 1
        com1 = dict(w_in=c32(w_in[l]), g8=c32(np.asarray(norm_g[l]).reshape(8, 128).T),
                    vgain_bc=c32(np.tile(np.asarray(v_gain[l])[None, :], (128, 1))),
                    wsT=c32(np.asarray(w_s[l]).transpose(2, 0, 1)), bsT=c32(np.asarray(b_s[l]).T),
                    tril01T=cn["tril01T"])
        r1 = _run(nc1, [dict(com1, xT=xT[c]) for c in range(8)])
        im2 = []
        for c in range(8):
            b, h = c // 4, c % 4
            d = dict(qk_recv=np.stack([r1[b * 4 + i]["qk_send"][h] for i in range(4)]),
                     v_recv=np.stack([r1[b * 4 + i]["v_send"][h] for i in range(4)]),
                     f_recv=np.stack([r1[b * 4 + i]["f_send"][h] for i in range(4)]),
                     biasA=host_biasA(np.asarray(rel_bias[l][h], f32)),
                     bf_col=np.full((128, 1), np.asarray(b_f, f32)[l, h], f32))
            for k in P2_CONSTS:
                d[k] = cn[k]
            im2.append(d)
        r2 = _run(nc2, im2)
        nc3 = build_prog3(final)
        com3 = dict(w_out=c32(w_out[l]), ident=cn["ident"],
                    bgain_bc=c32(np.tile(np.asarray(branch_gain[l]).reshape(1, 1024), (128, 1))))
        if final:
            com3["fg8"] = c32(np.asarray(final_g).reshape(8, 128).T)
        im3 = []
        for c in range(8):
            b, i = c // 4, c % 4
            im3.append(dict(com3, y_recv=np.stack([r2[b * 4 + h]["y_send"][i] for h in range(4)]),
                            yb=r1[c]["yb"], gs=r1[c]["gs"], xT=xT[c]))
        r3 = _run(nc3, im3)
        xT = [np.asarray(r3[c]["xT_out"]) for c in range(8)]
    out = np.empty((2, SEQ, D_MODEL), f32)
    for c in range(8):
        out[c // 4, (c % 4) * T_CORE:(c % 4 + 1) * T_CORE, :] = xT[c].T
    return out


GROUPS = [[0, 1, 2, 3], [4, 5, 6, 7]]


def build_fused(T=T_CORE, depth=2):
    S = 4 * T
    NTQ = 4 if T >= 2048 else 1
    TQ = T // NTQ
    NT8 = max(1, T // 512)
    T8 = T // NT8
    P = Prog()
    P.use_pid = True
    cn = host_consts()
    cst = {k: P.dram(k, list(v.shape), F32, "ExternalInput") for k, v in cn.items()}
    xT_in = P.dram("xT", [1024, T], F32, "ExternalInput")
    out_ext = P.dram("outT", [1024, T], F32, "ExternalOutput")
    xT_mid = P.dram("xT_mid", [1024, T], F32, "Internal")
    gs = P.dram("gs_i", [T, 1024], F32, "Internal")
    yb = P.dram("yb_i", [T, 256], F32, "Internal")
    cs = P.dram("cs_i", [6, S], BF16, "Internal")
    fg8 = P.dram("fg8", [128, 8], F32, "ExternalInput")
    cch = P.chan("coll")
    FSTOP = int(os.environ.get("FSTOP", "99"))

    def hsel(e):
        return bass.ds(P.pid_cache[id(e)], 1)

    def exchange(name, nch, ch, dt):
        send = P.dram(f"{name}_s", [nch, 4, ch], dt, "Internal")
        gath = P.dram(f"{name}_g", [nch, 16, ch], dt, "Internal")
        recv = P.dram(f"{name}_r", [nch, 4, ch], dt, "Internal")

        def gather(c0=0, c1=nch, wait_bufs=()):
            for c in range(c0, c1):
                P.coll(cch, "AllGather", GROUPS, send[c], gath[c], (), (), after=tuple(wait_bufs))

        def finish():
            P.barrier()
            d = Buf()
            g4 = gath.rearrange("c (a h) x -> c a h x", h=4)
            P.dma("sp", P.chan(), recv.rearrange("c a (o x) -> c a o x", o=1),
                  lambda e: g4[:, :, hsel(e), :], (), (d,))
        return send, recv, gather, finish

    for l in range(depth):
        final = l == depth - 1
        ext = lambda nm, shp: P.dram(f"{nm}{l}", shp, F32, "ExternalInput")
        qk_s, qk_r, qk_g, qk_f = exchange(f"qk{l}", 6 * NTQ, 64 * TQ, BF16)
        v_s, v_r, v_g, v_f = exchange(f"v{l}", 3 * NTQ, TQ * 64, BF16)
        f_s, f_r, f_g, f_f = exchange(f"f{l}", 1, T, F32)
        y_s, y_r, y_g, y_f = exchange(f"y{l}", 3 * NT8, T8 * 64, BF16)
        io1 = dict(xT=xT_in if l == 0 else xT_mid, w_in=ext("w_in", [1024, N_IN]), g8=ext("g8_", [128, 8]),
                   vgain_bc=ext("vgain_bc", [128, 256]), wsT=ext("wsT", [128, 4, 128]), bsT=ext("bsT", [128, 4]),
                   f_send=f_s[0], gs=gs, yb=yb)
        io1["qk_w"] = lambda h, ti, t0, qk_s=qk_s: qk_s[ti * NTQ + t0 // TQ, h].rearrange(
            "(d t) -> d t", d=64)[:, t0 % TQ:t0 % TQ + 512]
        io1["v_w"] = lambda vi, tok0, v_s=v_s: v_s[vi * NTQ + tok0 // TQ].rearrange(
            "h (t d) -> t h d", d=64)[tok0 % TQ:tok0 % TQ + 128]
        tpq = TQ // 512

        def after_tile(t, bufs, qk_g=qk_g, v_g=v_g):
            if (t + 1) % tpq == 0:
                tq = t // tpq
                for ti in range(6):
                    qk_g(ti * NTQ + tq, ti * NTQ + tq + 1, bufs)
                for vi in range(3):
                    v_g(vi * NTQ + tq, vi * NTQ + tq + 1, bufs)

        io1["after_tile"] = after_tile
        P.phase_begin()
        if not os.environ.get("SKIP1"):
            emit_p1(P, io1, T, cst)
        P.phase_end()
        if FSTOP <= 1:
            break
        f_g()
        qk_f(); v_f(); f_f()
        P.barrier()
        if FSTOP <= 3:
            break
        io2 = dict(biasA=ext("biasA", [128, 5, 128]), bf_col=ext("bf_col", [128, 1]), cs=cs, ntq=NTQ, y_dt=BF16)
        io2["after_mixer"] = lambda mi, bufs, y_g=y_g: y_g(mi * NT8, (mi + 1) * NT8, bufs)
        io2["qk"] = lambda i, idx, tq, qk_r=qk_r: qk_r[idx * NTQ + tq, i].rearrange("(d t) -> d t", d=64)
        io2["v"] = lambda i, m, tq, v_r=v_r: v_r[m * NTQ + tq, i].rearrange("(n p d) -> p n d", p=128, d=64)
        io2["f"] = lambda i, f_r=f_r: f_r[0, i].rearrange("(n p) -> n p", p=128)
        io2["y_w"] = lambda i, mi, off, nsub, y_s=y_s: y_s[mi * NT8 + off // T8, i].rearrange(
            "(t d) -> t d", d=64)[off % T8:off % T8 + nsub * 128].rearrange("(s p) d -> p s d", p=128)
        P.phase_begin()
        emit_p2(P, io2, S, T, cst)
        P.phase_end()
        if FSTOP <= 4:
            break
        y_f()
        P.barrier()
        io3 = dict(yb=yb, gs=gs, xT=io1["xT"], w_out=ext("w_out", [1024, 1024]), bgain_bc=ext("bgain_bc", [128, 1024]),
                   fg8=fg8, xT_out=out_ext if final else xT_mid, y_dt=BF16)
        io3["y"] = lambda mi, tok0, y_r=y_r: y_r[mi * NT8 + tok0 // T8].rearrange(
            "h (t d) -> t h d", d=64)[tok0 % T8:tok0 % T8 + 128]
        P.phase_begin()
        final_bufs = emit_p3(P, io3, T, cst, final)
        if final:
            P.wait_all("sp", final_bufs)
        P.phase_end()
    print("fused ops:", {k: len(v) for k, v in P.ops.items()}, "sems", P.nsem, flush=True)
    return P.emit()


def fused_inputs(x, norm_g, w_in, b_f, rel_bias, w_s, b_s, v_gain, branch_gain, w_out, final_g, T=T_CORE, depth=2):
    f32 = np.float32
    c32 = lambda a: np.ascontiguousarray(np.asarray(a, f32))
    cn = host_consts()
    com = dict(cn)
    com["fg8"] = c32(np.asarray(final_g).reshape(8, 128).T)
    for l in range(depth):
        com[f"w_in{l}"] = c32(w_in[l])
        com[f"g8_{l}"] = c32(np.asarray(norm_g[l]).reshape(8, 128).T)
        com[f"vgain_bc{l}"] = c32(np.tile(np.asarray(v_gain[l])[None, :], (128, 1)))
        com[f"wsT{l}"] = c32(np.asarray(w_s[l]).transpose(2, 0, 1))
        com[f"bsT{l}"] = c32(np.asarray(b_s[l]).T)
        com[f"w_out{l}"] = c32(w_out[l])
        com[f"bgain_bc{l}"] = c32(np.tile(np.asarray(branch_gain[l]).reshape(1, 1024), (128, 1)))
    ims = []
    for c in range(8):
        b, h = c // 4, c % 4
        d = dict(com)
        d["xT"] = c32(np.asarray(x)[b, h * T:(h + 1) * T, :].T)
        for l in range(depth):
            d[f"biasA{l}"] = host_biasA(np.asarray(rel_bias[l][h], f32))
            d[f"bf_col{l}"] = np.full((128, 1), np.asarray(b_f, f32)[l, h], f32)
        ims.append(d)
    return ims


def kernel(x, norm_g, w_in, b_f, rel_bias, w_s, b_s, v_gain, branch_gain, w_out, final_g):
    nc = build_fused()
    ims = fused_inputs(x, norm_g, w_in, b_f, rel_bias, w_s, b_s, v_gain, branch_gain, w_out, final_g)
    res = run_bass_kernel_spmd(nc, ims, core_ids=list(range(8))).results
    out = np.empty((2, SEQ, D_MODEL), np.float32)
    for c in range(8):
        out[c // 4, (c % 4) * T_CORE:(c % 4 + 1) * T_CORE, :] = np.asarray(res[c]["outT"]).T
    return out
```

```python
import os
import numpy as np
import ml_dtypes
from contextlib import ExitStack
import concourse.bass as bass
import concourse.mybir as mybir
from concourse.bass_utils import run_bass_kernel_spmd

F32 = mybir.dt.float32
BF16 = mybir.dt.bfloat16
AF = mybir.ActivationFunctionType
ALU = mybir.AluOpType
AX = mybir.AxisListType
NPBF = ml_dtypes.bfloat16

EPS = 1e-6
NEG = -30000.0
D_MODEL = 1024
N_IN = 3844
COL = dict(qA=0, kA=256, vA=512, gA=768, uB=1024, vB=1280, gB=1536, qC=1792, kC=2048,
           vC=2304, gC=2560, fC=2816, qD=2820, kD=3076, vD=3332, gD=3588)


class Buf:
    __slots__ = ("w", "r", "excl")

    def __init__(self, excl=False):
        self.w = None
        self.r = {}
        self.excl = excl


class Chan:
    _n = 0

    def __init__(self, sem):
        self.sem = sem
        self.val = 0
        Chan._n += 1
        self.uid = Chan._n


class Prog:
    ENGS = ("pe", "act", "dve", "pool", "sp")
    EPOCH = 8000

    def __init__(self):
        self.nc = bass.Bass("TRN2", target_bir_lowering=False)
        self.es = ExitStack()
        self.ops = {e: [] for e in self.ENGS}
        self.cnt = {e: 0 for e in self.ENGS}
        self.esems = {e: [] for e in self.ENGS}
        self.waited = {e: {} for e in self.ENGS}
        self.nsem = 0
        self.ntens = 0
        self.live_dma = {}
        self.free_chans = {}
        self.phase_chans = []
        self.pid_cache = {}
        self.use_pid = False
        self.arena = None
        self.arena_ptr = 0
        self.bank_ptr = 0
        self.es0 = ExitStack()

    def phase_begin(self):
        self.phase_chans = []
        self.arena_mark = (self.arena_ptr, self.bank_ptr)

    def phase_end(self):
        self.barrier()
        self.arena_ptr, self.bank_ptr = self.arena_mark
        for ch in self.phase_chans:
            self.free_chans.setdefault(ch.kind, []).append(ch)
        self.phase_chans = []

    def new_sem(self, name):
        self.nsem += 1
        return self.nc.alloc_semaphore(name=f"{name}{self.nsem}")

    def chan(self, kind="sp"):
        pool = self.free_chans.setdefault(kind, [])
        if pool:
            ch = pool.pop()
        else:
            ch = Chan(self.new_sem("ch"))
            ch.kind = kind
        self.phase_chans.append(ch)
        return ch

    ARENA_F32 = 45056
    _LET = "abcdefg"

    def _arena_init(self):
        if self.arena is None:
            self.arena = self.es0.enter_context(self.nc.sbuf_tensor("arena", [128, self.ARENA_F32], F32))
            self.psum_all = self.es0.enter_context(self.nc.psum_tensor("psum_all", [128, 4096], F32))
            self.banks = [self.psum_all[:, i * 512:(i + 1) * 512] for i in range(8)]

    def sb(self, shape, dt, name=None):
        self._arena_init()
        esz = 2 if dt == BF16 else 4
        n = int(np.prod(shape[1:]))
        n4 = (n * esz + 31) // 32 * 8
        off = self.arena_ptr
        assert off + n4 <= self.ARENA_F32, f"SBUF arena overflow ({name})"
        self.arena_ptr += n4
        ap = self.arena[0:shape[0], off:off + n4]
        if dt != F32:
            ap = ap.bitcast(dt)
        ap = ap[:, 0:n]
        if len(shape) > 2:
            names = " ".join(self._LET[:len(shape) - 1])
            kw = {self._LET[i]: shape[1 + i] for i in range(len(shape) - 1)}
            ap = ap.rearrange(f"p ({names}) -> p {names}", **kw)
        return ap

    def ps(self, shape, dt, name=None):
        self._arena_init()
        assert self.bank_ptr < 8, "out of PSUM banks"
        nb = 2 if (int(np.prod(shape[1:])) * (2 if dt == BF16 else 4)) > 2048 else 1
        if nb == 2:
            self.bank_ptr += self.bank_ptr % 2
            bk = self.psum_all[:, self.bank_ptr * 512:(self.bank_ptr + 2) * 512]
        else:
            bk = self.banks[self.bank_ptr][:]
        self.bank_ptr += nb
        assert self.bank_ptr <= 8, "out of PSUM banks"
        if dt != F32:
            bk = bk.bitcast(dt)
        return bk[0:shape[0], 0:int(np.prod(shape[1:]))]

    def dram(self, name, shape, dt, kind="Internal"):
        return self.nc.dram_tensor(name, list(shape), dt, kind=kind).ap()

    def _need(self, eng, ev):
        if ev is None:
            return None
        if ev[0] == "e":
            _, e2, idx = ev
            if e2 == eng and eng == "pe":
                return None
            key = ("e", e2)
            if self.waited[eng].get(key, 0) >= idx:
                return None
            self.waited[eng][key] = idx
            ep = (idx - 1) // self.EPOCH
            return (self.esems[e2][ep], (idx - 1) % self.EPOCH + 1)
        _, ch, val = ev
        key = ("d", ch.uid)
        if self.waited[eng].get(key, 0) >= val:
            return None
        self.waited[eng][key] = val
        return (ch.sem, val)

    def _deps(self, eng, reads, writes):
        evs = []
        for b in reads:
            evs.append(b.w)
            if b.excl:
                evs.extend(ev for ev in b.r.values() if not (ev[0] == "e" and ev[1] == eng))
        for b in writes:
            evs.append(b.w)
            evs.extend(b.r.values())
        return [w for w in (self._need(eng, ev) for ev in evs) if w]

    @staticmethod
    def _mark(ev, key, reads, writes):
        for b in reads:
            b.r[key] = ev
        for b in writes:
            b.w = ev
            b.r = {}

    def op(self, eng, fn, reads=(), writes=()):
        waits = self._deps(eng, reads, writes)
        self.cnt[eng] += 1
        idx = self.cnt[eng]
        ep = (idx - 1) // self.EPOCH
        while len(self.esems[eng]) <= ep:
            self.esems[eng].append(self.new_sem(eng))
        self.ops[eng].append((waits, fn, self.esems[eng][ep], 1))
        ev = ("e", eng, idx)
        self._mark(ev, ("e", eng), reads, writes)
        return ev

    def dma(self, eng, ch, out, in_, reads=(), writes=(), slow=False):
        waits = self._deps(eng, reads, writes)
        ch.val += 16
        import traceback as _tb
        site = _tb.extract_stack(limit=4)[:-1] if os.environ.get("DEBUGDMA") else None

        def fn(e, o=out, i=in_, slow=slow, site=site):
            o = o(e) if callable(o) else o
            i = i(e) if callable(i) else i
            if slow:
                return e.dma_start(out=o, in_=i, allow_slow_non_contiguous=True)
            try:
                r = e.dma_start(out=o, in_=i)
                self.ndma_ok = getattr(self, "ndma_ok", 0) + 1
                return r
            except Exception:
                print("DMA ok before:", getattr(self, "ndma_ok", 0), "site", site, flush=True)
                print("DMA FAIL out", o.shape, o.ap, "in", i.shape, i.ap, flush=True)
                raise
        self.ops[eng].append((waits, fn, ch.sem, 16))
        ev = ("d", ch, ch.val)
        self.live_dma[ch.uid] = ev
        self._mark(ev, ("d", ch.uid), reads, writes)
        return ev

    def coll(self, ch, kind, groups, in_ap, out_ap, reads=(), writes=(), inc=1, after=()):
        eng = "pool"
        waits = self._deps(eng, reads, writes) + self._deps(eng, (), after)
        ch.val += inc
        fn = lambda e, k=kind, g=groups, i=in_ap, o=out_ap: e.collective_compute(
            k, ALU.bypass, replica_groups=g, ins=[i], outs=[o])
        self.ops[eng].append((waits, fn, ch.sem, inc))
        ev = ("d", ch, ch.val)
        self.live_dma[ch.uid] = ev
        self._mark(ev, ("d", ch.uid), reads, writes)
        return ev

    def barrier(self):
        evs = [("e", e, self.cnt[e]) for e in self.ENGS if self.cnt[e] > 0]
        evs += list(self.live_dma.values())
        for eng in self.ENGS:
            waits = [w for w in (self._need(eng, ev) for ev in evs) if w]
            if waits:
                self.ops[eng].append((waits, None, None, 0))

    def wait_all(self, eng, bufs):
        waits = self._deps(eng, bufs, ())
        self.ops[eng].append((waits, None, None, 0))

    def flush(self):
        nc = self.nc
        if not any(self.ops[e] for e in self.ENGS):
            return
        with nc.Block() as block:
            table = (("pe", block.tensor), ("act", block.scalar), ("dve", block.vector),
                     ("pool", block.gpsimd), ("sp", block.sync))
            for name, deco in table:
                ops = self.ops[name]
                if not ops:
                    continue

                def body(engine, ops=ops, name=name):
                    if self.use_pid and name == "sp":
                        self.pid_cache[id(engine)] = engine.partition_id() % 4
                    for waits, fn, sem, inc in ops:
                        for (s, v) in waits:
                            engine.wait_ge(s, v)
                        if fn is not None:
                            try:
                                fn(engine).then_inc(sem, inc)
                            except Exception:
                                print("FAILED OP on", name, "waits", [(str(s), v) for s, v in waits], flush=True)
                                raise

                deco(body)
        self.ops = {e: [] for e in self.ENGS}

    def emit(self):
        self.flush()
        self.es.close()
        self.es0.close()
        return self.nc


def MM(P, out, lhsT, rhs, start, stop, reads, writes, skip=False):
    if skip:
        fn = lambda e, o=out, l=lhsT, r=rhs, a=start, b=stop: e.matmul(o, l, r, start=a, stop=b, skip_group_check=True)
    else:
        fn = lambda e, o=out, l=lhsT, r=rhs, a=start, b=stop: e.matmul(o, l, r, start=a, stop=b)
    return P.op("pe", fn, reads, writes)


def ACT(P, out, in_, func, reads, writes, bias=None, scale=None, accum=None):
    kw = {}
    if bias is not None:
        kw["bias"] = bias
    if scale is not None:
        kw["scale"] = scale
    if accum is not None:
        kw["accum_out"] = accum
    fn = lambda e, o=out, i=in_, f=func, kw=kw: e.activation(o, i, f, **kw)
    return P.op("act", fn, reads, writes)


def TS(P, eng, out, in0, s1, s2, op0, op1, reads, writes):
    if op1 is None:
        fn = lambda e, o=out, i=in0, a=s1, p0=op0: e.tensor_scalar(o, i, a, None, p0)
    else:
        fn = lambda e, o=out, i=in0, a=s1, b=s2, p0=op0, p1=op1: e.tensor_scalar(o, i, a, b, p0, p1)
    return P.op(eng, fn, reads, writes)


def TT(P, eng, out, in0, in1, op, reads, writes):
    fn = lambda e, o=out, a=in0, b=in1, p=op: e.tensor_tensor(o, a, b, p)
    return P.op(eng, fn, reads, writes)


def STT(P, eng, out, in0, scalar, in1, op0, op1, reads, writes):
    fn = lambda e, o=out, a=in0, s=scalar, b=in1, p0=op0, p1=op1: e.scalar_tensor_tensor(o, a, s, b, p0, p1)
    return P.op(eng, fn, reads, writes)


def CP(P, eng, out, in_, reads, writes):
    if eng == "act":
        fn = lambda e, o=out, i=in_: e.copy(o, i)
    else:
        fn = lambda e, o=out, i=in_: e.tensor_copy(o, i)
    return P.op(eng, fn, reads, writes)


def RSQRT(P, out, in_, eps, reads, writes):
    TS(P, "dve", out, in_, eps, None, ALU.add, None, reads, writes)
    ACT(P, out, out, AF.Sqrt, writes, writes)
    P.op("dve", lambda e, o=out: e.reciprocal(o, o), writes, writes)


def MEMSET(P, eng, ap, val, writes):
    fn = lambda e, a=ap, v=val: e.memset(a, v)
    return P.op(eng, fn, (), writes)


class Ring:
    def __init__(self, items):
        self.items = items
        self.i = 0

    def next(self):
        it = self.items[self.i % len(self.items)]
        self.i += 1
        return it


def host_consts():
    k = np.arange(128)
    c = {}
    c["ident"] = np.eye(128, dtype=np.float32)
    c["negtri"] = np.where(k[:, None] >= k[None, :], -1.0, 0.0).astype(np.float32)
    c["triU"] = (k[:, None] <= k[None, :]).astype(np.float32)
    c["SU"] = (k[:, None] < k[None, :]).astype(np.float32)
    q = np.arange(512)
    kk = (np.arange(4)[None, :, None] * 128 + k[:, None, None])
    c["negmaskC"] = np.where(kk <= q[None, None, :], 0.0, NEG).astype(np.float32)
    c["negmaskD"] = np.where(kk < q[None, None, :], 0.0, NEG).astype(np.float32)
    c["mask01D"] = (kk < q[None, None, :]).astype(np.float32)
    q1 = np.arange(128)
    kc = (2 * (np.arange(5)[None, :, None] - 4) + (k[:, None, None] // 64))
    cq = (q1[None, None, :] // 64)
    valid = (kc >= cq - 8) & (kc <= cq)
    c["maskA"] = np.where(valid, 0.0, NEG).astype(np.float32)
    c["tril01T"] = (k[:, None] <= k[None, :]).astype(np.float32)
    return c


def host_biasA(rel_bias_h):
    k = np.arange(128)
    q1 = np.arange(128)
    kpos = ((np.arange(5)[None, :, None] - 4) * 128 + k[:, None, None])
    d = np.clip(q1[None, None, :] - kpos, -128, 128) + 128
    return np.ascontiguousarray(rel_bias_h[d]).astype(np.float32)


import os
DBG = int(os.environ.get("P1DBG", "9"))
SILU = AF.Copy if os.environ.get("NOSILU") else AF.Silu


def emit_p1(P, io, T, cst):
    NT = T // 512
    W = P.sb([128, 8, N_IN], BF16, "W"); Wbs = [Buf() for _ in range(8)]
    HW_ = N_IN // 2
    wst = [(P.sb([128, HW_], F32, "wst"), Buf(), P.chan()) for _ in range(2)]
    w_v = io["w_in"].rearrange("(c p) n -> p c n", p=128)
    g32 = P.sb([128, 8], F32, "g32"); g32b = Buf()
    P.dma("sp", P.chan(), g32[:], io["g8"], (), (g32b,))
    TS(P, "dve", g32[:], g32[:], 32.0, None, ALU.mult, None, (g32b,), (g32b,))
    for c in range(8):
        for hf in range(2):
            st, sb_, ch = wst[hf]
            cs_ = slice(hf * HW_, (hf + 1) * HW_)
            P.dma("sp", ch, st[:], w_v[:, c, cs_], (), (sb_,))
            TS(P, "dve", W[:, c, cs_], st[:], g32[:, c:c + 1], None, ALU.mult, None, (sb_, g32b), (Wbs[c],))
    vg = P.sb([128, 256], F32, "vg"); vgb = Buf()
    P.dma("sp", P.chan(), vg[:], io["vgain_bc"], (), (vgb,))
    bsT = P.sb([128, 4], F32, "bsT"); bsb = Buf()
    P.dma("sp", P.chan(), bsT[:], io["bsT"], (), (bsb,))
    wsf = P.sb([128, 4, 128], F32, "wsf"); wsfb = Buf()
    P.dma("sp", P.chan(), wsf[:], io["wsT"], (), (wsfb,))
    trl = P.sb([128, 128], F32, "trl"); trlb = Buf()
    P.dma("sp", P.chan(), trl[:], cst["tril01T"], (), (trlb,))
    Wtr = P.sb([128, 4, 128], BF16, "Wtr"); Wtrb = Buf()
    for g in range(4):
        TT(P, "dve", Wtr[:, g, :], wsf[:, g, :], trl[:], ALU.mult, (wsfb, trlb), (Wtrb,))
    ones = P.sb([128, 128], BF16, "ones"); onesb = Buf()
    MEMSET(P, "dve", ones[:], 1.0, (onesb,))

    xts = Ring([(P.sb([128, 8, 512], F32, "xt"), Buf(), P.chan()) for _ in range(2)])
    sq = P.sb([128, 8, 512], BF16, "sq"); sqb = Buf()
    rbc = P.sb([128, 512], F32, "rbc"); rbcb = Buf()
    hTs = Ring([(P.sb([128, 8, 512], BF16, "hT"), [Buf() for _ in range(8)]) for _ in range(2)])
    banks = [(P.ps([128, 512], F32, "bk"), Buf(True)) for _ in range(8)]
    ssq_bank = banks[0]
    fm_banks = Ring(banks[1:3])
    tm_banks = Ring(banks[3:7])
    mix_bank = banks[7]
    fm_st = Ring([(P.sb([128, 512], BF16, "fmst"), Buf(), P.chan()) for _ in range(3)])
    f_st = Ring([(P.sb([4, 512], F32, "fst"), Buf(), P.chan()) for _ in range(2)])
    v_st = Ring([(P.sb([128, 3, 256], BF16, "vst"), Buf(), P.chan()) for _ in range(2)])
    g_st = Ring([(P.sb([128, 1024], F32, "gst"), Buf(), P.chan()) for _ in range(2)])
    yb_st = Ring([(P.sb([128, 256], F32, "ybst"), Buf(), P.chan()) for _ in range(2)])
    u_sb = P.sb([128, 256], F32, "u"); ub = Buf()
    vtmp = P.sb([128, 256], F32, "vtmp"); vtb = Buf()
    vn = P.sb([128, 256], BF16, "vn"); vnb = Buf()
    junk = P.sb([128, 256], F32, "junk"); jb = Buf()
    st4 = P.sb([128, 8], F32, "st4"); st4b = Buf()

    x_v = io["xT"].rearrange("(c p) t -> p c t", p=128)
    FM = [("qA", 0, 0.125), ("kA", 1, 1.0), ("qC", 2, 0.125), ("kC", 3, 1.0), ("qD", 4, 0.125), ("kD", 5, 1.0)]
    evac_i = 0
    xq = []

    def xload(tt):
        xt_, xb_, xch_ = xts.next()
        P.dma("sp", xch_, xt_[:], x_v[:, :, tt * 512:(tt + 1) * 512], (), (xb_,))
        xq.append((xt_, xb_))

    xload(0)
    for t in range(NT):
        ts = slice(t * 512, (t + 1) * 512)
        if t + 1 < NT:
            xload(t + 1)
        xt, xb = xq[t]
        ACT(P, sq[:], xt[:], AF.Square, (xb,), (sqb,))
        sbk, sbb = ssq_bank
        for c in range(8):
            MM(P, sbk[:], ones[:], sq[:, c, :], c == 0, c == 7, (onesb, sqb), (sbb,))
        RSQRT(P, rbc[:], sbk[:], 1024.0 * EPS, (sbb,), (rbcb,))
        hT, hb = hTs.next()
        for c in range(8):
            TT(P, "dve", hT[:, c, :], xt[:, c, :], rbc[:], ALU.mult,
               (xb, rbcb), (hb[c],))
        for name, ti, scl in FM:
            for hp in range(2):
                c0 = COL[name] + hp * 128
                bk, bb = fm_banks.next()
                for c in range(8):
                    MM(P, bk[:], W[:, c, c0:c0 + 128], hT[:, c, :], c == 0, c == 7, (Wbs[c], hb[c]), (bb,))
                st, stb, sch = fm_st.next()
                evac_i += 1
                if evac_i % 2 == 0:
                    ACT(P, st[:], bk[:], AF.Copy, (bb,), (stb,), scale=scl)
                else:
                    TS(P, "dve", st[:], bk[:], scl, None, ALU.mult, None, (bb,), (stb,))
                P.dma("sp", sch, io["qk_w"](2 * hp, ti, t * 512), st[0:64, :], (stb,), ())
                P.dma("sp", sch, io["qk_w"](2 * hp + 1, ti, t * 512), st[64:128, :], (stb,), ())
        if DBG < 2:
            continue
        bk, bb = fm_banks.next()
        for c in range(8):
            MM(P, bk[0:4, :], W[:, c, COL["fC"]:COL["fC"] + 4], hT[:, c, :], c == 0, c == 7, (Wbs[c], hb[c]), (bb,))
        st, stb, sch = f_st.next()
        CP(P, "dve", st[:], bk[0:4, :], (bb,), (stb,))
        P.dma("sp", sch, io["f_send"][:, ts], st[:], (stb,), ())
        if DBG < 3:
            continue
        for sub in range(4):
            tok0 = t * 512 + sub * 128
            hs = slice(sub * 128, (sub + 1) * 128)

            def tm_group(c0, n, col_off=0, bank=None):
                bk_, bb_ = bank if bank is not None else tm_banks.next()
                for c in range(8):
                    MM(P, bk_[:, col_off:col_off + n], hT[:, c, hs], W[:, c, c0:c0 + n], c == 0, c == 7,
                       (Wbs[c], hb[c]), (bb_,))
                return bk_, bb_

            vst, vstb, vch = v_st.next()
            gst, gstb, gch = g_st.next()
            bk, bb = tm_group(COL["vA"], 512)
            CP(P, "dve", vst[:, 0, :], bk[:, 0:256], (bb,), (vstb,))
            ACT(P, gst[:, 0:256], bk[:, 256:512], SILU, (bb,), (gstb,))
            bk, bb = tm_group(COL["vC"], 512)
            CP(P, "dve", vst[:, 1, :], bk[:, 0:256], (bb,), (vstb,))
            ACT(P, gst[:, 512:768], bk[:, 256:512], SILU, (bb,), (gstb,))
            bk, bb = tm_group(COL["vD"], 512)
            CP(P, "dve", vst[:, 2, :], bk[:, 0:256], (bb,), (vstb,))
            ACT(P, gst[:, 768:1024], bk[:, 256:512], SILU, (bb,), (gstb,))
            bk, bb = tm_group(COL["gB"], 256)
            ACT(P, gst[:, 256:512], bk[:, 0:256], SILU, (bb,), (gstb,))
            for vi in range(3 if not os.environ.get("NOV") else 0):
                P.dma("sp", vch, io["v_w"](vi, tok0),
                      vst[:, vi, :].rearrange("p (h d) -> p h d", h=4), (vstb,), ())
            if not os.environ.get("NOG"):
                P.dma("sp", gch, io["gs"][tok0:tok0 + 128, :], gst[:], (gstb,), ())
            if DBG < 4:
                continue
            bk, bb = tm_group(COL["uB"], 512)
            CP(P, "act", u_sb[:], bk[:, 0:256], (bb,), (ub,))
            vps = bk[:, 256:512]
            P.op("dve", lambda e, o=st4[:, 0:1], i=vps: e.reduce_sum(o, i, axis=AX.X), (bb,), (st4b,))
            ACT(P, junk[:], vps, AF.Square, (bb,), (jb, st4b), accum=st4[:, 1:2])
            TS(P, "dve", st4[:, 2:3], st4[:, 0:1], -1.0 / 256, None, ALU.mult, None, (st4b,), (st4b,))
            TT(P, "dve", st4[:, 3:4], st4[:, 2:3], st4[:, 2:3], ALU.mult, (st4b,), (st4b,))
            STT(P, "dve", st4[:, 4:5], st4[:, 1:2], 1.0 / 256, st4[:, 3:4], ALU.mult, ALU.subtract, (st4b,), (st4b,))
            RSQRT(P, st4[:, 5:6], st4[:, 4:5], EPS, (st4b,), (st4b,))
            TS(P, "dve", vtmp[:], vps, st4[:, 2:3], st4[:, 5:6], ALU.add, ALU.mult, (bb, st4b), (vtb,))
            TT(P, "dve", vn[:], vtmp[:], vg[:], ALU.mult, (vtb, vgb), (vnb,))
            mk, mb = mix_bank
            for g in range(4):
                gs_ = slice(g * 64, (g + 1) * 64)
                MM(P, mk[:, gs_], Wtr[:, g, :], vn[:, gs_], True, True, (Wtrb, vnb), (mb,))
            yst, ystb, ych = yb_st.next()
            for g in range(4):
                gs_ = slice(g * 64, (g + 1) * 64)
                STT(P, "dve", yst[:, gs_], mk[:, gs_], bsT[:, g:g + 1], u_sb[:, gs_], ALU.add, ALU.mult,
                    (mb, bsb, ub), (ystb,))
            P.dma("sp", ych, io["yb"][tok0:tok0 + 128, :], yst[:], (ystb,), ())
        if "after_tile" in io:
            io["after_tile"](t, [r[1] for r in fm_st.items + v_st.items])
    outs = [r[1] for r in fm_st.items + f_st.items + v_st.items + g_st.items + yb_st.items]
    return outs


SKEW_AC = tuple(int(v) for v in os.environ.get("SKEW_AC", "0,0,2").split(","))
SKEW_D = tuple(int(v) for v in os.environ.get("SKEW_D", "0,0,1,2,3,4,5").split(","))


def run_pipeline(tiles, skews):
    n = len(tiles)
    for st in range(n + max(skews)):
        for j, sk in enumerate(skews):
            i = st - sk
            if 0 <= i < n:
                tiles[i][j]()


def bc_last(ap2, n):
    return ap2.unsqueeze(2).broadcast_to([ap2.shape[0], ap2.shape[1], n])


def emit_p2(P, io, S, Tc, cst, mixers=("A", "C", "D")):
    NB = S // 128
    NQT = S // 512
    assert NB <= 128

    def cload(src, shape, dt=F32, name="c"):
        t = P.sb(shape, dt, name); b = Buf()
        P.dma("sp", P.chan(), t[:], src, (), (b,))
        return t, b

    ident_f, identfb = cload(cst["ident"], [128, 128], name="identf")
    ident_b = P.sb([128, 128], BF16, "identb"); identb = Buf()
    CP(P, "dve", ident_b[:], ident_f[:], (identfb,), (identb,))
    ntf, ntfb = cload(cst["negtri"], [128, 128], name="ntf")
    negtri = P.sb([128, 128], BF16, "negtri"); negtrib = Buf()
    CP(P, "dve", negtri[:], ntf[:], (ntfb,), (negtrib,))
    ones_col = P.sb([128, 1], BF16, "onec"); onecb = Buf()
    MEMSET(P, "dve", ones_col[:], 1.0, (onecb,))
    stg = P.sb([128, 4, 512], F32, "stg"); stgb = Buf()
    masks = {}
    for nm in ("negmaskC", "negmaskD", "mask01D"):
        P.dma("sp", P.chan(), stg[:], cst[nm], (), (stgb,))
        mt = P.sb([128, 4, 512], BF16, nm); mb_ = Buf()
        CP(P, "dve", mt[:], stg[:], (stgb,), (mb_,))
        masks[nm] = (mt, mb_)

    QT = P.sb([128, S], BF16, "QT"); QTb = Buf()
    KT = P.sb([128, S], BF16, "KT"); KTb = Buf()
    V = P.sb([128, NB, 65], BF16, "V"); Vb = Buf()
    MEMSET(P, "pool", V[:, :, 64:65], 1.0, (Vb,))
    qch, kch, vch = P.chan(), P.chan(), P.chan()
    nbl = Tc // 128

    def load_qkv(m):
        ntq = io["ntq"]
        tq_ = Tc // ntq
        nbq = tq_ // 128
        for i in range(4):
            for tq in range(ntq):
                c0 = i * Tc + tq * tq_
                P.dma("sp", qch, QT[0:64, c0:c0 + tq_], io["qk"](i, 2 * m, tq), (), (QTb,))
                P.dma("sp", kch, KT[0:64, c0:c0 + tq_], io["qk"](i, 2 * m + 1, tq), (), (KTb,))
                b0 = i * nbl + tq * nbq
                P.dma("sp", vch, V[:, b0:b0 + nbq, 0:64], io["v"](i, m, tq), (), (Vb,))

    banks = [(P.ps([128, 512], F32, "bk"), Buf(True)) for _ in range(8)]
    psum_all = P.psum_all
    Pts = Ring([(P.sb([128, 512], BF16, "Pt"), Buf()) for _ in range(4)])
    ysts = Ring([(P.sb([128, 4, 64], io.get("y_dt", F32), "yst"), Buf(), P.chan()) for _ in range(2)])
    after_mixer = io.get("after_mixer", lambda mi, bufs: None)
    ybufs = [r[1] for r in ysts.items]
    rc = P.sb([128, 4], F32, "rc"); rcb = Buf()

    def y_out(yst, ystb, ych, mi, tok0, nsub):
        i, off = tok0 // Tc, tok0 % Tc
        P.dma("sp", ych, io["y_w"](i, mi, off, nsub), yst[:, 0:nsub, :], (ystb,), ())

    if "A" in mixers:
        load_qkv(0)
        bA, bAb = cload(io["biasA"], [128, 5, 128], name="bA")
        mA, mAb = cload(cst["maskA"], [128, 5, 128], name="mA")
        TT(P, "dve", bA[:], bA[:], mA[:], ALU.add, (bAb, mAb), (bAb,))
        BAhi = P.sb([128, 5, 128], BF16, "BAhi"); BAlo = P.sb([128, 5, 128], BF16, "BAlo"); BAb = Buf()
        CP(P, "dve", BAhi[:], bA[:], (bAb,), (BAb,))
        TT(P, "dve", BAlo[:], bA[:], BAhi[:], ALU.subtract, (bAb, BAb), (BAb,))
        Sr = Ring(banks[0:4]); Or = Ring(banks[4:6])
        tiles = []
        for qb in range(NB):
            kbs = list(range(max(0, qb - 4), qb + 1))
            Obk, Ob = Or.next()
            qs = slice(qb * 128, (qb + 1) * 128)
            for idx, kb in enumerate(kbs):
                j = kb - qb + 4
                ks = slice(kb * 128, (kb + 1) * 128)
                Sbk, Sb = Sr.next()
                Pt, Ptb = Pts.next()

                def st0(Sbk=Sbk, Sb=Sb, ks=ks, qs=qs, j=j):
                    MM(P, Sbk[:, 0:128], KT[0:64, ks], QT[0:64, qs], True, False, (KTb, QTb), (Sb,))
                    MM(P, Sbk[:, 0:128], ident_b[:], BAhi[:, j, :], False, False, (identb, BAb), (Sb,))
                    MM(P, Sbk[:, 0:128], ident_b[:], BAlo[:, j, :], False, True, (identb, BAb), (Sb,))

                def st1(Sbk=Sbk, Sb=Sb, Pt=Pt, Ptb=Ptb):
                    ACT(P, Pt[:, 0:128], Sbk[:, 0:128], AF.Exp, (Sb,), (Ptb,))

                def st2(Pt=Pt, Ptb=Ptb, Obk=Obk, Ob=Ob, kb=kb, idx=idx, n=len(kbs), qb=qb):
                    MM(P, Obk[:, 0:65], Pt[:, 0:128], V[:, kb, :], idx == 0, idx == n - 1, (Ptb, Vb), (Ob,))
                    if idx == n - 1:
                        yst, ystb, ych = ysts.next()
                        P.op("dve", lambda e, o=rc[:, 0:1], i=Obk[:, 64:65]: e.reciprocal(o, i), (Ob,), (rcb,))
                        TS(P, "dve", yst[:, 0, :], Obk[:, 0:64], rc[:, 0:1], None, ALU.mult, None, (Ob, rcb), (ystb,))
                        y_out(yst, ystb, ych, 0, qb * 128, 1)

                tiles.append((st0, st1, st2))
        run_pipeline(tiles, SKEW_AC)
        after_mixer(0, ybufs)

    if "C" in mixers:
        load_qkv(1)
        Ff = P.sb([128, 128], F32, "Ff"); Ffb = Buf()
        fch = P.chan()
        for i in range(4):
            P.dma("sp", fch, Ff[i * nbl:(i + 1) * nbl, :], io["f"](i), (), (Ffb,))
        bfc, bfcb = cload(io["bf_col"], [128, 1], name="bfc")
        TS(P, "dve", bfc[:], bfc[:], -1.0, None, ALU.mult, None, (bfcb,), (bfcb,))
        ACT(P, Ff[0:NB, :], Ff[0:NB, :], AF.Exp, (Ffb, bfcb), (Ffb,), bias=bfc[0:NB, :], scale=-1.0)
        ACT(P, Ff[0:NB, :], Ff[0:NB, :], AF.Ln, (Ffb,), (Ffb,), bias=1.0)
        tot = P.sb([128, 1], F32, "tot"); totb = Buf()
        P.op("dve", lambda e, o=tot[0:NB, :], i=Ff[0:NB, :]: e.reduce_sum(o, i, axis=AX.X), (Ffb,), (totb,))
        onesf = P.sb([128, 128], F32, "onesf"); onesfb = Buf()
        MEMSET(P, "dve", onesf[:], 1.0, (onesfb,))
        totbc = P.sb([128, 128], F32, "totbc"); totbcb = Buf()
        TS(P, "dve", totbc[0:NB, :], onesf[0:NB, :], tot[0:NB, :], None, ALU.mult, None, (onesfb, totb), (totbcb,))
        triU, triUb = cload(cst["triU"], [128, 128], name="triU")
        SU, SUb = cload(cst["SU"], [128, 128], name="SU")
        b6, b6b = banks[6]
        b7, b7b = banks[7]
        P.op("pe", lambda e, o=b6[:, 0:NB], i=Ff[0:NB, :], idn=ident_f[0:NB, 0:NB]: e.transpose(o, i, idn),
             (Ffb, identfb), (b6b,))
        LT = P.sb([128, 128], F32, "LT"); LTb = Buf()
        CP(P, "dve", LT[:, 0:NB], b6[:, 0:NB], (b6b,), (LTb,))
        MM(P, b7[0:NB, 0:128], LT[:, 0:NB], triU[:], True, False, (LTb, triUb), (b7b,))
        MM(P, b7[0:NB, 0:128], SU[0:NB, 0:NB], totbc[0:NB, :], False, True, (SUb, totbcb), (b7b,))
        cpos = P.sb([128, 128], F32, "cpos"); cposb = Buf()
        CP(P, "dve", cpos[0:NB, :], b7[0:NB, 0:128], (b7b,), (cposb,))
        parts = P.sb([128, 6, 128], BF16, "parts"); partsb = Buf()
        r1 = P.sb([128, 128], F32, "r1"); r1b = Buf()
        CP(P, "dve", parts[0:NB, 3, :], cpos[0:NB, :], (cposb,), (partsb,))
        TT(P, "dve", r1[0:NB, :], cpos[0:NB, :], parts[0:NB, 3, :], ALU.subtract, (cposb, partsb), (r1b,))
        CP(P, "dve", parts[0:NB, 4, :], r1[0:NB, :], (r1b,), (partsb,))
        TT(P, "dve", r1[0:NB, :], r1[0:NB, :], parts[0:NB, 4, :], ALU.subtract, (r1b, partsb), (r1b,))
        CP(P, "dve", parts[0:NB, 5, :], r1[0:NB, :], (r1b,), (partsb,))
        for r in range(3):
            TS(P, "dve", parts[0:NB, r, :], parts[0:NB, 3 + r, :], -1.0, None, ALU.mult, None, (partsb,), (partsb,))
        csb = Buf()
        P.dma("sp", P.chan(), io["cs"].rearrange("r (i p) -> i r p", p=128), parts[0:NB, :, :], (partsb,), (csb,))
        MEMSET(P, "dve", QT[64:70, :], 1.0, (QTb,))
        MEMSET(P, "dve", KT[64:70, :], 1.0, (KTb,))
        P.dma("sp", qch, QT[64:67, :], io["cs"][0:3, :], (csb,), (QTb,))
        P.dma("sp", kch, KT[67:70, :], io["cs"][3:6, :], (csb,), (KTb,))
        nmC, nmCb = masks["negmaskC"]
        Sr = Ring(banks[0:4]); Or = Ring(banks[4:6])
        tiles = []
        for qt in range(NQT):
            q0 = qt * 512
            qs = slice(q0, q0 + 512)
            nkb = 4 * qt + 4
            Obk, Ob = Or.next()
            O3 = Obk[:, 0:260].rearrange("p (s d) -> p s d", d=65)
            for kb in range(nkb):
                ks = slice(kb * 128, (kb + 1) * 128)
                j = kb - 4 * qt
                Sbk, Sb = Sr.next()
                Pt, Ptb = Pts.next()

                def st0(Sbk=Sbk, Sb=Sb, ks=ks, qs=qs, j=j):
                    MM(P, Sbk[:], KT[0:70, ks], QT[0:70, qs], True, j < 0, (KTb, QTb), (Sb,))
                    if j >= 0:
                        MM(P, Sbk[:], ident_b[:], nmC[:, j, :], False, True, (identb, nmCb), (Sb,))

                def st1(Sbk=Sbk, Sb=Sb, Pt=Pt, Ptb=Ptb):
                    ACT(P, Pt[:], Sbk[:], AF.Exp, (Sb,), (Ptb,))

                def st2(Pt=Pt, Ptb=Ptb, O3=O3, Ob=Ob, kb=kb, j=j, qt=qt, q0=q0, nkb=nkb):
                    for sub in range(4):
                        if j > sub:
                            continue
                        MM(P, O3[:, sub, :], Pt[:, sub * 128:(sub + 1) * 128], V[:, kb, :], kb == 0 and sub == 0,
                           kb == 4 * qt + sub, (Ptb, Vb), (Ob,), skip=True)
                    if kb == nkb - 1:
                        yst, ystb, ych = ysts.next()
                        P.op("dve", lambda e, o=rc[:, 0:4], i=O3[:, :, 64]: e.reciprocal(o, i), (Ob,), (rcb,))
                        TT(P, "dve", yst[:], O3[:, :, 0:64], bc_last(rc[:, 0:4], 64), ALU.mult, (Ob, rcb), (ystb,))
                        y_out(yst, ystb, ych, 1, q0, 4)

                tiles.append((st0, st1, st2))
        run_pipeline(tiles, SKEW_AC)
        after_mixer(1, ybufs)

    if "D" in mixers:
        load_qkv(2)
        nmD, nmDb = masks["negmaskD"]
        m01, m01b = masks["mask01D"]
        zp = psum_all[:, 0:1024]; zpb = banks[0][1]
        ap_ = [(psum_all[:, 1024:2048], banks[2][1]), (psum_all[:, 2048:3072], banks[4][1])]
        Ar = Ring(ap_)
        OCr = Ring(banks[6:8])
        negones = P.sb([128, 128], BF16, "negones"); negonesb = Buf()
        MEMSET(P, "dve", negones[:], -1.0, (negonesb,))
        Es = Ring([(P.sb([128, 1024], F32, "E"), Buf()) for _ in range(3)])
        SPs = Ring([(P.sb([128, 1024], BF16, "SP"), Buf()) for _ in range(7)])
        PtD = Ring([(P.sb([128, 1024], BF16, "PtD"), Buf()) for _ in range(3)])
        acc = P.sb([128, 4, 64], F32, "acc"); accb = Buf()
        tmp = P.sb([128, 4, 64], F32, "tmp"); tmpb = Buf()
        carry = P.sb([128, 4], F32, "carry"); carryb = Buf()
        ec = P.sb([128, 4], F32, "ec"); ecb = Buf()
        tiles = []
        for qt in range(NQT):
            q0 = qt * 512
            qs = slice(q0, q0 + 512)
            kmax = 4 * qt + 3
            for k1 in range(kmax, -1, -2):
                k2 = k1 - 1
                kk = (k1, k2)
                kss = tuple(slice(k * 128, (k + 1) * 128) for k in kk)
                js = tuple(k - 4 * qt for k in kk)
                E, Eb = Es.next()
                SP, SPb = SPs.next()
                Abk, Ab = Ar.next()
                Pt, Ptb = PtD.next()
                OCbk, OCb = OCr.next()
                OC3 = OCbk[:, 0:260].rearrange("p (s d) -> p s d", d=65)

                def s0(kss=kss, qs=qs):
                    for h in range(2):
                        MM(P, zp[:, h * 512:(h + 1) * 512], KT[0:64, kss[h]], QT[0:64, qs], True, True, (KTb, QTb), (zpb,))

                def s1(E=E, Eb=Eb):
                    ACT(P, E[:], zp, AF.Exp, (zpb,), (Eb,))

                def s2(E=E, Eb=Eb, SP=SP, SPb=SPb, js=js):
                    ACT(P, SP[:], E[:], AF.Ln, (Eb,), (SPb,), bias=1.0)
                    for h in range(2):
                        if js[h] >= 0:
                            hs_ = slice(h * 512, (h + 1) * 512)
                            TT(P, "dve", SP[:, hs_], SP[:, hs_], m01[:, js[h], :], ALU.mult, (SPb, m01b), (SPb,))

                def s3(Abk=Abk, Ab=Ab, SP=SP, SPb=SPb, kss=kss, qs=qs, js=js):
                    for h in range(2):
                        o = Abk[:, h * 512:(h + 1) * 512]
                        hs_ = slice(h * 512, (h + 1) * 512)
                        last_is_tri = (h == 0) and js[h] < 0
                        MM(P, o, KT[0:64, kss[h]], QT[0:64, qs], True, False, (KTb, QTb), (Ab,))
                        MM(P, o, negtri[:], SP[:, hs_], False, last_is_tri, (negtrib, SPb), (Ab,))
                        if h == 1:
                            MM(P, o, negones[:], SP[:, 0:512], False, js[h] < 0, (negonesb, SPb), (Ab,))
                        if js[h] >= 0:
                            MM(P, o, ident_b[:], nmD[:, js[h], :], False, True, (identb, nmDb), (Ab,))

                def s4(Abk=Abk, Ab=Ab, Pt=Pt, Ptb=Ptb):
                    ACT(P, Pt[:], Abk, AF.Exp, (Ab,), (Ptb,))

                def s5(Pt=Pt, Ptb=Ptb, SP=SP, SPb=SPb, OC3=OC3, OCb=OCb, kk=kk):
                    for sub in range(4):
                        for h in range(2):
                            ss = slice(h * 512 + sub * 128, h * 512 + (sub + 1) * 128)
                            MM(P, OC3[:, sub, 0:64], Pt[:, ss], V[:, kk[h], 0:64], h == 0, h == 1, (Ptb, Vb), (OCb,))
                        for h in range(2):
                            ss = slice(h * 512 + sub * 128, h * 512 + (sub + 1) * 128)
                            MM(P, OC3[:, sub, 64:65], SP[:, ss], ones_col[:], h == 0, h == 1, (SPb, onecb), (OCb,))

                def s6(OC3=OC3, OCb=OCb, k1=k1, k2=k2, kmax=kmax, q0=q0):
                    if k1 == kmax:
                        CP(P, "dve", acc[:], OC3[:, :, 0:64], (OCb,), (accb,))
                        TS(P, "dve", carry[:], OC3[:, :, 64], -1.0, None, ALU.mult, None, (OCb,), (carryb,))
                    else:
                        ACT(P, ec[:], carry[:], AF.Exp, (carryb,), (ecb,))
                        TT(P, "dve", tmp[:], OC3[:, :, 0:64], bc_last(ec[:, 0:4], 64), ALU.mult, (OCb, ecb), (tmpb,))
                        TT(P, "dve", acc[:], acc[:], tmp[:], ALU.add, (accb, tmpb), (accb,))
                        TT(P, "dve", carry[:], carry[:], OC3[:, :, 64], ALU.subtract, (carryb, OCb), (carryb,))
                    if k2 == 0:
                        yst, ystb, ych = ysts.next()
                        CP(P, "dve", yst[:], acc[:], (accb,), (ystb,))
                        y_out(yst, ystb, ych, 2, q0, 4)

                tiles.append((s0, s1, s2, s3, s4, s5, s6))
        run_pipeline(tiles, SKEW_D)
        after_mixer(2, ybufs)
    return [r[1] for r in ysts.items]


def emit_p3(P, io, T, cst, final):
    NT = T // 512
    ident_f = P.sb([128, 128], F32, "identf"); identfb = Buf()
    P.dma("sp", P.chan(), ident_f[:], cst["ident"], (), (identfb,))
    ident_b = P.sb([128, 128], BF16, "identb"); identb = Buf()
    CP(P, "dve", ident_b[:], ident_f[:], (identfb,), (identb,))
    Wo = P.sb([128, 8, 1024], BF16, "Wo"); Wobs = [Buf() for _ in range(8)]
    wst = [(P.sb([128, 1024], F32, "wst"), Buf(), P.chan()) for _ in range(2)]
    w_v = io["w_out"].rearrange("(c p) n -> p c n", p=128)
    for c in range(8):
        st, sb_, ch = wst[c % 2]
        P.dma("sp", ch, st[:], w_v[:, c, :], (), (sb_,))
        CP(P, "dve" if c % 2 == 0 else "pool", Wo[:, c, :], st[:], (sb_,), (Wobs[c],))
    bg = P.sb([128, 1024], F32, "bg"); bgb = Buf()
    P.dma("sp", P.chan(), bg[:], io["bgain_bc"], (), (bgb,))
    TS(P, "dve", bg[:], bg[:], 16.0, None, ALU.mult, None, (bgb,), (bgb,))
    if final:
        fg = P.sb([128, 8], F32, "fg"); fgb = Buf()
        P.dma("sp", P.chan(), fg[:], io["fg8"], (), (fgb,))
        TS(P, "dve", fg[:], fg[:], 32.0, None, ALU.mult, None, (fgb,), (fgb,))
        ones = P.sb([128, 128], BF16, "ones"); onesb = Buf()
        MEMSET(P, "dve", ones[:], 1.0, (onesb,))
        sq = P.sb([128, 8, 512], BF16, "sq"); sqb = Buf()
        rbc = P.sb([128, 512], F32, "rbc"); rbcb = Buf()

    y_dt = io.get("y_dt", F32)
    yts = Ring([(P.sb([128, 3, 256], y_dt, "yt"), P.sb([128, 256], F32, "ytB"), Buf(), P.chan()) for _ in range(2)])
    gts = Ring([(P.sb([128, 1024], F32, "gt"), Buf(), P.chan()) for _ in range(2)])
    xts = Ring([(P.sb([128, 8, 512], F32, "xt"), Buf(), P.chan()) for _ in range(2)])
    xns = Ring([(P.sb([128, 8, 512], F32, "xn"), [Buf() for _ in range(8)], P.chan("pool")) for _ in range(2)])
    mTs = Ring([(P.sb([128, 8, 512], BF16, "mT"), Buf()) for _ in range(2)])
    t1 = P.sb([128, 1024], F32, "t1"); t1b = Buf()
    m = P.sb([128, 1024], BF16, "m"); mb = Buf()
    junk = P.sb([128, 256], F32, "junk"); jb = Buf()
    ssq = P.sb([128, 4], F32, "ssq"); ssqb = Buf()
    tps = Ring([(P.ps([128, 1024], BF16, "tp"), Buf(True)) for _ in range(2)])
    obanks = Ring([(P.ps([128, 512], F32, "ob"), Buf(True)) for _ in range(4)])
    if final:
        sbank = (P.ps([128, 512], F32, "sbk"), Buf(True))
    x_v = io["xT"].rearrange("(c p) t -> p c t", p=128)
    o_v = io["xT_out"].rearrange("(c p) t -> p c t", p=128)
    for t in range(NT):
        ts = slice(t * 512, (t + 1) * 512)
        xt, xb, xch = xts.next()
        P.dma("sp", xch, xt[:], x_v[:, :, ts], (), (xb,))
        mT, mTb = mTs.next()
        for sub in range(4):
            tok0 = t * 512 + sub * 128
            yt3, ytB, ytb, ych = yts.next()
            for mi in range(3):
                P.dma("sp", ych, yt3[:, mi, :].rearrange("p (h d) -> p h d", h=4),
                      io["y"](mi, tok0), (), (ytb,))
            P.dma("sp", ych, ytB[:], io["yb"][tok0:tok0 + 128, :], (), (ytb,))
            ysrc = (yt3[:, 0, :], ytB[:], yt3[:, 1, :], yt3[:, 2, :])
            gt, gtb, gch = gts.next()
            P.dma("sp", gch, gt[:], io["gs"][tok0:tok0 + 128, :], (), (gtb,))
            for br in range(4):
                ACT(P, junk[:], ysrc[br], AF.Square, (ytb,), (jb, ssqb), accum=ssq[:, br:br + 1])
            RSQRT(P, ssq[:], ssq[:], 256.0 * EPS, (ssqb,), (ssqb,))
            TT(P, "pool", t1[:], gt[:], bg[:], ALU.mult, (gtb, bgb), (t1b,))
            for br in range(4):
                cs_ = slice(br * 256, (br + 1) * 256)
                STT(P, "dve", m[:, cs_], ysrc[br], ssq[:, br:br + 1], t1[:, cs_], ALU.mult, ALU.mult,
                    (ytb, ssqb, t1b), (mb,))
            tp, tpb = tps.next()
            for c in range(8):
                P.op("pe", lambda e, o=tp[:, c * 128:(c + 1) * 128], i=m[:, c * 128:(c + 1) * 128], idn=ident_b[:]:
                     e.transpose(o, i, idn), (mb, identb), (tpb,))
            CP(P, "act", mT[:, :, sub * 128:(sub + 1) * 128], tp[:].rearrange("p (c t) -> p c t", c=8), (tpb,), (mTb,))
        xn, xnbs, xnch = xns.next()
        for cb in range(8):
            ob, obb = obanks.next()
            for c in range(8):
                MM(P, ob[:], Wo[:, c, cb * 128:(cb + 1) * 128], mT[:, c, :], c == 0, c == 7, (Wobs[c], mTb), (obb,))
            TT(P, "dve", xn[:, cb, :], xt[:, cb, :], ob[:], ALU.add, (xb, obb), (xnbs[cb],))
        if final:
            ACT(P, sq[:], xn[:], AF.Square, tuple(xnbs), (sqb,))
            sbk, sbb = sbank
            for c in range(8):
                MM(P, sbk[:], ones[:], sq[:, c, :], c == 0, c == 7, (onesb, sqb), (sbb,))
            RSQRT(P, rbc[:], sbk[:], 1024.0 * EPS, (sbb,), (rbcb,))
            for c in range(8):
                STT(P, "dve", xn[:, c, :], xn[:, c, :], fg[:, c:c + 1], rbc[:], ALU.mult, ALU.mult,
                    (xnbs[c], fgb, rbcb), (xnbs[c],))
        P.dma("pool", xnch, o_v[:, :, ts], xn[:], tuple(xnbs), ())
    return [b for r in xns.items for b in r[1]]


T_CORE = 4096
SEQ = 16384
_CACHE = {}


def _ext(P, name, shape, dt, kind):
    return P.dram(name, shape, dt, kind)


def static_p1_io(io):
    io["qk_w"] = lambda h, ti, t0: io["qk_send"][h, ti, :, t0:t0 + 512]
    io["v_w"] = lambda vi, tok0: io["v_send"][:, vi, tok0:tok0 + 128, :].rearrange("h t d -> t h d")


def static_p2_io(io):
    io["ntq"] = 1
    io["qk"] = lambda i, idx, tq: io["qk_recv"][i, idx]
    io["v"] = lambda i, m, tq: io["v_recv"][i, m].rearrange("(n p) d -> p n d", p=128)
    io["f"] = lambda i: io["f_recv"][i].rearrange("(n p) -> n p", p=128)
    io["y_w"] = lambda i, mi, off, nsub: io["y_send"][i, mi, off:off + nsub * 128, :].rearrange("(s p) d -> p s d", p=128)


def static_p3_io(io):
    io["y"] = lambda mi, tok0: io["y_recv"][:, mi, tok0:tok0 + 128, :].rearrange("h t d -> t h d")


def build_prog1():
    P = Prog(); T = T_CORE; io = {}
    io["xT"] = _ext(P, "xT", [1024, T], F32, "ExternalInput")
    io["w_in"] = _ext(P, "w_in", [1024, N_IN], F32, "ExternalInput")
    io["g8"] = _ext(P, "g8", [128, 8], F32, "ExternalInput")
    io["vgain_bc"] = _ext(P, "vgain_bc", [128, 256], F32, "ExternalInput")
    io["wsT"] = _ext(P, "wsT", [128, 4, 128], F32, "ExternalInput")
    io["bsT"] = _ext(P, "bsT", [128, 4], F32, "ExternalInput")
    cst = {"tril01T": _ext(P, "tril01T", [128, 128], F32, "ExternalInput")}
    io["qk_send"] = _ext(P, "qk_send", [4, 6, 64, T], BF16, "ExternalOutput")
    io["v_send"] = _ext(P, "v_send", [4, 3, T, 64], BF16, "ExternalOutput")
    io["f_send"] = _ext(P, "f_send", [4, T], F32, "ExternalOutput")
    io["gs"] = _ext(P, "gs", [T, 1024], F32, "ExternalOutput")
    io["yb"] = _ext(P, "yb", [T, 256], F32, "ExternalOutput")
    static_p1_io(io)
    outs = emit_p1(P, io, T, cst)
    P.wait_all("sp", outs)
    return P.emit()


P2_CONSTS = ("ident", "negtri", "triU", "SU", "negmaskC", "negmaskD", "mask01D", "maskA")


def build_prog2():
    P = Prog(); S = SEQ; Tc = T_CORE; io = {}
    io["qk_recv"] = _ext(P, "qk_recv", [4, 6, 64, Tc], BF16, "ExternalInput")
    io["v_recv"] = _ext(P, "v_recv", [4, 3, Tc, 64], BF16, "ExternalInput")
    io["f_recv"] = _ext(P, "f_recv", [4, Tc], F32, "ExternalInput")
    io["biasA"] = _ext(P, "biasA", [128, 5, 128], F32, "ExternalInput")
    io["bf_col"] = _ext(P, "bf_col", [128, 1], F32, "ExternalInput")
    io["y_send"] = _ext(P, "y_send", [4, 3, Tc, 64], F32, "ExternalOutput")
    io["cs"] = P.dram("cs", [6, S], BF16, "Internal")
    cn = host_consts()
    cst = {k: _ext(P, k, list(cn[k].shape), F32, "ExternalInput") for k in P2_CONSTS}
    static_p2_io(io)
    outs = emit_p2(P, io, S, Tc, cst)
    P.wait_all("sp", outs)
    return P.emit()


def build_prog3(final):
    P = Prog(); T = T_CORE; io = {}
    io["y_recv"] = _ext(P, "y_recv", [4, 3, T, 64], F32, "ExternalInput")
    io["yb"] = _ext(P, "yb", [T, 256], F32, "ExternalInput")
    io["gs"] = _ext(P, "gs", [T, 1024], F32, "ExternalInput")
    io["xT"] = _ext(P, "xT", [1024, T], F32, "ExternalInput")
    io["w_out"] = _ext(P, "w_out", [1024, 1024], F32, "ExternalInput")
    io["bgain_bc"] = _ext(P, "bgain_bc", [128, 1024], F32, "ExternalInput")
    if final:
        io["fg8"] = _ext(P, "fg8", [128, 8], F32, "ExternalInput")
    io["xT_out"] = _ext(P, "xT_out", [1024, T], F32, "ExternalOutput")
    cst = {"ident": _ext(P, "ident", [128, 128], F32, "ExternalInput")}
    static_p3_io(io)
    outs = emit_p3(P, io, T, cst, final)
    P.wait_all("sp", outs)
    return P.emit()


def _run(nc, in_maps):
    res = run_bass_kernel_spmd(nc, in_maps, core_ids=list(range(8)))
    return res.results


def kernel_unfused(x, norm_g, w_in, b_f, rel_bias, w_s, b_s, v_gain, branch_gain, w_out, final_g):
    f32 = np.float32
    x = np.asarray(x, f32)
    cn = host_consts()
    depth = 2
    c32 = lambda a: np.ascontiguousarray(np.asarray(a, f32))
    xT = [c32(x[c // 4, (c % 4) * T_CORE:(c % 4 + 1) * T_CORE, :].T) for c in range(8)]
    nc1 = build_prog1()
    nc2 = build_prog2()
    for l in range(depth):
        final = l == depth - 1
        com1 = dict(w_in=c32(w_in[l]), g8=c32(np.asarray(norm_g[l]).reshape(8, 128).T),
                    vgain_bc=c32(np.tile(np.asarray(v_gain[l])[None, :], (128, 1))),
                    wsT=c32(np.asarray(w_s[l]).transpose(2, 0, 1)), bsT=c32(np.asarray(b_s[l]).T),
                    tril01T=cn["tril01T"])
        r1 = _run(nc1, [dict(com1, xT=xT[c]) for c in range(8)])
        im2 = []
        for c in range(8):
            b, h = c // 4, c % 4
            d = dict(qk_recv=np.stack([r1[b * 4 + i]["qk_send"][h] for i in range(4)]),
                     v_recv=np.stack([r1[b * 4 + i]["v_send"][h] for i in range(4)]),
                     f_recv=np.stack([r1[b * 4 + i]["f_send"][h] for i in range(4)]),
                     biasA=host_biasA(np.asarray(rel_bias[l][h], f32)),
                     bf_col=np.full((128, 1), np.asarray(b_f, f32)[l, h], f32))
            for k in P2_CONSTS:
                d[k] = cn[k]
            im2.append(d)
        r2 = _run(nc2, im2)
        nc3 = build_prog3(final)
        com3 = dict(w_out=c32(w_out[l]), ident=cn["ident"],
                    bgain_bc=c32(np.tile(np.asarray(branch_gain[l]).reshape(1, 1024), (128, 1))))
        if final:
            com3["fg8"] = c32(np.asarray(final_g).reshape(8, 128).T)
        im3 = []
        for c in range(8):
            b, i = c // 4, c % 4
            im3.append(dict(com3, y_recv=np.stack([r2[b * 4 + h]["y_send"][i] for h in range(4)]),
                            yb=r1[c]["yb"], gs=r1[c]["gs"], xT=xT[c]))
        r3 = _run(nc3, im3)
        xT = [np.asarray(r3[c]["xT_out"]) for c in range(8)]
    out = np.empty((2, SEQ, D_MODEL), f32)
    for c in range(8):
        out[c // 4, (c % 4) * T_CORE:(c % 4 + 1) * T_CORE, :] = xT[c].T
    return out


GROUPS = [[0, 1, 2, 3], [4, 5, 6, 7]]


def build_fused(T=T_CORE, depth=2):
    S = 4 * T
    NTQ = 4 if T >= 2048 else 1
    TQ = T // NTQ
    NT8 = max(1, T // 512)
    T8 = T // NT8
    P = Prog()
    P.use_pid = True
    cn = host_consts()
    cst = {k: P.dram(k, list(v.shape), F32, "ExternalInput") for k, v in cn.items()}
    xT_in = P.dram("xT", [1024, T], F32, "ExternalInput")
    out_ext = P.dram("outT", [1024, T], F32, "ExternalOutput")
    xT_mid = P.dram("xT_mid", [1024, T], F32, "Internal")
    gs = P.dram("gs_i", [T, 1024], F32, "Internal")
    yb = P.dram("yb_i", [T, 256], F32, "Internal")
    cs = P.dram("cs_i", [6, S], BF16, "Internal")
    fg8 = P.dram("fg8", [128, 8], F32, "ExternalInput")
    cch = P.chan("coll")
    FSTOP = int(os.environ.get("FSTOP", "99"))

    def hsel(e):
        return bass.ds(P.pid_cache[id(e)], 1)

    def exchange(name, nch, ch, dt):
        send = P.dram(f"{name}_s", [nch, 4, ch], dt, "Internal")
        gath = P.dram(f"{name}_g", [nch, 16, ch], dt, "Internal")
        recv = P.dram(f"{name}_r", [nch, 4, ch], dt, "Internal")

        def gather(c0=0, c1=nch, wait_bufs=()):
            for c in range(c0, c1):
                P.coll(cch, "AllGather", GROUPS, send[c], gath[c], (), (), after=tuple(wait_bufs))

        def finish():
            P.barrier()
            d = Buf()
            g4 = gath.rearrange("c (a h) x -> c a h x", h=4)
            P.dma("sp", P.chan(), recv.rearrange("c a (o x) -> c a o x", o=1),
                  lambda e: g4[:, :, hsel(e), :], (), (d,))
        return send, recv, gather, finish

    for l in range(depth):
        final = l == depth - 1
        ext = lambda nm, shp: P.dram(f"{nm}{l}", shp, F32, "ExternalInput")
        qk_s, qk_r, qk_g, qk_f = exchange(f"qk{l}", 6 * NTQ, 64 * TQ, BF16)
        v_s, v_r, v_g, v_f = exchange(f"v{l}", 3 * NTQ, TQ * 64, BF16)
        f_s, f_r, f_g, f_f = exchange(f"f{l}", 1, T, F32)
        y_s, y_r, y_g, y_f = exchange(f"y{l}", 3 * NT8, T8 * 64, BF16)
        io1 = dict(xT=xT_in if l == 0 else xT_mid, w_in=ext("w_in", [1024, N_IN]), g8=ext("g8_", [128, 8]),
                   vgain_bc=ext("vgain_bc", [128, 256]), wsT=ext("wsT", [128, 4, 128]), bsT=ext("bsT", [128, 4]),
                   f_send=f_s[0], gs=gs, yb=yb)
        io1["qk_w"] = lambda h, ti, t0, qk_s=qk_s: qk_s[ti * NTQ + t0 // TQ, h].rearrange(
            "(d t) -> d t", d=64)[:, t0 % TQ:t0 % TQ + 512]
        io1["v_w"] = lambda vi, tok0, v_s=v_s: v_s[vi * NTQ + tok0 // TQ].rearrange(
            "h (t d) -> t h d", d=64)[tok0 % TQ:tok0 % TQ + 128]
        tpq = TQ // 512

        def after_tile(t, bufs, qk_g=qk_g, v_g=v_g):
            if (t + 1) % tpq == 0:
                tq = t // tpq
                for ti in range(6):
                    qk_g(ti * NTQ + tq, ti * NTQ + tq + 1, bufs)
                for vi in range(3):
                    v_g(vi * NTQ + tq, vi * NTQ + tq + 1, bufs)

        if not os.environ.get("NOHOOK"):
            io1["after_tile"] = after_tile
        P.phase_begin()
        if not os.environ.get("SKIP1"):
            emit_p1(P, io1, T, cst)
        P.phase_end()
        if FSTOP <= 1:
            break
        f_g()
        qk_f(); v_f(); f_f()
        P.barrier()
        if FSTOP <= 3:
            break
        io2 = dict(biasA=ext("biasA", [128, 5, 128]), bf_col=ext("bf_col", [128, 1]), cs=cs, ntq=NTQ, y_dt=BF16)
        io2["after_mixer"] = lambda mi, bufs, y_g=y_g: y_g(mi * NT8, (mi + 1) * NT8, bufs)
        io2["qk"] = lambda i, idx, tq, qk_r=qk_r: qk_r[idx * NTQ + tq, i].rearrange("(d t) -> d t", d=64)
        io2["v"] = lambda i, m, tq, v_r=v_r: v_r[m * NTQ + tq, i].rearrange("(n p d) -> p n d", p=128, d=64)
        io2["f"] = lambda i, f_r=f_r: f_r[0, i].rearrange("(n p) -> n p", p=128)
        io2["y_w"] = lambda i, mi, off, nsub, y_s=y_s: y_s[mi * NT8 + off // T8, i].rearrange(
            "(t d) -> t d", d=64)[off % T8:off % T8 + nsub * 128].rearrange("(s p) d -> p s d", p=128)
        P.phase_begin()
        emit_p2(P, io2, S, T, cst)
        P.phase_end()
        if FSTOP <= 4:
            break
        y_f()
        P.barrier()
        io3 = dict(yb=yb, gs=gs, xT=io1["xT"], w_out=ext("w_out", [1024, 1024]), bgain_bc=ext("bgain_bc", [128, 1024]),
                   fg8=fg8, xT_out=out_ext if final else xT_mid, y_dt=BF16)
        io3["y"] = lambda mi, tok0, y_r=y_r: y_r[mi * NT8 + tok0 // T8].rearrange(
            "h (t d) -> t h d", d=64)[tok0 % T8:tok0 % T8 + 128]
        P.phase_begin()
        final_bufs = emit_p3(P, io3, T, cst, final)
        if final:
            P.wait_all("sp", final_bufs)
        P.phase_end()
    print("fused ops:", {k: len(v) for k, v in P.ops.items()}, "sems", P.nsem, flush=True)
    return P.emit()


def fused_inputs(x, norm_g, w_in, b_f, rel_bias, w_s, b_s, v_gain, branch_gain, w_out, final_g, T=T_CORE, depth=2):
    f32 = np.float32
    c32 = lambda a: np.ascontiguousarray(np.asarray(a, f32))
    cn = host_consts()
    com = dict(cn)
    com["fg8"] = c32(np.asarray(final_g).reshape(8, 128).T)
    for l in range(depth):
        com[f"w_in{l}"] = c32(w_in[l])
        com[f"g8_{l}"] = c32(np.asarray(norm_g[l]).reshape(8, 128).T)
        com[f"vgain_bc{l}"] = c32(np.tile(np.asarray(v_gain[l])[None, :], (128, 1)))
        com[f"wsT{l}"] = c32(np.asarray(w_s[l]).transpose(2, 0, 1))
        com[f"bsT{l}"] = c32(np.asarray(b_s[l]).T)
        com[f"w_out{l}"] = c32(w_out[l])
        com[f"bgain_bc{l}"] = c32(np.tile(np.asarray(branch_gain[l]).reshape(1, 1024), (128, 1)))
    ims = []
    for c in range(8):
        b, h = c // 4, c % 4
        d = dict(com)
        d["xT"] = c32(np.asarray(x)[b, h * T:(h + 1) * T, :].T)
        for l in range(depth):
            d[f"biasA{l}"] = host_biasA(np.asarray(rel_bias[l][h], f32))
            d[f"bf_col{l}"] = np.full((128, 1), np.asarray(b_f, f32)[l, h], f32)
        ims.append(d)
    return ims


def kernel(x, norm_g, w_in, b_f, rel_bias, w_s, b_s, v_gain, branch_gain, w_out, final_g):
    nc = build_fused()
    ims = fused_inputs(x, norm_g, w_in, b_f, rel_bias, w_s, b_s, v_gain, branch_gain, w_out, final_g)
    res = run_bass_kernel_spmd(nc, ims, core_ids=list(range(8))).results
    out = np.empty((2, SEQ, D_MODEL), np.float32)
    for c in range(8):
        out[c // 4, (c % 4) * T_CORE:(c % 4 + 1) * T_CORE, :] = np.asarray(res[c]["outT"]).T
    return out
```

```python
import os
import numpy as np
import ml_dtypes
from contextlib import ExitStack
import concourse.bass as bass
import concourse.mybir as mybir
from concourse.bass_utils import run_bass_kernel_spmd

F32 = mybir.dt.float32
BF16 = mybir.dt.bfloat16
AF = mybir.ActivationFunctionType
ALU = mybir.AluOpType
AX = mybir.AxisListType
NPBF = ml_dtypes.bfloat16

EPS = 1e-6
NEG = -30000.0
D_MODEL = 1024
N_IN = 3844
COL = dict(qA=0, kA=256, vA=512, gA=768, uB=1024, vB=1280, gB=1536, qC=1792, kC=2048,
           vC=2304, gC=2560, fC=2816, qD=2820, kD=3076, vD=3332, gD=3588)


class Buf:
    __slots__ = ("w", "r", "excl")

    def __init__(self, excl=False):
        self.w = None
        self.r = {}
        self.excl = excl


class Chan:
    _n = 0

    def __init__(self, sem):
        self.sem = sem
        self.val = 0
        Chan._n += 1
        self.uid = Chan._n


class Prog:
    ENGS = ("pe", "act", "dve", "pool", "sp")
    EPOCH = 8000

    def __init__(self):
        self.nc = bass.Bass("TRN2", target_bir_lowering=False)
        self.es = ExitStack()
        self.ops = {e: [] for e in self.ENGS}
        self.cnt = {e: 0 for e in self.ENGS}
        self.esems = {e: [] for e in self.ENGS}
        self.waited = {e: {} for e in self.ENGS}
        self.nsem = 0
        self.ntens = 0
        self.live_dma = {}
        self.free_chans = {}
        self.phase_chans = []
        self.pid_cache = {}
        self.use_pid = False
        self.arena = None
        self.arena_ptr = 0
        self.bank_ptr = 0
        self.es0 = ExitStack()

    def phase_begin(self):
        self.phase_chans = []
        self.arena_mark = (self.arena_ptr, self.bank_ptr)

    def phase_end(self):
        self.barrier()
        self.arena_ptr, self.bank_ptr = self.arena_mark
        for ch in self.phase_chans:
            self.free_chans.setdefault(ch.kind, []).append(ch)
        self.phase_chans = []

    def new_sem(self, name):
        self.nsem += 1
        return self.nc.alloc_semaphore(name=f"{name}{self.nsem}")

    def chan(self, kind="sp"):
        pool = self.free_chans.setdefault(kind, [])
        if pool:
            ch = pool.pop()
        else:
            ch = Chan(self.new_sem("ch"))
            ch.kind = kind
        self.phase_chans.append(ch)
        return ch

    ARENA_F32 = 45056
    _LET = "abcdefg"

    def _arena_init(self):
        if self.arena is None:
            self.arena = self.es0.enter_context(self.nc.sbuf_tensor("arena", [128, self.ARENA_F32], F32))
            self.psum_all = self.es0.enter_context(self.nc.psum_tensor("psum_all", [128, 4096], F32))
            self.banks = [self.psum_all[:, i * 512:(i + 1) * 512] for i in range(8)]

    def sb(self, shape, dt, name=None):
        self._arena_init()
        esz = 2 if dt == BF16 else 4
        n = int(np.prod(shape[1:]))
        n4 = (n * esz + 31) // 32 * 8
        off = self.arena_ptr
        assert off + n4 <= self.ARENA_F32, f"SBUF arena overflow ({name})"
        self.arena_ptr += n4
        ap = self.arena[0:shape[0], off:off + n4]
        if dt != F32:
            ap = ap.bitcast(dt)
        ap = ap[:, 0:n]
        if len(shape) > 2:
            names = " ".join(self._LET[:len(shape) - 1])
            kw = {self._LET[i]: shape[1 + i] for i in range(len(shape) - 1)}
            ap = ap.rearrange(f"p ({names}) -> p {names}", **kw)
        return ap

    def ps(self, shape, dt, name=None):
        self._arena_init()
        assert self.bank_ptr < 8, "out of PSUM banks"
        nb = 2 if (int(np.prod(shape[1:])) * (2 if dt == BF16 else 4)) > 2048 else 1
        if nb == 2:
            self.bank_ptr += self.bank_ptr % 2
            bk = self.psum_all[:, self.bank_ptr * 512:(self.bank_ptr + 2) * 512]
        else:
            bk = self.banks[self.bank_ptr][:]
        self.bank_ptr += nb
        assert self.bank_ptr <= 8, "out of PSUM banks"
        if dt != F32:
            bk = bk.bitcast(dt)
        return bk[0:shape[0], 0:int(np.prod(shape[1:]))]

    def dram(self, name, shape, dt, kind="Internal"):
        return self.nc.dram_tensor(name, list(shape), dt, kind=kind).ap()

    def _need(self, eng, ev):
        if ev is None:
            return None
        if ev[0] == "e":
            _, e2, idx = ev
            if e2 == eng and eng == "pe":
                return None
            key = ("e", e2)
            if self.waited[eng].get(key, 0) >= idx:
                return None
            self.waited[eng][key] = idx
            ep = (idx - 1) // self.EPOCH
            return (self.esems[e2][ep], (idx - 1) % self.EPOCH + 1)
        _, ch, val = ev
        key = ("d", ch.uid)
        if self.waited[eng].get(key, 0) >= val:
            return None
        self.waited[eng][key] = val
        return (ch.sem, val)

    def _deps(self, eng, reads, writes):
        evs = []
        for b in reads:
            evs.append(b.w)
            if b.excl:
                evs.extend(ev for ev in b.r.values() if not (ev[0] == "e" and ev[1] == eng))
        for b in writes:
            evs.append(b.w)
            evs.extend(b.r.values())
        return [w for w in (self._need(eng, ev) for ev in evs) if w]

    @staticmethod
    def _mark(ev, key, reads, writes):
        for b in reads:
            b.r[key] = ev
        for b in writes:
            b.w = ev
            b.r = {}

    def op(self, eng, fn, reads=(), writes=()):
        waits = self._deps(eng, reads, writes)
        self.cnt[eng] += 1
        idx = self.cnt[eng]
        ep = (idx - 1) // self.EPOCH
        while len(self.esems[eng]) <= ep:
            self.esems[eng].append(self.new_sem(eng))
        self.ops[eng].append((waits, fn, self.esems[eng][ep], 1))
        ev = ("e", eng, idx)
        self._mark(ev, ("e", eng), reads, writes)
        return ev

    def dma(self, eng, ch, out, in_, reads=(), writes=(), slow=False):
        waits = self._deps(eng, reads, writes)
        ch.val += 16
        import traceback as _tb
        site = _tb.extract_stack(limit=4)[:-1] if os.environ.get("DEBUGDMA") else None

        def fn(e, o=out, i=in_, slow=slow, site=site):
            o = o(e) if callable(o) else o
            i = i(e) if callable(i) else i
            if slow:
                return e.dma_start(out=o, in_=i, allow_slow_non_contiguous=True)
            try:
                r = e.dma_start(out=o, in_=i)
                self.ndma_ok = getattr(self, "ndma_ok", 0) + 1
                return r
            except Exception:
                print("DMA ok before:", getattr(self, "ndma_ok", 0), "site", site, flush=True)
                print("DMA FAIL out", o.shape, o.ap, "in", i.shape, i.ap, flush=True)
                raise
        self.ops[eng].append((waits, fn, ch.sem, 16))
        ev = ("d", ch, ch.val)
        self.live_dma[ch.uid] = ev
        self._mark(ev, ("d", ch.uid), reads, writes)
        return ev

    def coll(self, ch, kind, groups, in_ap, out_ap, reads=(), writes=(), inc=1, after=()):
        eng = "pool"
        waits = self._deps(eng, reads, writes) + self._deps(eng, (), after)
        ch.val += inc
        fn = lambda e, k=kind, g=groups, i=in_ap, o=out_ap: e.collective_compute(
            k, ALU.bypass, replica_groups=g, ins=[i], outs=[o])
        self.ops[eng].append((waits, fn, ch.sem, inc))
        ev = ("d", ch, ch.val)
        self.live_dma[ch.uid] = ev
        self._mark(ev, ("d", ch.uid), reads, writes)
        return ev

    def barrier(self):
        evs = [("e", e, self.cnt[e]) for e in self.ENGS if self.cnt[e] > 0]
        evs += list(self.live_dma.values())
        for eng in self.ENGS:
            waits = [w for w in (self._need(eng, ev) for ev in evs) if w]
            if waits:
                self.ops[eng].append((waits, None, None, 0))

    def wait_all(self, eng, bufs):
        waits = self._deps(eng, bufs, ())
        self.ops[eng].append((waits, None, None, 0))

    def flush(self):
        nc = self.nc
        if not any(self.ops[e] for e in self.ENGS):
            return
        with nc.Block() as block:
            table = (("pe", block.tensor), ("act", block.scalar), ("dve", block.vector),
                     ("pool", block.gpsimd), ("sp", block.sync))
            for name, deco in table:
                ops = self.ops[name]
                if not ops:
                    continue

                def body(engine, ops=ops, name=name):
                    if self.use_pid and name == "sp":
                        self.pid_cache[id(engine)] = engine.partition_id() % 4
                    for waits, fn, sem, inc in ops:
                        for (s, v) in waits:
                            engine.wait_ge(s, v)
                        if fn is not None:
                            try:
                                fn(engine).then_inc(sem, inc)
                            except Exception:
                                print("FAILED OP on", name, "waits", [(str(s), v) for s, v in waits], flush=True)
                                raise

                deco(body)
        self.ops = {e: [] for e in self.ENGS}

    def emit(self):
        self.flush()
        self.es.close()
        self.es0.close()
        return self.nc


def MM(P, out, lhsT, rhs, start, stop, reads, writes, skip=False):
    if skip:
        fn = lambda e, o=out, l=lhsT, r=rhs, a=start, b=stop: e.matmul(o, l, r, start=a, stop=b, skip_group_check=True)
    else:
        fn = lambda e, o=out, l=lhsT, r=rhs, a=start, b=stop: e.matmul(o, l, r, start=a, stop=b)
    return P.op("pe", fn, reads, writes)


def ACT(P, out, in_, func, reads, writes, bias=None, scale=None, accum=None):
    kw = {}
    if bias is not None:
        kw["bias"] = bias
    if scale is not None:
        kw["scale"] = scale
    if accum is not None:
        kw["accum_out"] = accum
    fn = lambda e, o=out, i=in_, f=func, kw=kw: e.activation(o, i, f, **kw)
    return P.op("act", fn, reads, writes)


def TS(P, eng, out, in0, s1, s2, op0, op1, reads, writes):
    if op1 is None:
        fn = lambda e, o=out, i=in0, a=s1, p0=op0: e.tensor_scalar(o, i, a, None, p0)
    else:
        fn = lambda e, o=out, i=in0, a=s1, b=s2, p0=op0, p1=op1: e.tensor_scalar(o, i, a, b, p0, p1)
    return P.op(eng, fn, reads, writes)


def TT(P, eng, out, in0, in1, op, reads, writes):
    fn = lambda e, o=out, a=in0, b=in1, p=op: e.tensor_tensor(o, a, b, p)
    return P.op(eng, fn, reads, writes)


def STT(P, eng, out, in0, scalar, in1, op0, op1, reads, writes):
    fn = lambda e, o=out, a=in0, s=scalar, b=in1, p0=op0, p1=op1: e.scalar_tensor_tensor(o, a, s, b, p0, p1)
    return P.op(eng, fn, reads, writes)


def CP(P, eng, out, in_, reads, writes):
    if eng == "act":
        fn = lambda e, o=out, i=in_: e.copy(o, i)
    else:
        fn = lambda e, o=out, i=in_: e.tensor_copy(o, i)
    return P.op(eng, fn, reads, writes)


def RSQRT(P, out, in_, eps, reads, writes):
    TS(P, "dve", out, in_, eps, None, ALU.add, None, reads, writes)
    ACT(P, out, out, AF.Sqrt, writes, writes)
    P.op("dve", lambda e, o=out: e.reciprocal(o, o), writes, writes)


def MEMSET(P, eng, ap, val, writes):
    fn = lambda e, a=ap, v=val: e.memset(a, v)
    return P.op(eng, fn, (), writes)


class Ring:
    def __init__(self, items):
        self.items = items
        self.i = 0

    def next(self):
        it = self.items[self.i % len(self.items)]
        self.i += 1
        return it


def host_consts():
    k = np.arange(128)
    c = {}
    c["ident"] = np.eye(128, dtype=np.float32)
    c["negtri"] = np.where(k[:, None] >= k[None, :], -1.0, 0.0).astype(np.float32)
    c["triU"] = (k[:, None] <= k[None, :]).astype(np.float32)
    c["SU"] = (k[:, None] < k[None, :]).astype(np.float32)
    q = np.arange(512)
    kk = (np.arange(4)[None, :, None] * 128 + k[:, None, None])
    c["negmaskC"] = np.where(kk <= q[None, None, :], 0.0, NEG).astype(np.float32)
    c["negmaskD"] = np.where(kk < q[None, None, :], 0.0, NEG).astype(np.float32)
    c["mask01D"] = (kk < q[None, None, :]).astype(np.float32)
    q1 = np.arange(128)
    kc = (2 * (np.arange(5)[None, :, None] - 4) + (k[:, None, None] // 64))
    cq = (q1[None, None, :] // 64)
    valid = (kc >= cq - 8) & (kc <= cq)
    c["maskA"] = np.where(valid, 0.0, NEG).astype(np.float32)
    c["tril01T"] = (k[:, None] <= k[None, :]).astype(np.float32)
    return c


def host_biasA(rel_bias_h):
    k = np.arange(128)
    q1 = np.arange(128)
    kpos = ((np.arange(5)[None, :, None] - 4) * 128 + k[:, None, None])
    d = np.clip(q1[None, None, :] - kpos, -128, 128) + 128
    return np.ascontiguousarray(rel_bias_h[d]).astype(np.float32)


import os
DBG = int(os.environ.get("P1DBG", "9"))
SILU = AF.Copy if os.environ.get("NOSILU") else AF.Silu


def emit_p1(P, io, T, cst):
    NT = T // 512
    W = P.sb([128, 8, N_IN], BF16, "W"); Wbs = [Buf() for _ in range(8)]
    HW_ = N_IN // 2
    wst = [(P.sb([128, HW_], F32, "wst"), Buf(), P.chan()) for _ in range(2)]
    w_v = io["w_in"].rearrange("(c p) n -> p c n", p=128)
    g32 = P.sb([128, 8], F32, "g32"); g32b = Buf()
    P.dma("sp", P.chan(), g32[:], io["g8"], (), (g32b,))
    TS(P, "dve", g32[:], g32[:], 32.0, None, ALU.mult, None, (g32b,), (g32b,))
    for c in range(8):
        for hf in range(2):
            st, sb_, ch = wst[hf]
            cs_ = slice(hf * HW_, (hf + 1) * HW_)
            P.dma("sp", ch, st[:], w_v[:, c, cs_], (), (sb_,))
            TS(P, "dve", W[:, c, cs_], st[:], g32[:, c:c + 1], None, ALU.mult, None, (sb_, g32b), (Wbs[c],))
    vg = P.sb([128, 256], F32, "vg"); vgb = Buf()
    P.dma("sp", P.chan(), vg[:], io["vgain_bc"], (), (vgb,))
    bsT = P.sb([128, 4], F32, "bsT"); bsb = Buf()
    P.dma("sp", P.chan(), bsT[:], io["bsT"], (), (bsb,))
    wsf = P.sb([128, 4, 128], F32, "wsf"); wsfb = Buf()
    P.dma("sp", P.chan(), wsf[:], io["wsT"], (), (wsfb,))
    trl = P.sb([128, 128], F32, "trl"); trlb = Buf()
    P.dma("sp", P.chan(), trl[:], cst["tril01T"], (), (trlb,))
    Wtr = P.sb([128, 4, 128], BF16, "Wtr"); Wtrb = Buf()
    for g in range(4):
        TT(P, "dve", Wtr[:, g, :], wsf[:, g, :], trl[:], ALU.mult, (wsfb, trlb), (Wtrb,))
    ones = P.sb([128, 128], BF16, "ones"); onesb = Buf()
    MEMSET(P, "dve", ones[:], 1.0, (onesb,))

    xts = Ring([(P.sb([128, 8, 512], F32, "xt"), Buf(), P.chan()) for _ in range(2)])
    sq = P.sb([128, 8, 512], BF16, "sq"); sqb = Buf()
    rbc = P.sb([128, 512], F32, "rbc"); rbcb = Buf()
    hTs = Ring([(P.sb([128, 8, 512], BF16, "hT"), [Buf() for _ in range(8)]) for _ in range(2)])
    banks = [(P.ps([128, 512], F32, "bk"), Buf(True)) for _ in range(8)]
    ssq_bank = banks[0]
    fm_banks = Ring(banks[1:3])
    tm_banks = Ring(banks[3:7])
    mix_bank = banks[7]
    fm_st = Ring([(P.sb([128, 512], BF16, "fmst"), Buf(), P.chan()) for _ in range(3)])
    f_st = Ring([(P.sb([4, 512], F32, "fst"), Buf(), P.chan()) for _ in range(2)])
    v_st = Ring([(P.sb([128, 3, 256], BF16, "vst"), Buf(), P.chan()) for _ in range(2)])
    g_st = Ring([(P.sb([128, 1024], io.get("gs_dt", F32), "gst"), Buf(), P.chan()) for _ in range(2)])
    yb_st = Ring([(P.sb([128, 256], F32, "ybst"), Buf(), P.chan()) for _ in range(2)])
    u_sb = P.sb([128, 256], F32, "u"); ub = Buf()
    vtmp = P.sb([128, 256], F32, "vtmp"); vtb = Buf()
    vn = P.sb([128, 256], BF16, "vn"); vnb = Buf()
    junk = P.sb([128, 256], F32, "junk"); jb = Buf()
    st4 = P.sb([128, 8], F32, "st4"); st4b = Buf()

    x_v = io["xT"].rearrange("(c p) t -> p c t", p=128)
    FM = [("qA", 0, 0.125), ("kA", 1, 1.0), ("qC", 2, 0.125), ("kC", 3, 1.0), ("qD", 4, 0.125), ("kD", 5, 1.0)]
    evac_i = 0
    xq = []

    def xload(tt):
        xt_, xb_, xch_ = xts.next()
        P.dma("sp", xch_, xt_[:], x_v[:, :, tt * 512:(tt + 1) * 512], (), (xb_,))
        xq.append((xt_, xb_))

    xload(0)
    for t in range(NT):
        ts = slice(t * 512, (t + 1) * 512)
        if t + 1 < NT:
            xload(t + 1)
        xt, xb = xq[t]
        ACT(P, sq[:], xt[:], AF.Square, (xb,), (sqb,))
        sbk, sbb = ssq_bank
        for c in range(8):
            MM(P, sbk[:], ones[:], sq[:, c, :], c == 0, c == 7, (onesb, sqb), (sbb,))
        RSQRT(P, rbc[:], sbk[:], 1024.0 * EPS, (sbb,), (rbcb,))
        hT, hb = hTs.next()
        for c in range(8):
            TT(P, "dve", hT[:, c, :], xt[:, c, :], rbc[:], ALU.mult,
               (xb, rbcb), (hb[c],))
        for name, ti, scl in FM:
            for hp in range(2):
                c0 = COL[name] + hp * 128
                bk, bb = fm_banks.next()
                for c in range(8):
                    MM(P, bk[:], W[:, c, c0:c0 + 128], hT[:, c, :], c == 0, c == 7, (Wbs[c], hb[c]), (bb,))
                st, stb, sch = fm_st.next()
                evac_i += 1
                if evac_i % 2 == 0:
                    ACT(P, st[:], bk[:], AF.Copy, (bb,), (stb,), scale=scl)
                else:
                    TS(P, "dve", st[:], bk[:], scl, None, ALU.mult, None, (bb,), (stb,))
                P.dma("sp", sch, io["qk_w"](2 * hp, ti, t * 512), st[0:64, :], (stb,), ())
                P.dma("sp", sch, io["qk_w"](2 * hp + 1, ti, t * 512), st[64:128, :], (stb,), ())
        if DBG < 2:
            continue
        bk, bb = fm_banks.next()
        for c in range(8):
            MM(P, bk[0:4, :], W[:, c, COL["fC"]:COL["fC"] + 4], hT[:, c, :], c == 0, c == 7, (Wbs[c], hb[c]), (bb,))
        st, stb, sch = f_st.next()
        CP(P, "dve", st[:], bk[0:4, :], (bb,), (stb,))
        P.dma("sp", sch, io["f_send"][:, ts], st[:], (stb,), ())
        if DBG < 3:
            continue
        for sub in range(4):
            tok0 = t * 512 + sub * 128
            hs = slice(sub * 128, (sub + 1) * 128)

            def tm_group(c0, n, col_off=0, bank=None):
                bk_, bb_ = bank if bank is not None else tm_banks.next()
                for c in range(8):
                    MM(P, bk_[:, col_off:col_off + n], hT[:, c, hs], W[:, c, c0:c0 + n], c == 0, c == 7,
                       (Wbs[c], hb[c]), (bb_,))
                return bk_, bb_

            vst, vstb, vch = v_st.next()
            gst, gstb, gch = g_st.next()
            bk, bb = tm_group(COL["vA"], 512)
            CP(P, "dve", vst[:, 0, :], bk[:, 0:256], (bb,), (vstb,))
            ACT(P, gst[:, 0:256], bk[:, 256:512], SILU, (bb,), (gstb,))
            bk, bb = tm_group(COL["vC"], 512)
            CP(P, "dve", vst[:, 1, :], bk[:, 0:256], (bb,), (vstb,))
            ACT(P, gst[:, 512:768], bk[:, 256:512], SILU, (bb,), (gstb,))
            bk, bb = tm_group(COL["vD"], 512)
            CP(P, "dve", vst[:, 2, :], bk[:, 0:256], (bb,), (vstb,))
            ACT(P, gst[:, 768:1024], bk[:, 256:512], SILU, (bb,), (gstb,))
            bk, bb = tm_group(COL["gB"], 256)
            ACT(P, gst[:, 256:512], bk[:, 0:256], SILU, (bb,), (gstb,))
            for vi in range(3 if not os.environ.get("NOV") else 0):
                P.dma("sp", vch, io["v_w"](vi, tok0),
                      vst[:, vi, :].rearrange("p (h d) -> p h d", h=4), (vstb,), ())
            if not os.environ.get("NOG"):
                P.dma("sp", gch, io["gs"][tok0:tok0 + 128, :], gst[:], (gstb,), ())
            if DBG < 4:
                continue
            bk, bb = tm_group(COL["uB"], 512)
            CP(P, "act", u_sb[:], bk[:, 0:256], (bb,), (ub,))
            vps = bk[:, 256:512]
            P.op("dve", lambda e, o=st4[:, 0:1], i=vps: e.reduce_sum(o, i, axis=AX.X), (bb,), (st4b,))
            ACT(P, junk[:], vps, AF.Square, (bb,), (jb, st4b), accum=st4[:, 1:2])
            TS(P, "dve", st4[:, 2:3], st4[:, 0:1], -1.0 / 256, None, ALU.mult, None, (st4b,), (st4b,))
            TT(P, "dve", st4[:, 3:4], st4[:, 2:3], st4[:, 2:3], ALU.mult, (st4b,), (st4b,))
            STT(P, "dve", st4[:, 4:5], st4[:, 1:2], 1.0 / 256, st4[:, 3:4], ALU.mult, ALU.subtract, (st4b,), (st4b,))
            RSQRT(P, st4[:, 5:6], st4[:, 4:5], EPS, (st4b,), (st4b,))
            TS(P, "dve", vtmp[:], vps, st4[:, 2:3], st4[:, 5:6], ALU.add, ALU.mult, (bb, st4b), (vtb,))
            TT(P, "dve", vn[:], vtmp[:], vg[:], ALU.mult, (vtb, vgb), (vnb,))
            mk, mb = mix_bank
            for g in range(4):
                gs_ = slice(g * 64, (g + 1) * 64)
                MM(P, mk[:, gs_], Wtr[:, g, :], vn[:, gs_], True, True, (Wtrb, vnb), (mb,))
            yst, ystb, ych = yb_st.next()
            for g in range(4):
                gs_ = slice(g * 64, (g + 1) * 64)
                STT(P, "dve", yst[:, gs_], mk[:, gs_], bsT[:, g:g + 1], u_sb[:, gs_], ALU.add, ALU.mult,
                    (mb, bsb, ub), (ystb,))
            P.dma("sp", ych, io["yb"][tok0:tok0 + 128, :], yst[:], (ystb,), ())
        if "after_tile" in io:
            io["after_tile"](t, [r[1] for r in fm_st.items + v_st.items])
    outs = [r[1] for r in fm_st.items + f_st.items + v_st.items + g_st.items + yb_st.items]
    return outs


SKEW_AC = tuple(int(v) for v in os.environ.get("SKEW_AC", "0,0,2").split(","))
SKEW_D = tuple(int(v) for v in os.environ.get("SKEW_D", "0,0,1,2,3,4,5").split(","))


def run_pipeline(tiles, skews):
    n = len(tiles)
    for st in range(n + max(skews)):
        for j, sk in enumerate(skews):
            i = st - sk
            if 0 <= i < n:
                tiles[i][j]()


def bc_last(ap2, n):
    return ap2.unsqueeze(2).broadcast_to([ap2.shape[0], ap2.shape[1], n])


def emit_p2(P, io, S, Tc, cst, mixers=("A", "C", "D")):
    NB = S // 128
    NQT = S // 512
    assert NB <= 128

    def cload(src, shape, dt=F32, name="c"):
        t = P.sb(shape, dt, name); b = Buf()
        P.dma("sp", P.chan(), t[:], src, (), (b,))
        return t, b

    ident_f, identfb = cload(cst["ident"], [128, 128], name="identf")
    ident_b = P.sb([128, 128], BF16, "identb"); identb = Buf()
    CP(P, "dve", ident_b[:], ident_f[:], (identfb,), (identb,))
    ntf, ntfb = cload(cst["negtri"], [128, 128], name="ntf")
    negtri = P.sb([128, 128], BF16, "negtri"); negtrib = Buf()
    CP(P, "dve", negtri[:], ntf[:], (ntfb,), (negtrib,))
    ones_col = P.sb([128, 1], BF16, "onec"); onecb = Buf()
    MEMSET(P, "dve", ones_col[:], 1.0, (onecb,))
    stg = P.sb([128, 4, 512], F32, "stg"); stgb = Buf()
    masks = {}
    for nm in ("negmaskC", "negmaskD", "mask01D"):
        P.dma("sp", P.chan(), stg[:], cst[nm], (), (stgb,))
        mt = P.sb([128, 4, 512], BF16, nm); mb_ = Buf()
        CP(P, "dve", mt[:], stg[:], (stgb,), (mb_,))
        masks[nm] = (mt, mb_)

    QT = P.sb([128, S], BF16, "QT"); QTb = Buf()
    KT = P.sb([128, S], BF16, "KT"); KTb = Buf()
    V = P.sb([128, NB, 65], BF16, "V"); Vb = Buf()
    MEMSET(P, "pool", V[:, :, 64:65], 1.0, (Vb,))
    qch, kch, vch = P.chan(), P.chan(), P.chan()
    nbl = Tc // 128

    def load_qkv(m):
        ntq = io["ntq"]
        tq_ = Tc // ntq
        nbq = tq_ // 128
        for i in range(4):
            for tq in range(ntq):
                c0 = i * Tc + tq * tq_
                P.dma("sp", qch, QT[0:64, c0:c0 + tq_], io["qk"](i, 2 * m, tq), (), (QTb,))
                P.dma("sp", kch, KT[0:64, c0:c0 + tq_], io["qk"](i, 2 * m + 1, tq), (), (KTb,))
                b0 = i * nbl + tq * nbq
                P.dma("sp", vch, V[:, b0:b0 + nbq, 0:64], io["v"](i, m, tq), (), (Vb,))

    banks = [(P.ps([128, 512], F32, "bk"), Buf(True)) for _ in range(8)]
    psum_all = P.psum_all
    Pts = Ring([(P.sb([128, 512], BF16, "Pt"), Buf()) for _ in range(4)])
    ysts = Ring([(P.sb([128, 4, 64], io.get("y_dt", F32), "yst"), Buf(), P.chan()) for _ in range(2)])
    after_mixer = io.get("after_mixer", lambda mi, bufs: None)
    ybufs = [r[1] for r in ysts.items]
    rc = P.sb([128, 4], F32, "rc"); rcb = Buf()

    def y_out(yst, ystb, ych, mi, tok0, nsub):
        i, off = tok0 // Tc, tok0 % Tc
        P.dma("sp", ych, io["y_w"](i, mi, off, nsub), yst[:, 0:nsub, :], (ystb,), ())

    if "A" in mixers:
        load_qkv(0)
        bA, bAb = cload(io["biasA"], [128, 5, 128], name="bA")
        mA, mAb = cload(cst["maskA"], [128, 5, 128], name="mA")
        TT(P, "dve", bA[:], bA[:], mA[:], ALU.add, (bAb, mAb), (bAb,))
        BAhi = P.sb([128, 5, 128], BF16, "BAhi"); BAlo = P.sb([128, 5, 128], BF16, "BAlo"); BAb = Buf()
        CP(P, "dve", BAhi[:], bA[:], (bAb,), (BAb,))
        TT(P, "dve", BAlo[:], bA[:], BAhi[:], ALU.subtract, (bAb, BAb), (BAb,))
        Sr = Ring(banks[0:4]); Or = Ring(banks[4:6])
        tiles = []
        for qb in range(NB):
            kbs = list(range(max(0, qb - 4), qb + 1))
            Obk, Ob = Or.next()
            qs = slice(qb * 128, (qb + 1) * 128)
            for idx, kb in enumerate(kbs):
                j = kb - qb + 4
                ks = slice(kb * 128, (kb + 1) * 128)
                Sbk, Sb = Sr.next()
                Pt, Ptb = Pts.next()

                def st0(Sbk=Sbk, Sb=Sb, ks=ks, qs=qs, j=j):
                    MM(P, Sbk[:, 0:128], KT[0:64, ks], QT[0:64, qs], True, False, (KTb, QTb), (Sb,))
                    MM(P, Sbk[:, 0:128], ident_b[:], BAhi[:, j, :], False, False, (identb, BAb), (Sb,))
                    MM(P, Sbk[:, 0:128], ident_b[:], BAlo[:, j, :], False, True, (identb, BAb), (Sb,))

                def st1(Sbk=Sbk, Sb=Sb, Pt=Pt, Ptb=Ptb):
                    ACT(P, Pt[:, 0:128], Sbk[:, 0:128], AF.Exp, (Sb,), (Ptb,))

                def st2(Pt=Pt, Ptb=Ptb, Obk=Obk, Ob=Ob, kb=kb, idx=idx, n=len(kbs), qb=qb):
                    MM(P, Obk[:, 0:65], Pt[:, 0:128], V[:, kb, :], idx == 0, idx == n - 1, (Ptb, Vb), (Ob,))
                    if idx == n - 1:
                        yst, ystb, ych = ysts.next()
                        P.op("dve", lambda e, o=rc[:, 0:1], i=Obk[:, 64:65]: e.reciprocal(o, i), (Ob,), (rcb,))
                        TS(P, "dve", yst[:, 0, :], Obk[:, 0:64], rc[:, 0:1], None, ALU.mult, None, (Ob, rcb), (ystb,))
                        y_out(yst, ystb, ych, 0, qb * 128, 1)

                tiles.append((st0, st1, st2))
        run_pipeline(tiles, SKEW_AC)
        after_mixer(0, ybufs)

    if "C" in mixers:
        load_qkv(1)
        Ff = P.sb([128, 128], F32, "Ff"); Ffb = Buf()
        fch = P.chan()
        for i in range(4):
            P.dma("sp", fch, Ff[i * nbl:(i + 1) * nbl, :], io["f"](i), (), (Ffb,))
        bfc, bfcb = cload(io["bf_col"], [128, 1], name="bfc")
        TS(P, "dve", bfc[:], bfc[:], -1.0, None, ALU.mult, None, (bfcb,), (bfcb,))
        ACT(P, Ff[0:NB, :], Ff[0:NB, :], AF.Exp, (Ffb, bfcb), (Ffb,), bias=bfc[0:NB, :], scale=-1.0)
        ACT(P, Ff[0:NB, :], Ff[0:NB, :], AF.Ln, (Ffb,), (Ffb,), bias=1.0)
        tot = P.sb([128, 1], F32, "tot"); totb = Buf()
        P.op("dve", lambda e, o=tot[0:NB, :], i=Ff[0:NB, :]: e.reduce_sum(o, i, axis=AX.X), (Ffb,), (totb,))
        onesf = P.sb([128, 128], F32, "onesf"); onesfb = Buf()
        MEMSET(P, "dve", onesf[:], 1.0, (onesfb,))
        totbc = P.sb([128, 128], F32, "totbc"); totbcb = Buf()
        TS(P, "dve", totbc[0:NB, :], onesf[0:NB, :], tot[0:NB, :], None, ALU.mult, None, (onesfb, totb), (totbcb,))
        triU, triUb = cload(cst["triU"], [128, 128], name="triU")
        SU, SUb = cload(cst["SU"], [128, 128], name="SU")
        b6, b6b = banks[6]
        b7, b7b = banks[7]
        P.op("pe", lambda e, o=b6[:, 0:NB], i=Ff[0:NB, :], idn=ident_f[0:NB, 0:NB]: e.transpose(o, i, idn),
             (Ffb, identfb), (b6b,))
        LT = P.sb([128, 128], F32, "LT"); LTb = Buf()
        CP(P, "dve", LT[:, 0:NB], b6[:, 0:NB], (b6b,), (LTb,))
        MM(P, b7[0:NB, 0:128], LT[:, 0:NB], triU[:], True, False, (LTb, triUb), (b7b,))
        MM(P, b7[0:NB, 0:128], SU[0:NB, 0:NB], totbc[0:NB, :], False, True, (SUb, totbcb), (b7b,))
        cpos = P.sb([128, 128], F32, "cpos"); cposb = Buf()
        CP(P, "dve", cpos[0:NB, :], b7[0:NB, 0:128], (b7b,), (cposb,))
        parts = P.sb([128, 6, 128], BF16, "parts"); partsb = Buf()
        r1 = P.sb([128, 128], F32, "r1"); r1b = Buf()
        CP(P, "dve", parts[0:NB, 3, :], cpos[0:NB, :], (cposb,), (partsb,))
        TT(P, "dve", r1[0:NB, :], cpos[0:NB, :], parts[0:NB, 3, :], ALU.subtract, (cposb, partsb), (r1b,))
        CP(P, "dve", parts[0:NB, 4, :], r1[0:NB, :], (r1b,), (partsb,))
        TT(P, "dve", r1[0:NB, :], r1[0:NB, :], parts[0:NB, 4, :], ALU.subtract, (r1b, partsb), (r1b,))
        CP(P, "dve", parts[0:NB, 5, :], r1[0:NB, :], (r1b,), (partsb,))
        for r in range(3):
            TS(P, "dve", parts[0:NB, r, :], parts[0:NB, 3 + r, :], -1.0, None, ALU.mult, None, (partsb,), (partsb,))
        csb = Buf()
        P.dma("sp", P.chan(), io["cs"].rearrange("r (i p) -> i r p", p=128), parts[0:NB, :, :], (partsb,), (csb,))
        MEMSET(P, "dve", QT[64:70, :], 1.0, (QTb,))
        MEMSET(P, "dve", KT[64:70, :], 1.0, (KTb,))
        P.dma("sp", qch, QT[64:67, :], io["cs"][0:3, :], (csb,), (QTb,))
        P.dma("sp", kch, KT[67:70, :], io["cs"][3:6, :], (csb,), (KTb,))
        nmC, nmCb = masks["negmaskC"]
        Sr = Ring(banks[0:4]); Or = Ring(banks[4:6])
        tiles = []
        for qt in range(NQT):
            q0 = qt * 512
            qs = slice(q0, q0 + 512)
            nkb = 4 * qt + 4
            Obk, Ob = Or.next()
            O3 = Obk[:, 0:260].rearrange("p (s d) -> p s d", d=65)
            for kb in range(nkb):
                ks = slice(kb * 128, (kb + 1) * 128)
                j = kb - 4 * qt
                Sbk, Sb = Sr.next()
                Pt, Ptb = Pts.next()

                def st0(Sbk=Sbk, Sb=Sb, ks=ks, qs=qs, j=j):
                    MM(P, Sbk[:], KT[0:70, ks], QT[0:70, qs], True, j < 0, (KTb, QTb), (Sb,))
                    if j >= 0:
                        MM(P, Sbk[:], ident_b[:], nmC[:, j, :], False, True, (identb, nmCb), (Sb,))

                def st1(Sbk=Sbk, Sb=Sb, Pt=Pt, Ptb=Ptb):
                    ACT(P, Pt[:], Sbk[:], AF.Exp, (Sb,), (Ptb,))

                def st2(Pt=Pt, Ptb=Ptb, O3=O3, Ob=Ob, kb=kb, j=j, qt=qt, q0=q0, nkb=nkb):
                    for sub in range(4):
                        if j > sub:
                            continue
                        MM(P, O3[:, sub, :], Pt[:, sub * 128:(sub + 1) * 128], V[:, kb, :], kb == 0 and sub == 0,
                           kb == 4 * qt + sub, (Ptb, Vb), (Ob,), skip=True)
                    if kb == nkb - 1:
                        yst, ystb, ych = ysts.next()
                        P.op("dve", lambda e, o=rc[:, 0:4], i=O3[:, :, 64]: e.reciprocal(o, i), (Ob,), (rcb,))
                        TT(P, "dve", yst[:], O3[:, :, 0:64], bc_last(rc[:, 0:4], 64), ALU.mult, (Ob, rcb), (ystb,))
                        y_out(yst, ystb, ych, 1, q0, 4)

                tiles.append((st0, st1, st2))
        run_pipeline(tiles, SKEW_AC)
        after_mixer(1, ybufs)

    if "D" in mixers:
        load_qkv(2)
        nmD, nmDb = masks["negmaskD"]
        m01, m01b = masks["mask01D"]
        zp = psum_all[:, 0:1024]; zpb = banks[0][1]
        ap_ = [(psum_all[:, 1024:2048], banks[2][1]), (psum_all[:, 2048:3072], banks[4][1])]
        Ar = Ring(ap_)
        OCr = Ring(banks[6:8])
        negones = P.sb([128, 128], BF16, "negones"); negonesb = Buf()
        MEMSET(P, "dve", negones[:], -1.0, (negonesb,))
        Es = Ring([(P.sb([128, 1024], F32, "E"), Buf()) for _ in range(3)])
        SPs = Ring([(P.sb([128, 1024], BF16, "SP"), Buf()) for _ in range(7)])
        PtD = Ring([(P.sb([128, 1024], BF16, "PtD"), Buf()) for _ in range(3)])
        acc = P.sb([128, 4, 64], F32, "acc"); accb = Buf()
        tmp = P.sb([128, 4, 64], F32, "tmp"); tmpb = Buf()
        carry = P.sb([128, 4], F32, "carry"); carryb = Buf()
        ec = P.sb([128, 4], F32, "ec"); ecb = Buf()
        tiles = []
        for qt in range(NQT):
            q0 = qt * 512
            qs = slice(q0, q0 + 512)
            kmax = 4 * qt + 3
            for k1 in range(kmax, -1, -2):
                k2 = k1 - 1
                kk = (k1, k2)
                kss = tuple(slice(k * 128, (k + 1) * 128) for k in kk)
                js = tuple(k - 4 * qt for k in kk)
                E, Eb = Es.next()
                SP, SPb = SPs.next()
                Abk, Ab = Ar.next()
                Pt, Ptb = PtD.next()
                OCbk, OCb = OCr.next()
                OC3 = OCbk[:, 0:260].rearrange("p (s d) -> p s d", d=65)

                def s0(kss=kss, qs=qs):
                    for h in range(2):
                        MM(P, zp[:, h * 512:(h + 1) * 512], KT[0:64, kss[h]], QT[0:64, qs], True, True, (KTb, QTb), (zpb,))

                def s1(E=E, Eb=Eb):
                    ACT(P, E[:], zp, AF.Exp, (zpb,), (Eb,))

                def s2(E=E, Eb=Eb, SP=SP, SPb=SPb, js=js):
                    ACT(P, SP[:], E[:], AF.Ln, (Eb,), (SPb,), bias=1.0)
                    for h in range(2):
                        if js[h] >= 0:
                            hs_ = slice(h * 512, (h + 1) * 512)
                            TT(P, "dve", SP[:, hs_], SP[:, hs_], m01[:, js[h], :], ALU.mult, (SPb, m01b), (SPb,))

                def s3(Abk=Abk, Ab=Ab, SP=SP, SPb=SPb, kss=kss, qs=qs, js=js):
                    for h in range(2):
                        o = Abk[:, h * 512:(h + 1) * 512]
                        hs_ = slice(h * 512, (h + 1) * 512)
                        last_is_tri = (h == 0) and js[h] < 0
                        MM(P, o, KT[0:64, kss[h]], QT[0:64, qs], True, False, (KTb, QTb), (Ab,))
                        MM(P, o, negtri[:], SP[:, hs_], False, last_is_tri, (negtrib, SPb), (Ab,))
                        if h == 1:
                            MM(P, o, negones[:], SP[:, 0:512], False, js[h] < 0, (negonesb, SPb), (Ab,))
                        if js[h] >= 0:
                            MM(P, o, ident_b[:], nmD[:, js[h], :], False, True, (identb, nmDb), (Ab,))

                def s4(Abk=Abk, Ab=Ab, Pt=Pt, Ptb=Ptb):
                    ACT(P, Pt[:], Abk, AF.Exp, (Ab,), (Ptb,))

                def s5(Pt=Pt, Ptb=Ptb, SP=SP, SPb=SPb, OC3=OC3, OCb=OCb, kk=kk):
                    for sub in range(4):
                        for h in range(2):
                            ss = slice(h * 512 + sub * 128, h * 512 + (sub + 1) * 128)
                            MM(P, OC3[:, sub, 0:64], Pt[:, ss], V[:, kk[h], 0:64], h == 0, h == 1, (Ptb, Vb), (OCb,))
                        for h in range(2):
                            ss = slice(h * 512 + sub * 128, h * 512 + (sub + 1) * 128)
                            MM(P, OC3[:, sub, 64:65], SP[:, ss], ones_col[:], h == 0, h == 1, (SPb, onecb), (OCb,))

                def s6(OC3=OC3, OCb=OCb, k1=k1, k2=k2, kmax=kmax, q0=q0):
                    if k1 == kmax:
                        CP(P, "dve", acc[:], OC3[:, :, 0:64], (OCb,), (accb,))
                        TS(P, "dve", carry[:], OC3[:, :, 64], -1.0, None, ALU.mult, None, (OCb,), (carryb,))
                    else:
                        ACT(P, ec[:], carry[:], AF.Exp, (carryb,), (ecb,))
                        TT(P, "dve", tmp[:], OC3[:, :, 0:64], bc_last(ec[:, 0:4], 64), ALU.mult, (OCb, ecb), (tmpb,))
                        TT(P, "dve", acc[:], acc[:], tmp[:], ALU.add, (accb, tmpb), (accb,))
                        TT(P, "dve", carry[:], carry[:], OC3[:, :, 64], ALU.subtract, (carryb, OCb), (carryb,))
                    if k2 == 0:
                        yst, ystb, ych = ysts.next()
                        CP(P, "dve", yst[:], acc[:], (accb,), (ystb,))
                        y_out(yst, ystb, ych, 2, q0, 4)

                tiles.append((s0, s1, s2, s3, s4, s5, s6))
        run_pipeline(tiles, SKEW_D)
        after_mixer(2, ybufs)
    return [r[1] for r in ysts.items]


def emit_p3(P, io, T, cst, final):
    NT = T // 512
    ident_f = P.sb([128, 128], F32, "identf"); identfb = Buf()
    P.dma("sp", P.chan(), ident_f[:], cst["ident"], (), (identfb,))
    ident_b = P.sb([128, 128], BF16, "identb"); identb = Buf()
    CP(P, "dve", ident_b[:], ident_f[:], (identfb,), (identb,))
    Wo = P.sb([128, 8, 1024], BF16, "Wo"); Wobs = [Buf() for _ in range(8)]
    wst = [(P.sb([128, 1024], F32, "wst"), Buf(), P.chan()) for _ in range(2)]
    w_v = io["w_out"].rearrange("(c p) n -> p c n", p=128)
    for c in range(8):
        st, sb_, ch = wst[c % 2]
        P.dma("sp", ch, st[:], w_v[:, c, :], (), (sb_,))
        CP(P, "dve" if c % 2 == 0 else "pool", Wo[:, c, :], st[:], (sb_,), (Wobs[c],))
    bg = P.sb([128, 1024], F32, "bg"); bgb = Buf()
    P.dma("sp", P.chan(), bg[:], io["bgain_bc"], (), (bgb,))
    TS(P, "dve", bg[:], bg[:], 16.0, None, ALU.mult, None, (bgb,), (bgb,))
    if final:
        fg = P.sb([128, 8], F32, "fg"); fgb = Buf()
        P.dma("sp", P.chan(), fg[:], io["fg8"], (), (fgb,))
        TS(P, "dve", fg[:], fg[:], 32.0, None, ALU.mult, None, (fgb,), (fgb,))
        ones = P.sb([128, 128], BF16, "ones"); onesb = Buf()
        MEMSET(P, "dve", ones[:], 1.0, (onesb,))
        sq = P.sb([128, 8, 512], BF16, "sq"); sqb = Buf()
        rbc = P.sb([128, 512], F32, "rbc"); rbcb = Buf()

    y_dt = io.get("y_dt", F32)
    yts = Ring([(P.sb([128, 3, 256], y_dt, "yt"), P.sb([128, 256], F32, "ytB"), Buf(), P.chan()) for _ in range(2)])
    gts = Ring([(P.sb([128, 1024], io.get("gs_dt", F32), "gt"), Buf(), P.chan()) for _ in range(2)])
    xts = Ring([(P.sb([128, 8, 512], F32, "xt"), Buf(), P.chan()) for _ in range(2)])
    xns = Ring([(P.sb([128, 8, 512], F32, "xn"), [Buf() for _ in range(8)], P.chan("pool")) for _ in range(2)])
    mTs = Ring([(P.sb([128, 8, 512], BF16, "mT"), Buf()) for _ in range(2)])
    t1 = P.sb([128, 1024], F32, "t1"); t1b = Buf()
    m = P.sb([128, 1024], BF16, "m"); mb = Buf()
    junk = P.sb([128, 256], F32, "junk"); jb = Buf()
    ssq = P.sb([128, 4], F32, "ssq"); ssqb = Buf()
    tps = Ring([(P.ps([128, 1024], BF16, "tp"), Buf(True)) for _ in range(2)])
    obanks = Ring([(P.ps([128, 512], F32, "ob"), Buf(True)) for _ in range(4)])
    if final:
        sbank = (P.ps([128, 512], F32, "sbk"), Buf(True))
    x_v = io["xT"].rearrange("(c p) t -> p c t", p=128)
    o_v = io["xT_out"].rearrange("(c p) t -> p c t", p=128)
    for t in range(NT):
        ts = slice(t * 512, (t + 1) * 512)
        xt, xb, xch = xts.next()
        P.dma("sp", xch, xt[:], x_v[:, :, ts], (), (xb,))
        mT, mTb = mTs.next()
        for sub in range(4):
            tok0 = t * 512 + sub * 128
            yt3, ytB, ytb, ych = yts.next()
            for mi in range(3):
                P.dma("sp", ych, yt3[:, mi, :].rearrange("p (h d) -> p h d", h=4),
                      io["y"](mi, tok0), (), (ytb,))
            P.dma("sp", ych, ytB[:], io["yb"][tok0:tok0 + 128, :], (), (ytb,))
            ysrc = (yt3[:, 0, :], ytB[:], yt3[:, 1, :], yt3[:, 2, :])
            gt, gtb, gch = gts.next()
            P.dma("sp", gch, gt[:], io["gs"][tok0:tok0 + 128, :], (), (gtb,))
            for br in range(4):
                ACT(P, junk[:], ysrc[br], AF.Square, (ytb,), (jb, ssqb), accum=ssq[:, br:br + 1])
            RSQRT(P, ssq[:], ssq[:], 256.0 * EPS, (ssqb,), (ssqb,))
            TT(P, "pool", t1[:], gt[:], bg[:], ALU.mult, (gtb, bgb), (t1b,))
            for br in range(4):
                cs_ = slice(br * 256, (br + 1) * 256)
                STT(P, "dve", m[:, cs_], ysrc[br], ssq[:, br:br + 1], t1[:, cs_], ALU.mult, ALU.mult,
                    (ytb, ssqb, t1b), (mb,))
            tp, tpb = tps.next()
            for c in range(8):
                P.op("pe", lambda e, o=tp[:, c * 128:(c + 1) * 128], i=m[:, c * 128:(c + 1) * 128], idn=ident_b[:]:
                     e.transpose(o, i, idn), (mb, identb), (tpb,))
            CP(P, "act", mT[:, :, sub * 128:(sub + 1) * 128], tp[:].rearrange("p (c t) -> p c t", c=8), (tpb,), (mTb,))
        xn, xnbs, xnch = xns.next()
        for cb in range(8):
            ob, obb = obanks.next()
            for c in range(8):
                MM(P, ob[:], Wo[:, c, cb * 128:(cb + 1) * 128], mT[:, c, :], c == 0, c == 7, (Wobs[c], mTb), (obb,))
            TT(P, "dve", xn[:, cb, :], xt[:, cb, :], ob[:], ALU.add, (xb, obb), (xnbs[cb],))
        if final:
            ACT(P, sq[:], xn[:], AF.Square, tuple(xnbs), (sqb,))
            sbk, sbb = sbank
            for c in range(8):
                MM(P, sbk[:], ones[:], sq[:, c, :], c == 0, c == 7, (onesb, sqb), (sbb,))
            RSQRT(P, rbc[:], sbk[:], 1024.0 * EPS, (sbb,), (rbcb,))
            for c in range(8):
                STT(P, "dve", xn[:, c, :], xn[:, c, :], fg[:, c:c + 1], rbc[:], ALU.mult, ALU.mult,
                    (xnbs[c], fgb, rbcb), (xnbs[c],))
        P.dma("pool", xnch, o_v[:, :, ts], xn[:], tuple(xnbs), ())
    return [b for r in xns.items for b in r[1]]


T_CORE = 4096
SEQ = 16384
_CACHE = {}


def _ext(P, name, shape, dt, kind):
    return P.dram(name, shape, dt, kind)


def static_p1_io(io):
    io["qk_w"] = lambda h, ti, t0: io["qk_send"][h, ti, :, t0:t0 + 512]
    io["v_w"] = lambda vi, tok0: io["v_send"][:, vi, tok0:tok0 + 128, :].rearrange("h t d -> t h d")


def static_p2_io(io):
    io["ntq"] = 1
    io["qk"] = lambda i, idx, tq: io["qk_recv"][i, idx]
    io["v"] = lambda i, m, tq: io["v_recv"][i, m].rearrange("(n p) d -> p n d", p=128)
    io["f"] = lambda i: io["f_recv"][i].rearrange("(n p) -> n p", p=128)
    io["y_w"] = lambda i, mi, off, nsub: io["y_send"][i, mi, off:off + nsub * 128, :].rearrange("(s p) d -> p s d", p=128)


def static_p3_io(io):
    io["y"] = lambda mi, tok0: io["y_recv"][:, mi, tok0:tok0 + 128, :].rearrange("h t d -> t h d")


def build_prog1():
    P = Prog(); T = T_CORE; io = {}
    io["xT"] = _ext(P, "xT", [1024, T], F32, "ExternalInput")
    io["w_in"] = _ext(P, "w_in", [1024, N_IN], F32, "ExternalInput")
    io["g8"] = _ext(P, "g8", [128, 8], F32, "ExternalInput")
    io["vgain_bc"] = _ext(P, "vgain_bc", [128, 256], F32, "ExternalInput")
    io["wsT"] = _ext(P, "wsT", [128, 4, 128], F32, "ExternalInput")
    io["bsT"] = _ext(P, "bsT", [128, 4], F32, "ExternalInput")
    cst = {"tril01T": _ext(P, "tril01T", [128, 128], F32, "ExternalInput")}
    io["qk_send"] = _ext(P, "qk_send", [4, 6, 64, T], BF16, "ExternalOutput")
    io["v_send"] = _ext(P, "v_send", [4, 3, T, 64], BF16, "ExternalOutput")
    io["f_send"] = _ext(P, "f_send", [4, T], F32, "ExternalOutput")
    io["gs"] = _ext(P, "gs", [T, 1024], F32, "ExternalOutput")
    io["yb"] = _ext(P, "yb", [T, 256], F32, "ExternalOutput")
    static_p1_io(io)
    outs = emit_p1(P, io, T, cst)
    P.wait_all("sp", outs)
    return P.emit()


P2_CONSTS = ("ident", "negtri", "triU", "SU", "negmaskC", "negmaskD", "mask01D", "maskA")


def build_prog2():
    P = Prog(); S = SEQ; Tc = T_CORE; io = {}
    io["qk_recv"] = _ext(P, "qk_recv", [4, 6, 64, Tc], BF16, "ExternalInput")
    io["v_recv"] = _ext(P, "v_recv", [4, 3, Tc, 64], BF16, "ExternalInput")
    io["f_recv"] = _ext(P, "f_recv", [4, Tc], F32, "ExternalInput")
    io["biasA"] = _ext(P, "biasA", [128, 5, 128], F32, "ExternalInput")
    io["bf_col"] = _ext(P, "bf_col", [128, 1], F32, "ExternalInput")
    io["y_send"] = _ext(P, "y_send", [4, 3, Tc, 64], F32, "ExternalOutput")
    io["cs"] = P.dram("cs", [6, S], BF16, "Internal")
    cn = host_consts()
    cst = {k: _ext(P, k, list(cn[k].shape), F32, "ExternalInput") for k in P2_CONSTS}
    static_p2_io(io)
    outs = emit_p2(P, io, S, Tc, cst)
    P.wait_all("sp", outs)
    return P.emit()


def build_prog3(final):
    P = Prog(); T = T_CORE; io = {}
    io["y_recv"] = _ext(P, "y_recv", [4, 3, T, 64], F32, "ExternalInput")
    io["yb"] = _ext(P, "yb", [T, 256], F32, "ExternalInput")
    io["gs"] = _ext(P, "gs", [T, 1024], F32, "ExternalInput")
    io["xT"] = _ext(P, "xT", [1024, T], F32, "ExternalInput")
    io["w_out"] = _ext(P, "w_out", [1024, 1024], F32, "ExternalInput")
    io["bgain_bc"] = _ext(P, "bgain_bc", [128, 1024], F32, "ExternalInput")
    if final:
        io["fg8"] = _ext(P, "fg8", [128, 8], F32, "ExternalInput")
    io["xT_out"] = _ext(P, "xT_out", [1024, T], F32, "ExternalOutput")
    cst = {"ident": _ext(P, "ident", [128, 128], F32, "ExternalInput")}
    static_p3_io(io)
    outs = emit_p3(P, io, T, cst, final)
    P.wait_all("sp", outs)
    return P.emit()


def _run(nc, in_maps):
    res = run_bass_kernel_spmd(nc, in_maps, core_ids=list(range(8)))
    return res.results


def kernel_unfused(x, norm_g, w_in, b_f, rel_bias, w_s, b_s, v_gain, branch_gain, w_out, final_g):
    f32 = np.float32
    x = np.asarray(x, f32)
    cn = host_consts()
    depth = 2
    c32 = lambda a: np.ascontiguousarray(np.asarray(a, f32))
    xT = [c32(x[c // 4, (c % 4) * T_CORE:(c % 4 + 1) * T_CORE, :].T) for c in range(8)]
    nc1 = build_prog1()
    nc2 = build_prog2()
    for l in range(depth):
        final = l == depth - 1
        com1 = dict(w_in=c32(w_in[l]), g8=c32(np.asarray(norm_g[l]).reshape(8, 128).T),
                    vgain_bc=c32(np.tile(np.asarray(v_gain[l])[None, :], (128, 1))),
                    wsT=c32(np.asarray(w_s[l]).transpose(2, 0, 1)), bsT=c32(np.asarray(b_s[l]).T),
                    tril01T=cn["tril01T"])
        r1 = _run(nc1, [dict(com1, xT=xT[c]) for c in range(8)])
        im2 = []
        for c in range(8):
            b, h = c // 4, c % 4
            d = dict(qk_recv=np.stack([r1[b * 4 + i]["qk_send"][h] for i in range(4)]),
                     v_recv=np.stack([r1[b * 4 + i]["v_send"][h] for i in range(4)]),
                     f_recv=np.stack([r1[b * 4 + i]["f_send"][h] for i in range(4)]),
                     biasA=host_biasA(np.asarray(rel_bias[l][h], f32)),
                     bf_col=np.full((128, 1), np.asarray(b_f, f32)[l, h], f32))
            for k in P2_CONSTS:
                d[k] = cn[k]
            im2.append(d)
        r2 = _run(nc2, im2)
        nc3 = build_prog3(final)
        com3 = dict(w_out=c32(w_out[l]), ident=cn["ident"],
                    bgain_bc=c32(np.tile(np.asarray(branch_gain[l]).reshape(1, 1024), (128, 1))))
        if final:
            com3["fg8"] = c32(np.asarray(final_g).reshape(8, 128).T)
        im3 = []
        for c in range(8):
            b, i = c // 4, c % 4
            im3.append(dict(com3, y_recv=np.stack([r2[b * 4 + h]["y_send"][i] for h in range(4)]),
                            yb=r1[c]["yb"], gs=r1[c]["gs"], xT=xT[c]))
        r3 = _run(nc3, im3)
        xT = [np.asarray(r3[c]["xT_out"]) for c in range(8)]
    out = np.empty((2, SEQ, D_MODEL), f32)
    for c in range(8):
        out[c // 4, (c % 4) * T_CORE:(c % 4 + 1) * T_CORE, :] = xT[c].T
    return out


GROUPS = [[0, 1, 2, 3], [4, 5, 6, 7]]


def build_fused(T=T_CORE, depth=2):
    S = 4 * T
    NTQ = 4 if T >= 2048 else 1
    TQ = T // NTQ
    NT8 = max(1, T // 512)
    T8 = T // NT8
    P = Prog()
    P.use_pid = True
    cn = host_consts()
    cst = {k: P.dram(k, list(v.shape), F32, "ExternalInput") for k, v in cn.items()}
    xT_in = P.dram("xT", [1024, T], F32, "ExternalInput")
    out_ext = P.dram("outT", [1024, T], F32, "ExternalOutput")
    xT_mid = P.dram("xT_mid", [1024, T], F32, "Internal")
    gs = P.dram("gs_i", [T, 1024], BF16, "Internal")
    yb = P.dram("yb_i", [T, 256], F32, "Internal")
    cs = P.dram("cs_i", [6, S], BF16, "Internal")
    fg8 = P.dram("fg8", [128, 8], F32, "ExternalInput")
    cch = P.chan("coll")
    FSTOP = int(os.environ.get("FSTOP", "99"))

    def hsel(e):
        return bass.ds(P.pid_cache[id(e)], 1)

    def exchange(name, nch, ch, dt):
        send = P.dram(f"{name}_s", [nch, 4, ch], dt, "Internal")
        gath = P.dram(f"{name}_g", [nch, 16, ch], dt, "Internal")
        recv = P.dram(f"{name}_r", [nch, 4, ch], dt, "Internal")

        def gather(c0=0, c1=nch, wait_bufs=()):
            for c in range(c0, c1):
                P.coll(cch, "AllGather", GROUPS, send[c], gath[c], (), (), after=tuple(wait_bufs))

        def finish():
            P.barrier()
            d = Buf()
            g4 = gath.rearrange("c (a h) x -> c a h x", h=4)
            P.dma("sp", P.chan(), recv.rearrange("c a (o x) -> c a o x", o=1),
                  lambda e: g4[:, :, hsel(e), :], (), (d,))
        return send, recv, gather, finish

    for l in range(depth):
        final = l == depth - 1
        ext = lambda nm, shp: P.dram(f"{nm}{l}", shp, F32, "ExternalInput")
        qk_s, qk_r, qk_g, qk_f = exchange(f"qk{l}", 6 * NTQ, 64 * TQ, BF16)
        v_s, v_r, v_g, v_f = exchange(f"v{l}", 3 * NTQ, TQ * 64, BF16)
        f_s, f_r, f_g, f_f = exchange(f"f{l}", 1, T, F32)
        y_s, y_r, y_g, y_f = exchange(f"y{l}", 3 * NT8, T8 * 64, BF16)
        io1 = dict(xT=xT_in if l == 0 else xT_mid, w_in=ext("w_in", [1024, N_IN]), g8=ext("g8_", [128, 8]),
                   vgain_bc=ext("vgain_bc", [128, 256]), wsT=ext("wsT", [128, 4, 128]), bsT=ext("bsT", [128, 4]),
                   f_send=f_s[0], gs=gs, yb=yb, gs_dt=BF16)
        io1["qk_w"] = lambda h, ti, t0, qk_s=qk_s: qk_s[ti * NTQ + t0 // TQ, h].rearrange(
            "(d t) -> d t", d=64)[:, t0 % TQ:t0 % TQ + 512]
        io1["v_w"] = lambda vi, tok0, v_s=v_s: v_s[vi * NTQ + tok0 // TQ].rearrange(
            "h (t d) -> t h d", d=64)[tok0 % TQ:tok0 % TQ + 128]
        tpq = TQ // 512

        def after_tile(t, bufs, qk_g=qk_g, v_g=v_g):
            if (t + 1) % tpq == 0:
                tq = t // tpq
                for ti in range(6):
                    qk_g(ti * NTQ + tq, ti * NTQ + tq + 1, bufs)
                for vi in range(3):
                    v_g(vi * NTQ + tq, vi * NTQ + tq + 1, bufs)

        if not os.environ.get("NOHOOK"):
            io1["after_tile"] = after_tile
        P.phase_begin()
        if not os.environ.get("SKIP1"):
            emit_p1(P, io1, T, cst)
        P.phase_end()
        if FSTOP <= 1:
            break
        f_g()
        qk_f(); v_f(); f_f()
        P.barrier()
        if FSTOP <= 3:
            break
        io2 = dict(biasA=ext("biasA", [128, 5, 128]), bf_col=ext("bf_col", [128, 1]), cs=cs, ntq=NTQ, y_dt=BF16)
        io2["after_mixer"] = lambda mi, bufs, y_g=y_g: y_g(mi * NT8, (mi + 1) * NT8, bufs)
        io2["qk"] = lambda i, idx, tq, qk_r=qk_r: qk_r[idx * NTQ + tq, i].rearrange("(d t) -> d t", d=64)
        io2["v"] = lambda i, m, tq, v_r=v_r: v_r[m * NTQ + tq, i].rearrange("(n p d) -> p n d", p=128, d=64)
        io2["f"] = lambda i, f_r=f_r: f_r[0, i].rearrange("(n p) -> n p", p=128)
        io2["y_w"] = lambda i, mi, off, nsub, y_s=y_s: y_s[mi * NT8 + off // T8, i].rearrange(
            "(t d) -> t d", d=64)[off % T8:off % T8 + nsub * 128].rearrange("(s p) d -> p s d", p=128)
        P.phase_begin()
        emit_p2(P, io2, S, T, cst)
        P.phase_end()
        if FSTOP <= 4:
            break
        y_f()
        P.barrier()
        io3 = dict(yb=yb, gs=gs, xT=io1["xT"], w_out=ext("w_out", [1024, 1024]), bgain_bc=ext("bgain_bc", [128, 1024]),
                   fg8=fg8, xT_out=out_ext if final else xT_mid, y_dt=BF16, gs_dt=BF16)
        io3["y"] = lambda mi, tok0, y_r=y_r: y_r[mi * NT8 + tok0 // T8].rearrange(
            "h (t d) -> t h d", d=64)[tok0 % T8:tok0 % T8 + 128]
        P.phase_begin()
        final_bufs = emit_p3(P, io3, T, cst, final)
        if final:
            P.wait_all("sp", final_bufs)
        P.phase_end()
    print("fused ops:", {k: len(v) for k, v in P.ops.items()}, "sems", P.nsem, flush=True)
    return P.emit()


def fused_inputs(x, norm_g, w_in, b_f, rel_bias, w_s, b_s, v_gain, branch_gain, w_out, final_g, T=T_CORE, depth=2):
    f32 = np.float32
    c32 = lambda a: np.ascontiguousarray(np.asarray(a, f32))
    cn = host_consts()
    com = dict(cn)
    com["fg8"] = c32(np.asarray(final_g).reshape(8, 128).T)
    for l in range(depth):
        com[f"w_in{l}"] = c32(w_in[l])
        com[f"g8_{l}"] = c32(np.asarray(norm_g[l]).reshape(8, 128).T)
        com[f"vgain_bc{l}"] = c32(np.tile(np.asarray(v_gain[l])[None, :], (128, 1)))
        com[f"wsT{l}"] = c32(np.asarray(w_s[l]).transpose(2, 0, 1))
        com[f"bsT{l}"] = c32(np.asarray(b_s[l]).T)
        com[f"w_out{l}"] = c32(w_out[l])
        com[f"bgain_bc{l}"] = c32(np.tile(np.asarray(branch_gain[l]).reshape(1, 1024), (128, 1)))
    ims = []
    for c in range(8):
        b, h = c // 4, c % 4
        d = dict(com)
        d["xT"] = c32(np.asarray(x)[b, h * T:(h + 1) * T, :].T)
        for l in range(depth):
            d[f"biasA{l}"] = host_biasA(np.asarray(rel_bias[l][h], f32))
            d[f"bf_col{l}"] = np.full((128, 1), np.asarray(b_f, f32)[l, h], f32)
        ims.append(d)
    return ims


def kernel(x, norm_g, w_in, b_f, rel_bias, w_s, b_s, v_gain, branch_gain, w_out, final_g):
    nc = build_fused()
    ims = fused_inputs(x, norm_g, w_in, b_f, rel_bias, w_s, b_s, v_gain, branch_gain, w_out, final_g)
    res = run_bass_kernel_spmd(nc, ims, core_ids=list(range(8))).results
    out = np.empty((2, SEQ, D_MODEL), np.float32)
    for c in range(8):
        out[c // 4, (c % 4) * T_CORE:(c % 4 + 1) * T_CORE, :] = np.asarray(res[c]["outT"]).T
    return out
```
